# Optimizing a Trainium2 kernel written in Bass

```python
import numpy as np
import jax
import jax.numpy as jnp
from jax import lax

D_MODEL = 4096
BATCH = 8
SEQ = 2048
DEPTH = 4

HEAD_DIM = 128
NSA_HEADS = 16
NSA_KV_GROUPS = 4
NSA_HPG = NSA_HEADS // NSA_KV_GROUPS
CMP_BLOCK = 32
CMP_STRIDE = 16
CMP_HIDDEN = 128
SLC_BLOCK = 64
SLC_TOPK = 16
WINDOW = 512
ATTN_Q_BLOCK = 128
SLC_Q_BLOCK = 16
ROPE_THETA = 500000.0
ROPE_DIM = HEAD_DIM // 4
GLA_HEADS = 4
GLA_DK = 256
GLA_DV = 512
GLA_RANK = 16
GLA_TAU = 16.0
GLA_CHUNK = 64
CONV_WIDTH = 31
FFN_HIDDEN = (8 * D_MODEL + 3 * 256 - 1) // (3 * 256) * 256
N_MOD = 6
N_EVEN = (DEPTH + 1) // 2
N_ODD = DEPTH // 2

NSA_Q = NSA_HEADS * HEAD_DIM
NSA_KV = NSA_KV_GROUPS * HEAD_DIM
EVEN_IN_SPLITS = (NSA_Q, NSA_KV, NSA_KV, NSA_KV, NSA_KV, NSA_KV, NSA_KV, NSA_HEADS * 3,
                  GLA_HEADS * GLA_DK, GLA_HEADS * GLA_DK, GLA_HEADS * GLA_DV, GLA_RANK, GLA_HEADS * GLA_DV)
EVEN_IN = sum(EVEN_IN_SPLITS)
MIX_OUT = NSA_Q + GLA_HEADS * GLA_DV

kernel_name = 'hybrid_nsa_gla_conformer_adaln'


def rms_norm(x, g, eps=1e-6):
    xf = x.astype(jnp.float32)
    y = xf * lax.rsqrt(jnp.mean(xf * xf, axis=-1, keepdims=True) + eps)
    return (y * g.astype(jnp.float32)).astype(x.dtype)


def layer_norm(x, g, b, eps=1e-5):
    xf = x.astype(jnp.float32)
    mu = jnp.mean(xf, axis=-1, keepdims=True)
    var = jnp.mean(jnp.square(xf - mu), axis=-1, keepdims=True)
    y = (xf - mu) * lax.rsqrt(var + eps)
    return (y * g.astype(jnp.float32) + b.astype(jnp.float32)).astype(x.dtype)


def partial_rope(x, pos):
    half = ROPE_DIM // 2
    inv_freq = ROPE_THETA ** (-jnp.arange(half, dtype=jnp.float32) / half)
    ang = pos.astype(jnp.float32)[:, None] * inv_freq[None, :]
    cos, sin = jnp.cos(ang), jnp.sin(ang)
    xf = x.astype(jnp.float32)
    x1, x2 = xf[..., :half], xf[..., half:ROPE_DIM]
    out = jnp.concatenate([x1 * cos - x2 * sin, x2 * cos + x1 * sin, xf[..., ROPE_DIM:]], axis=-1)
    return out.astype(x.dtype)


def masked_softmax(s, mask):
    s = jnp.where(mask, s.astype(jnp.float32), -1e30)
    return jnp.where(mask, jax.nn.softmax(s, axis=-1), 0.0)


def nsa_mixer(q, kc, vc, ks, vs, kw, vw, gate_logits, q_norm_g, k_norm_g, cmp_pos, cmp_w1, cmp_w2):
    B, S, _ = q.shape
    G, HPG, DH = NSA_KV_GROUPS, NSA_HPG, HEAD_DIM
    t = jnp.arange(S, dtype=jnp.int32)
    dt = q.dtype

    q = rms_norm(q.reshape(B, S, NSA_HEADS, DH), q_norm_g)
    q = q.reshape(B, S, G, HPG, DH).transpose(0, 2, 3, 1, 4)
    q = partial_rope(q, t) * (DH ** -0.5)

    def kv_heads(a):
        return a.reshape(B, S, G, DH).transpose(0, 2, 1, 3)
    kc, vc, ks, vs, kw, vw = map(kv_heads, (kc, vc, ks, vs, kw, vw))

    n_cmp = (S - CMP_BLOCK) // CMP_STRIDE + 1
    cmp_start = np.arange(n_cmp) * CMP_STRIDE
    blk_tok = cmp_start[:, None] + np.arange(CMP_BLOCK)[None, :]

    def compress(tok, j):
        blocks = tok[:, :, blk_tok] + cmp_pos[j]
        flat = blocks.reshape(B, G, n_cmp, CMP_BLOCK * DH)
        return jax.nn.gelu(flat @ cmp_w1[j]) @ cmp_w2[j]

    cmp_end = jnp.asarray(cmp_start + CMP_BLOCK - 1, dtype=jnp.int32)
    k_cmp = partial_rope(rms_norm(compress(kc, 0), k_norm_g[0]), cmp_end)
    v_cmp = compress(vc, 1)
    s_cmp = jnp.einsum('bghsd,bgnd->bghsn', q, k_cmp)
    p_cmp = masked_softmax(s_cmp, cmp_end[None, :] <= t[:, None])
    o_cmp = jnp.einsum('bghsn,bgnd->bghsd', p_cmp.astype(dt), v_cmp)

    n_slc = S // SLC_BLOCK
    n_top = min(SLC_TOPK, n_slc)
    slc_start = np.arange(n_slc) * SLC_BLOCK
    overlap = np.clip(np.minimum(cmp_start[:, None] + CMP_BLOCK, slc_start[None, :] + SLC_BLOCK)
                      - np.maximum(cmp_start[:, None], slc_start[None, :]), 0, None) / CMP_STRIDE
    imp = jnp.einsum('bghsn,nj->bgsj', p_cmp, jnp.asarray(overlap, dtype=jnp.float32))
    blk = jnp.arange(n_slc, dtype=jnp.int32)[None, :]
    cur = (t // SLC_BLOCK)[:, None]
    forced = (blk == 0) | (blk == cur) | (blk == cur - 1)
    valid = blk * SLC_BLOCK <= t[:, None]
    imp = jnp.where(forced, jnp.inf, jnp.where(valid, imp, -jnp.inf))
    top_val, top_idx = lax.top_k(imp, n_top)
    top_ok = top_val > -jnp.inf

    k_sb = rms_norm(ks, k_norm_g[1])
    k_sb = partial_rope(k_sb, t).reshape(B, G, n_slc, SLC_BLOCK, DH)
    v_sb = vs.reshape(B, G, n_slc, SLC_BLOCK, DH)
    nqs = S // SLC_Q_BLOCK
    q_s = jnp.moveaxis(q.reshape(B, G, HPG, nqs, SLC_Q_BLOCK, DH), 3, 0)
    idx_s = jnp.moveaxis(top_idx.reshape(B, G, nqs, SLC_Q_BLOCK, n_top), 2, 0)
    ok_s = jnp.moveaxis(top_ok.reshape(B, G, nqs, SLC_Q_BLOCK, n_top), 2, 0)
    t_s = t.reshape(nqs, SLC_Q_BLOCK)
    b_ix = jnp.arange(B)[:, None, None, None]
    g_ix = jnp.arange(G)[None, :, None, None]

    def slc_block(args):
        qb, idx, ok, tq = args
        kg = k_sb[b_ix, g_ix, idx]
        vg = v_sb[b_ix, g_ix, idx]
        s = jnp.einsum('bghqd,bgqnkd->bghqnk', qb, kg)
        kpos = idx[..., None] * SLC_BLOCK + jnp.arange(SLC_BLOCK, dtype=jnp.int32)
        m = (ok[..., None] & (kpos <= tq[:, None, None]))[:, :, None]
        shp = s.shape
        p = masked_softmax(s.reshape(shp[:4] + (-1,)),
                           jnp.broadcast_to(m, shp).reshape(shp[:4] + (-1,))).reshape(shp)
        return jnp.einsum('bghqnk,bgqnkd->bghqd', p.astype(dt), vg)

    o_slc = lax.map(slc_block, (q_s, idx_s, ok_s, t_s))
    o_slc = jnp.moveaxis(o_slc, 0, 3).reshape(B, G, HPG, S, DH)

    k_w = partial_rope(rms_norm(kw, k_norm_g[2]), t)
    pad = ((0, 0), (0, 0), (WINDOW, 0), (0, 0))
    k_pad, v_pad = jnp.pad(k_w, pad), jnp.pad(vw, pad)
    nqw = S // ATTN_Q_BLOCK
    span = ATTN_Q_BLOCK + WINDOW
    q_w = jnp.moveaxis(q.reshape(B, G, HPG, nqw, ATTN_Q_BLOCK, DH), 3, 0)

    def win_block(args):
        i, qb = args
        start = i * ATTN_Q_BLOCK
        kb = lax.dynamic_slice_in_dim(k_pad, start, span, axis=2)
        vb = lax.dynamic_slice_in_dim(v_pad, start, span, axis=2)
        tq = start + jnp.arange(ATTN_Q_BLOCK, dtype=jnp.int32)
        kp = start - WINDOW + jnp.arange(span, dtype=jnp.int32)
        m = (kp[None, :] >= 0) & (kp[None, :] <= tq[:, None]) & (tq[:, None] - kp[None, :] < WINDOW)
        s = jnp.einsum('bghqd,bgkd->bghqk', qb, kb)
        p = masked_softmax(s, m)
        return jnp.einsum('bghqk,bgkd->bghqd', p.astype(dt), vb)

    o_win = lax.map(win_block, (jnp.arange(nqw, dtype=jnp.int32), q_w))
    o_win = jnp.moveaxis(o_win, 0, 3).reshape(B, G, HPG, S, DH)

    g = jax.nn.sigmoid(gate_logits.astype(jnp.float32)).reshape(B, S, G, HPG, 3).transpose(0, 2, 3, 1, 4)
    o = g[..., 0:1] * o_cmp + g[..., 1:2] * o_slc + g[..., 2:3] * o_win
    return o.transpose(0, 3, 1, 2, 4).reshape(B, S, NSA_Q).astype(dt)


def gla_mixer(q, k, v, a_low, r, w_a2, b_a, norm_g):
    B, S, _ = q.shape
    nC = S // GLA_CHUNK
    f32 = jnp.float32

    def heads(a, d):
        return a.astype(f32).reshape(B, nC, GLA_CHUNK, GLA_HEADS, d).transpose(1, 0, 3, 2, 4)

    log_alpha = jax.nn.log_sigmoid((a_low @ w_a2 + b_a).astype(f32)) / GLA_TAU
    qh = heads(q, GLA_DK) * (GLA_DK ** -0.5)
    kh, vh, lah = heads(k, GLA_DK), heads(v, GLA_DV), heads(log_alpha, GLA_DK)
    bcum = jnp.cumsum(lah, axis=3)
    q_in = qh * jnp.exp(bcum)
    k_in = kh * jnp.exp(-bcum)
    k_out = kh * jnp.exp(bcum[..., -1:, :] - bcum)
    decay = jnp.exp(bcum[..., -1, :])
    causal = jnp.tril(jnp.ones((GLA_CHUNK, GLA_CHUNK), dtype=bool))
    a_intra = jnp.where(causal, jnp.einsum('nbhid,nbhjd->nbhij', q_in, k_in), 0.0)
    o_intra = jnp.einsum('nbhij,nbhjv->nbhiv', a_intra, vh)

    def step(state, xs):
        q_c, k_c, v_c, d_c = xs
        o_inter = jnp.einsum('bhid,bhdv->bhiv', q_c, state)
        new_state = d_c[..., None] * state + jnp.einsum('bhjd,bhjv->bhdv', k_c, v_c)
        return new_state, o_inter

    state0 = jnp.zeros((B, GLA_HEADS, GLA_DK, GLA_DV), f32)
    _, o_inter = lax.scan(step, state0, (q_in, k_out, vh, decay))
    o = (o_intra + o_inter).transpose(1, 0, 3, 2, 4).reshape(B, S, GLA_HEADS, GLA_DV)
    o = rms_norm(o, norm_g) * jax.nn.silu(r.astype(f32).reshape(B, S, GLA_HEADS, GLA_DV))
    return o.reshape(B, S, GLA_HEADS * GLA_DV).astype(q.dtype)


def hybrid_attention(h, w_in, w_out, q_norm_g, k_norm_g, cmp_pos, cmp_w1, cmp_w2,
                     gla_w_a2, gla_b_a, gla_norm_g):
    offsets = [int(o) for o in np.cumsum(EVEN_IN_SPLITS)[:-1]]
    (q, kc, vc, ks, vs, kw, vw, gl, gq, gk, gv, ga, gr) = jnp.split(h @ w_in, offsets, axis=-1)
    o_nsa = nsa_mixer(q, kc, vc, ks, vs, kw, vw, gl, q_norm_g, k_norm_g, cmp_pos, cmp_w1, cmp_w2)
    o_gla = gla_mixer(gq, gk, gv, ga, gr, gla_w_a2, gla_b_a, gla_norm_g)
    return jnp.concatenate([o_nsa, o_gla], axis=-1) @ w_out


def conformer_conv(h, w_pw1, b_pw1, w_dw, b_dw, ln_g, ln_b, w_pw2, b_pw2):
    u = h @ w_pw1 + b_pw1
    u = u[..., :D_MODEL] * jax.nn.sigmoid(u[..., D_MODEL:])
    u = lax.conv_general_dilated(u, w_dw[:, None, :], window_strides=(1,),
                                 padding=[(CONV_WIDTH - 1, 0)],
                                 dimension_numbers=('NWC', 'WIO', 'NWC'),
                                 feature_group_count=D_MODEL) + b_dw
    u = jax.nn.silu(layer_norm(u, ln_g, ln_b))
    return u @ w_pw2 + b_pw2


def swiglu(h, w_gate, w_up, w_down):
    return (jax.nn.silu(h @ w_gate) * (h @ w_up)) @ w_down


def setup_inputs(seed: int = 0) -> dict:
    key = jax.random.key(seed)
    keys = iter(jax.random.split(key, 28))
    D = D_MODEL

    def normal(shape, scale):
        return jax.random.normal(next(keys), shape, jnp.float32) * scale

    def gain(shape):
        return 1.0 + normal(shape, 0.02)

    return {
        'x': normal((BATCH, SEQ, D), 1.0),
        'c': normal((BATCH, D), 1.0),
        'w_mod': normal((D, N_MOD * D), 0.5 * D ** -0.5),
        'b_mod': normal((N_MOD * D,), 0.02),
        'ada_table': normal((DEPTH, N_MOD, D), 0.1),
        'norm_mix_g': gain((DEPTH, D)),
        'norm_ffn_g': gain((DEPTH, D)),
        'w_in': normal((N_EVEN, D, EVEN_IN), D ** -0.5),
        'w_out': normal((N_EVEN, MIX_OUT, D), MIX_OUT ** -0.5),
        'q_norm_g': gain((N_EVEN, HEAD_DIM)),
        'k_norm_g': gain((N_EVEN, 3, HEAD_DIM)),
        'cmp_pos': normal((N_EVEN, 2, CMP_BLOCK, HEAD_DIM), 0.5),
        'cmp_w1': normal((N_EVEN, 2, CMP_BLOCK * HEAD_DIM, CMP_HIDDEN), (CMP_BLOCK * HEAD_DIM) ** -0.5),
        'cmp_w2': normal((N_EVEN, 2, CMP_HIDDEN, HEAD_DIM), CMP_HIDDEN ** -0.5),
        'gla_w_a2': normal((N_EVEN, GLA_RANK, GLA_HEADS * GLA_DK), GLA_RANK ** -0.5),
        'gla_b_a': normal((N_EVEN, GLA_HEADS * GLA_DK), 0.1),
        'gla_norm_g': gain((N_EVEN, GLA_DV)),
        'cv_w_pw1': normal((N_ODD, D, 2 * D), D ** -0.5),
        'cv_b_pw1': normal((N_ODD, 2 * D), 0.02),
        'cv_w_dw': normal((N_ODD, CONV_WIDTH, D), CONV_WIDTH ** -0.5),
        'cv_b_dw': normal((N_ODD, D), 0.02),
        'cv_ln_g': gain((N_ODD, D)),
        'cv_ln_b': normal((N_ODD, D), 0.02),
        'cv_w_pw2': normal((N_ODD, D, D), D ** -0.5),
        'cv_b_pw2': normal((N_ODD, D), 0.02),
        'ffn_w_gate': normal((DEPTH, D, FFN_HIDDEN), D ** -0.5),
        'ffn_w_up': normal((DEPTH, D, FFN_HIDDEN), D ** -0.5),
        'ffn_w_down': normal((DEPTH, FFN_HIDDEN, D), FFN_HIDDEN ** -0.5),
    }


def reference(x, c, w_mod, b_mod, ada_table, norm_mix_g, norm_ffn_g, w_in, w_out, q_norm_g,
              k_norm_g, cmp_pos, cmp_w1, cmp_w2, gla_w_a2, gla_b_a, gla_norm_g, cv_w_pw1,
              cv_b_pw1, cv_w_dw, cv_b_dw, cv_ln_g, cv_ln_b, cv_w_pw2, cv_b_pw2, ffn_w_gate,
              ffn_w_up, ffn_w_down):
    B = x.shape[0]
    mod = (jax.nn.silu(c) @ w_mod + b_mod).reshape(B, N_MOD, D_MODEL)
    for layer in range(DEPTH):
        m = mod + ada_table[layer]
        sh_a, sc_a, g_a, sh_f, sc_f, g_f = [m[:, i, None, :] for i in range(N_MOD)]
        h = rms_norm(x, norm_mix_g[layer]) * (1.0 + sc_a) + sh_a
        j = layer // 2
        if layer % 2 == 0:
            y = hybrid_attention(h, w_in[j], w_out[j], q_norm_g[j], k_norm_g[j], cmp_pos[j],
                                 cmp_w1[j], cmp_w2[j], gla_w_a2[j], gla_b_a[j], gla_norm_g[j])
        else:
            y = conformer_conv(h, cv_w_pw1[j], cv_b_pw1[j], cv_w_dw[j], cv_b_dw[j], cv_ln_g[j],
                               cv_ln_b[j], cv_w_pw2[j], cv_b_pw2[j])
        x = x + g_a * y
        h = rms_norm(x, norm_ffn_g[layer]) * (1.0 + sc_f) + sh_f
        x = x + g_f * swiglu(h, ffn_w_gate[layer], ffn_w_up[layer], ffn_w_down[layer])
    return x
```

```python
import contextlib
import numpy as np
import concourse.bass as bass
import concourse.mybir as mybir
from concourse.bass_utils import run_bass_kernel_spmd

F32 = mybir.dt.float32
BF16 = mybir.dt.bfloat16
ALU = mybir.AluOpType
AF = mybir.ActivationFunctionType
AX = mybir.AxisListType

S = 2048
D = 4096
KC = 32
FH = 11008
FC = 86
DEPTH = 4
EVEN_IN = 11328
NT = S // 128


class Buf:
    __slots__ = ("name", "writers", "readers", "pw", "pr", "excl", "opener")

    def __init__(self, name="", excl=False):
        self.name = name
        self.excl = excl
        self.opener = None
        self.writers = []
        self.readers = []
        self.pw = []
        self.pr = []


class Ins:
    __slots__ = ("eng", "fn", "deps", "signal", "sigidx", "dma", "sem", "semval", "idx", "ndma")


class FW:
    ENGS = ("pe", "act", "dve", "pool", "sp")

    def __init__(self, nc):
        self.nc = nc
        self.ins = []
        self.last = {e: None for e in self.ENGS}
        self.last_dma = {}

    def op(self, eng, fn, reads=(), writes=(), dma=False, ndma=1, semkey=None, join=False, extra=None):
        i = Ins()
        i.eng, i.fn, i.dma, i.ndma = eng, fn, dma, ndma
        i.signal, i.sigidx, i.sem, i.semval = False, None, None, None
        i.idx = len(self.ins)
        deps = {}
        xr = [b for b in reads if b.excl]
        reads = [b for b in reads if not b.excl]
        for b in reads:
            for w in b.writers:
                deps[w] = "raw"
        for b in xr:
            for w in b.writers:
                deps[w] = "raw"
            for r in b.readers:
                deps.setdefault(r, "war")
        for b in writes:
            if join and not b.excl:
                for w in b.pw:
                    deps.setdefault(w, "waw")
                for r in b.pr:
                    deps.setdefault(r, "war")
                if b.opener is not None:
                    deps.setdefault(b.opener, "raw")
            else:
                for w in b.writers:
                    deps.setdefault(w, "waw")
            for r in b.readers:
                deps.setdefault(r, "war")
        if extra:
            for j in extra:
                deps[j] = "raw"
        for b in reads:
            b.readers.append(i.idx)
        for b in xr:
            b.readers.append(i.idx)
        for b in writes:
            if join and not b.excl:
                b.writers.append(i.idx)
            else:
                b.pw, b.pr = b.writers, b.readers
                b.writers = [i.idx]
                b.readers = []
                b.opener = i.idx
        deps.pop(i.idx, None)
        i.deps = deps
        if dma:
            i.sem = semkey if semkey is not None else writes[0]
            self.last_dma[id(i.sem)] = i.idx
        else:
            if fn is not None:
                self.last[eng] = i.idx
        self.ins.append(i)
        return i

    def barrier(self):
        ex = [v for v in self.last.values() if v is not None] + list(self.last_dma.values())
        for e in self.ENGS:
            self.op(e, None, extra=ex)
        self.last_dma = {}

    def emit(self):
        nc = self.nc
        ins = self.ins
        for i in ins:
            real = {}
            best = {}
            for j, kind in i.deps.items():
                pj = ins[j]
                if not pj.dma and not i.dma and pj.eng == i.eng and i.fn is not None:
                    if i.eng == "pe" or (kind != "raw" and i.eng != "pool"):
                        continue
                if pj.dma:
                    real[j] = kind
                else:
                    if pj.eng not in best or j > best[pj.eng]:
                        best[pj.eng] = j
            for e_, j in best.items():
                real[j] = "dep"
                ins[j].signal = True
            i.deps = real
        cnt = {e: 0 for e in self.ENGS}
        for i in ins:
            if not i.dma and i.signal:
                cnt[i.eng] += 1
                i.sigidx = cnt[i.eng]
        klast = {}
        for i in ins:
            if i.dma:
                klast[id(i.sem)] = max(klast.get(id(i.sem), 0), i.idx)
        for i in ins:
            for j in i.deps:
                if ins[j].dma:
                    k = id(ins[j].sem)
                    klast[k] = max(klast[k], i.idx)
        bar_ends = [i.idx for i in ins if i.fn is None and i.eng == "sp" and len(i.deps) > 1]
        import bisect
        phys = []
        keys = {}
        for i in ins:
            if not i.dma:
                continue
            k = id(i.sem)
            if k not in keys:
                pos = bisect.bisect_left(bar_ends, i.idx)
                lastbar = bar_ends[pos - 1] if pos > 0 else -1
                found = None
                sw = (i.eng == "pool")
                if not sw:
                    for n, p in enumerate(phys):
                        if p[0] <= lastbar and not p[2]:
                            found = n
                            break
                if found is None:
                    phys.append([klast[k], 0, sw])
                    found = len(phys) - 1
                else:
                    phys[found][0] = klast[k]
                keys[k] = found
            n = keys[k]
            phys[n][1] += 16 * i.ndma
            i.semval = phys[n][1]
            i.sem = n
        self.n_dma_sems = len(phys)
        with contextlib.ExitStack() as st:
            esem = {e: st.enter_context(nc.semaphore("s_" + e)) for e in self.ENGS}
            dsem = [st.enter_context(nc.semaphore("d_%d" % n)) for n in range(len(phys))]
            for i in ins:
                if i.dma:
                    i.sem = dsem[i.sem]
            block = st.enter_context(nc.Block())
            per = {e: [i for i in ins if i.eng == e] for e in self.ENGS}

            def run(e, engobj):
                waited = {}
                for i in per[e]:
                    for j in sorted(i.deps):
                        pj = ins[j]
                        s, v = (pj.sem, pj.semval) if pj.dma else (esem[pj.eng], pj.sigidx)
                        if waited.get(id(s), 0) >= v:
                            continue
                        waited[id(s)] = v
                        engobj.wait_ge(s, v)
                    if i.fn is None:
                        continue
                    r = i.fn(engobj)
                    if i.dma:
                        rs = r if isinstance(r, (list, tuple)) else [r]
                        assert len(rs) == i.ndma, (len(rs), i.ndma)
                        for x in rs:
                            x.then_inc(i.sem, 16)
                    elif i.signal:
                        last = r[-1] if isinstance(r, (list, tuple)) else r
                        last.then_inc(esem[e], 1)

            @block.tensor
            def _(eng):
                run("pe", eng)

            @block.scalar
            def _(eng):
                run("act", eng)

            @block.vector
            def _(eng):
                run("dve", eng)

            @block.gpsimd
            def _(eng):
                run("pool", eng)

            @block.sync
            def _(eng):
                run("sp", eng)
        return cnt


class Rot:
    def __init__(self, tiles):
        self.tiles = tiles
        self.bufs = [Buf() for _ in tiles]
        self.n = 0

    def next(self):
        k = self.n % len(self.tiles)
        self.n += 1
        return self.tiles[k], self.bufs[k]


class KB:
    def __init__(self, cfg, taps=()):
        self.cfg = cfg
        self.nc = nc = bass.Bass("TRN2", target_bir_lowering=False)
        self.fw = FW(nc)
        self.st = contextlib.ExitStack()
        self.din = {}
        self.nscr = 0

    def inp(self, name, shape, dt=F32):
        if name not in self.din:
            self.din[name] = self.nc.dram_tensor(name, list(shape), dt, kind="ExternalInput").ap()
        return self.din[name]

    def scratch(self, shape, dt):
        self.nscr += 1
        return self.nc.dram_tensor("scr%d" % self.nscr, list(shape), dt, kind="Internal").ap()

    def arena_init(self, words):
        self.arena = self.st.enter_context(self.nc.sbuf_tensor("arena", [128, words], F32))
        self.awords = words
        self.aoff = 0
        self.amark = []

    def alloc(self, shape, dt=F32):
        n = int(np.prod(shape[1:]))
        w = n if dt == F32 else (n + 1) // 2
        assert self.aoff + w <= self.awords, ("arena overflow", self.aoff, w, self.awords)
        ap = self.arena[:, self.aoff:self.aoff + w]
        self.aoff += w
        if dt != F32:
            ap = ap.bitcast(dt)
            if n % 2:
                ap = ap[:, 0:n]
        if len(shape) == 3:
            ap = ap.rearrange("p (a b) -> p a b", a=shape[1])
        elif len(shape) == 4:
            ap = ap.rearrange("p (a b c) -> p a b c", a=shape[1], b=shape[2])
        if shape[0] < 128:
            ap = ap[0:shape[0]]
        return ap

    def push(self):
        self.amark.append(self.aoff)

    def pop(self):
        self.fw.barrier()
        self.aoff = self.amark.pop()


def wblocks_plain(K, N, kmax=32, nw=256):
    out = []
    kc = K // 128
    nks = (kc + kmax - 1) // kmax
    ksz = [(kc + nks - 1 - i) // nks for i in range(nks)]
    for n0 in range(0, N, nw):
        w = min(nw, N - n0)
        k0 = 0
        for kk in ksz:
            out.append((k0, kk, n0, w))
            k0 += kk
    return out


class WMat:
    def __init__(self, kb, name, w_ap, K, N, kmax=32, nw=256, segs=None):
        self.blocks = []
        self.buf = Buf(name)
        fw = kb.fw
        wv = w_ap.rearrange("(c p) n -> p c n", p=128)
        bl = wblocks_plain(K, N, kmax, nw) if segs is None else [(0, K // 128, n0, w) for (n0, w) in segs]
        km = max(b[1] for b in bl)
        scr_all = kb.scratch([len(bl), 128, km * nw], BF16)
        for bi, (k0, kk, n0, w) in enumerate(bl):
            scr = scr_all[bi, :, 0:kk * w]
            self.blocks.append((scr, k0, kk, n0, w))
            fw.op("pool", lambda e, scr=scr, k0=k0, kk=kk, n0=n0, w=w: e.dma_start(
                out=scr.rearrange("p (c n) -> p c n", c=kk), in_=wv[:, k0:k0 + kk, n0:n0 + w]),
                writes=[self.buf], dma=True, semkey=self.buf, join=True)


def build(cfg):
    kb = KB(cfg)
    nc, fw = kb.nc, kb.fw
    st = kb.st
    x_in = kb.inp("x", [S, D])
    out_ap = nc.dram_tensor("out", [S, D], F32, kind="ExternalOutput").ap()
    xT = kb.scratch([KC, 128, S], F32)
    b_xT = [Buf("xT%d" % t) for t in range(NT)]

    kb.arena_init(51000)
    ps = [st.enter_context(nc.psum_tensor("ps%d" % i, [128, 512], F32)) for i in range(8)]
    b_ps = [Buf("ps%d" % i, excl=True) for i in range(8)]

    identf = kb.alloc([128, 128], F32)
    identb = kb.alloc([128, 128], BF16)
    onesD = kb.alloc([128, 128], BF16)
    b_const = Buf("const")
    c_ident = kb.inp("c_ident", [128, 128])
    fw.op("sp", lambda e: e.dma_start(out=identf, in_=c_ident), writes=[b_const], dma=True)
    fw.op("act", lambda e: e.activation(out=identb, in_=identf, func=AF.Copy), reads=[b_const], writes=[b_const])
    fw.op("pool", lambda e: e.memset(onesD, 1.0 / D), writes=[b_const], join=True)

    nslot = 3
    wslots = Rot([kb.alloc([128, 32 * 256], BF16) for _ in range(nslot)])

    def load_block(wm, bi):
        scr, k0, kk, n0, w = wm.blocks[bi]
        t, b = wslots.next()
        fw.op("sp", lambda e: e.dma_start(out=t[:, 0:kk * w], in_=scr), reads=[wm.buf], writes=[b], dma=True)
        return t[:, 0:kk * w].rearrange("p (c n) -> p c n", c=kk), b, k0, kk, n0, w

    modv = kb.alloc([128, DEPTH, 6, KC], F32)
    b_mod = Buf("mod")
    if cfg.get("mod", True):
        kb.push()
        cT_in = kb.inp("cT", [128, KC])
        bmod_in = kb.inp("b_modT", [128, 6 * KC])
        ada_in = kb.inp("adaT", [128, DEPTH * 6 * KC])
        gmix_in = kb.inp("gmixT", [128, DEPTH * KC])
        gffn_in = kb.inp("gffnT", [128, DEPTH * KC])
        wmod_in = kb.inp("w_mod", [D, 6 * D])
        cT = kb.alloc([128, KC], F32)
        sc = kb.alloc([128, KC], F32)
        bm = kb.alloc([128, 6, KC], F32)
        ada = kb.alloc([128, DEPTH, 6, KC], F32)
        gmx = kb.alloc([128, DEPTH, KC], F32)
        gff = kb.alloc([128, DEPTH, KC], F32)
        b_v = Buf()
        fw.op("sp", lambda e: [e.dma_start(out=cT, in_=cT_in),
                               e.dma_start(out=bm.rearrange("p a b -> p (a b)"), in_=bmod_in),
                               e.dma_start(out=ada.rearrange("p a b c -> p (a b c)"), in_=ada_in),
                               e.dma_start(out=gmx.rearrange("p a b -> p (a b)"), in_=gmix_in),
                               e.dma_start(out=gff.rearrange("p a b -> p (a b)"), in_=gffn_in)],
              writes=[b_v], dma=True, ndma=5)
        b_sc = Buf()
        fw.op("act", lambda e: e.activation(out=sc, in_=cT, func=AF.Silu), reads=[b_v], writes=[b_sc])
        wpan = Rot([kb.alloc([128, KC, 256], F32) for _ in range(2)])
        wmv = wmod_in.rearrange("(c p) n -> p c n", p=128)
        psm = ps[0][:, 0:192]
        for pn in range(96):
            t, b = wpan.next()
            fw.op("sp", lambda e, t=t, pn=pn: e.dma_start(out=t, in_=wmv[:, :, pn * 256:(pn + 1) * 256]), writes=[b], dma=True)
            for j in range(2):
                ntile = pn * 2 + j
                for c in range(KC):
                    fw.op("pe", lambda e, t=t, j=j, c=c, ntile=ntile: e.matmul(
                        psm[:, ntile:ntile + 1], lhsT=t[:, c, j * 128:(j + 1) * 128], rhs=sc[:, c:c + 1],
                        start=(c == 0), stop=(c == KC - 1)), reads=[b, b_sc], writes=[b_ps[0]])
        mm = kb.alloc([128, 6, KC], F32)
        b_mm = Buf()
        fw.op("dve", lambda e: e.tensor_tensor(out=mm.rearrange("p a b -> p (a b)"), in0=psm, in1=bm.rearrange("p a b -> p (a b)"), op=ALU.add),
              reads=[b_ps[0], b_v], writes=[b_mm])
        for l in range(DEPTH):
            fw.op("dve", lambda e, l=l: e.tensor_tensor(out=modv[:, l], in0=mm, in1=ada[:, l], op=ALU.add),
                  reads=[b_mm, b_v], writes=[b_mod], join=(l > 0))
        for l in range(DEPTH):
            for (si, gsrc) in ((1, gmx), (4, gff)):
                fw.op("dve", lambda e, l=l, si=si, gsrc=gsrc: e.scalar_tensor_tensor(
                    out=modv[:, l, si], in0=modv[:, l, si], scalar=1.0, in1=gsrc[:, l], op0=ALU.add, op1=ALU.mult),
                    reads=[b_mod, b_v], writes=[b_mod])
        kb.pop()

    layers = cfg["layers"]
    WM = {}
    for (kind, l) in layers:
        j = l // 2
        if kind == "ffn":
            wg = kb.inp("ffn_w_gate", [DEPTH, D, FH])
            wu = kb.inp("ffn_w_up", [DEPTH, D, FH])
            wd = kb.inp("ffn_w_down", [DEPTH, FH, D])
            WM[("g", l)] = WMat(kb, "wg%d" % l, wg[l], D, FH)
            WM[("u", l)] = WMat(kb, "wu%d" % l, wu[l], D, FH)
            WM[("d", l)] = WMat(kb, "wd%d" % l, wd[l], FH, D)
        elif kind == "even":
            wi = kb.inp("w_in", [2, D, EVEN_IN])
            wo = kb.inp("w_out", [2, D, D])
            WM[("win", l)] = WMat(kb, "win%d" % l, wi[j], D, EVEN_IN, segs=EV_SEGS)
            WM[("wout", l)] = WMat(kb, "wout%d" % l, wo[j], D, D)
        elif kind == "odd":
            w1 = kb.inp("cv_w_pw1", [2, D, 2 * D])
            w2 = kb.inp("cv_w_pw2", [2, D, D])
            WM[("pw1", l)] = WMat(kb, "pw1_%d" % l, w1[j], D, 2 * D, nw=128)
            WM[("pw2", l)] = WMat(kb, "pw2_%d" % l, w2[j], D, D)

    if cfg.get("xin", True):
        kb.push()
        xrow = Rot([kb.alloc([128, D], F32) for _ in range(2)])
        xst = Rot([kb.alloc([128, 4, 128], F32) for _ in range(4)])
        for t in range(NT):
            xr, bx = xrow.next()
            fw.op("sp", lambda e, xr=xr, t=t: e.dma_start(out=xr, in_=x_in[t * 128:(t + 1) * 128, :]), writes=[bx], dma=True)
            for cg in range(8):
                pb = 4 + (cg % 2)
                pst = ps[pb][:].rearrange("p (a b) -> p a b", a=4)
                for k in range(4):
                    c = cg * 4 + k
                    fw.op("pe", lambda e, xr=xr, c=c, k=k, pst=pst: e.transpose(out=pst[:, k, :], in_=xr[:, c * 128:(c + 1) * 128], identity=identf),
                          reads=[bx, b_const], writes=[b_ps[pb]])
                xs, bs = xst.next()
                eng = "dve" if cg % 2 == 0 else "act"
                if eng == "dve":
                    fw.op("dve", lambda e, xs=xs, pst=pst: e.tensor_copy(out=xs, in_=pst), reads=[b_ps[pb]], writes=[bs])
                else:
                    fw.op("act", lambda e, xs=xs, pst=pst: e.activation(out=xs, in_=pst, func=AF.Copy), reads=[b_ps[pb]], writes=[bs])
                fw.op("act", lambda e, xs=xs, cg=cg, t=t: e.dma_start(
                    out=xT[cg * 4:(cg + 1) * 4, :, t * 128:(t + 1) * 128].rearrange("c p n -> p c n"), in_=xs),
                    reads=[bs], writes=[b_xT[t]], dma=True, semkey=bs, join=True)
        kb.pop()

    def norm_stage(l, which, t0, n, hT, b_hT, xc_rot, sq_rot):
        sh = modv[:, l, 3 * which + 0]
        gsc = modv[:, l, 3 * which + 1]
        tl = list(range(t0 // 128, (t0 + n) // 128))
        rb = [b_xT[t] for t in tl]
        pss = ps[6][:, 0:n]
        for c in range(KC):
            xc, bxc = xc_rot.next()
            fw.op("sp", lambda e, xc=xc, c=c: e.dma_start(out=xc[:, 0:n], in_=xT[c, :, t0:t0 + n]), reads=rb, writes=[bxc], dma=True)
            sq, bsq = sq_rot.next()
            fw.op("act", lambda e, xc=xc, sq=sq: e.activation(out=sq[:, 0:n], in_=xc[:, 0:n], func=AF.Square), reads=[bxc], writes=[bsq])
            fw.op("pe", lambda e, sq=sq, c=c: e.matmul(pss, lhsT=onesD, rhs=sq[:, 0:n], start=(c == 0), stop=(c == KC - 1)),
                  reads=[bsq, b_const], writes=[b_ps[6]])
        rstd = kb_rstd[:, 0:n]
        fw.op("act", lambda e: e.activation(out=rstd, in_=pss, func=AF.Sqrt, bias=eps6[:, 0:1], scale=1.0), reads=[b_ps[6], b_const], writes=[b_rstd])
        fw.op("dve", lambda e: e.reciprocal(out=rstd, in_=rstd), reads=[b_rstd], writes=[b_rstd])
        for c in range(KC):
            xc, bxc = xc_rot.next()
            fw.op("sp", lambda e, xc=xc, c=c: e.dma_start(out=xc[:, 0:n], in_=xT[c, :, t0:t0 + n]), reads=rb, writes=[bxc], dma=True)
            fw.op("dve", lambda e, xc=xc: e.tensor_tensor(out=xc[:, 0:n], in0=xc[:, 0:n], in1=rstd, op=ALU.mult), reads=[bxc, b_rstd], writes=[bxc])
            fw.op("act", lambda e, xc=xc, c=c: e.activation(out=hT[:, c, 0:n], in_=xc[:, 0:n], func=AF.Identity,
                                                            bias=sh[:, c:c + 1], scale=gsc[:, c:c + 1]),
                  reads=[bxc, b_mod], writes=[b_hT], join=(c > 0))

    kb_rstd = kb.alloc([128, 512], F32)
    b_rstd = Buf("rstd")
    eps6 = kb.alloc([128, 2], F32)
    fw.op("pool", lambda e: e.memset(eps6[:, 0:1], 1e-6), writes=[b_const], join=True)
    fw.op("pool", lambda e: e.memset(eps6[:, 1:2], 1e-5), writes=[b_const], join=True)

    def resid_update(l, which, c2, t0, n, psum_ap, b_psum, xr_rot):
        gate = modv[:, l, 3 * which + 2]
        tl = list(range(t0 // 128, (t0 + n) // 128))
        rb = [b_xT[t] for t in tl]
        xr, bxr = xr_rot.next()
        fw.op("sp", lambda e: e.dma_start(out=xr[:, 0:n], in_=xT[c2, :, t0:t0 + n]), reads=rb, writes=[bxr], dma=True)
        fw.op("dve", lambda e: e.scalar_tensor_tensor(out=xr[:, 0:n], in0=psum_ap, scalar=gate[:, c2:c2 + 1], in1=xr[:, 0:n],
                                                      op0=ALU.mult, op1=ALU.add), reads=[bxr, b_psum, b_mod], writes=[bxr])
        fw.op("act", lambda e: e.dma_start(out=xT[c2, :, t0:t0 + n], in_=xr[:, 0:n]), reads=[bxr], writes=rb, dma=True, semkey=bxr, join=True)

    def ffn_layer(l):
        kb.push()
        TG = 512
        hT = kb.alloc([128, KC, TG], BF16)
        b_hT = Buf("hT")
        actT = kb.alloc([128, FC, TG], BF16)
        b_act = [Buf() for _ in range(FC)]
        xc_rot = Rot([kb.alloc([128, TG], F32) for _ in range(4)])
        sq_rot = Rot([kb.alloc([128, TG], BF16) for _ in range(2)])
        sg_rot = Rot([kb.alloc([128, TG], F32) for _ in range(4)])
        wg, wu, wd = WM[("g", l)], WM[("u", l)], WM[("d", l)]
        for tg in range(S // TG):
            t0 = tg * TG
            norm_stage(l, 1, t0, TG, hT, b_hT, xc_rot, sq_rot)
            for fp in range(FC // 2):
                wt, bw, k0, kk, n0, w = load_block(wg, fp)
                sgs = []
                for j in range(2):
                    for c in range(KC):
                        fw.op("pe", lambda e, wt=wt, j=j, c=c: e.matmul(ps[j][:], lhsT=wt[:, c, j * 128:(j + 1) * 128], rhs=hT[:, c, :],
                                                                       start=(c == 0), stop=(c == KC - 1)), reads=[bw, b_hT], writes=[b_ps[j]])
                    sg, bsg = sg_rot.next()
                    fw.op("act", lambda e, sg=sg, j=j: e.activation(out=sg, in_=ps[j][:], func=AF.Silu), reads=[b_ps[j]], writes=[bsg])
                    sgs.append((sg, bsg))
                wt, bw, k0, kk, n0, w = load_block(wu, fp)
                for j in range(2):
                    for c in range(KC):
                        fw.op("pe", lambda e, wt=wt, j=j, c=c: e.matmul(ps[2 + j][:], lhsT=wt[:, c, j * 128:(j + 1) * 128], rhs=hT[:, c, :],
                                                                       start=(c == 0), stop=(c == KC - 1)), reads=[bw, b_hT], writes=[b_ps[2 + j]])
                    sg, bsg = sgs[j]
                    f = fp * 2 + j
                    fw.op("dve", lambda e, sg=sg, j=j, f=f: e.tensor_tensor(out=actT[:, f, :], in0=sg, in1=ps[2 + j][:], op=ALU.mult),
                          reads=[bsg, b_ps[2 + j]], writes=[b_act[f]])
            nks = len(wd.blocks) // (D // 256)
            for dp in range(D // 256):
                for ks in range(nks):
                    wt, bw, k0, kk, n0, w = load_block(wd, dp * nks + ks)
                    for j in range(2):
                        pbd = 4 + 2 * (dp % 2) + j
                        for c in range(kk):
                            f = k0 + c
                            fw.op("pe", lambda e, wt=wt, j=j, c=c, f=f, pbd=pbd: e.matmul(ps[pbd][:], lhsT=wt[:, c, j * 128:(j + 1) * 128], rhs=actT[:, f, :],
                                                                                         start=(f == 0), stop=(f == FC - 1)),
                                  reads=[bw, b_act[f]], writes=[b_ps[pbd]])
                for j in range(2):
                    pbd = 4 + 2 * (dp % 2) + j
                    resid_update(l, 1, dp * 2 + j, t0, TG, ps[pbd][:], b_ps[pbd], xc_rot)
        kb.pop()

    def odd_layer(l):
        j = l // 2
        kb.push()
        TG = 512
        W = 31
        hT = kb.alloc([128, KC, TG], BF16)
        b_hT = Buf("hT")
        vT = kb.alloc([128, KC, TG], F32)
        b_v = [Buf() for _ in range(KC)]
        halo = kb.alloc([128, KC, 32], F32)
        b_halo = [Buf() for _ in range(KC)]
        xc_rot = Rot([kb.alloc([128, TG], F32) for _ in range(4)])
        sq_rot = Rot([kb.alloc([128, TG], BF16) for _ in range(2)])
        u_rot = Rot([kb.alloc([128, 32 + TG], F32) for _ in range(2)])
        a_rot = Rot([kb.alloc([128, TG], F32) for _ in range(2)])
        vb_rot = Rot([kb.alloc([128, 2, TG], BF16) for _ in range(2)])
        v2_rot = Rot([kb.alloc([128, TG], F32) for _ in range(2)])
        tm_rot = Rot([kb.alloc([128, TG], F32) for _ in range(4)])
        cvv = kb.alloc([128, 6, KC], F32)
        wdw = kb.alloc([128, KC, W], F32)
        b_cv = Buf()
        cv_in = kb.inp("cv_vecT", [2, 128, 6 * KC])
        wdw_in = kb.inp("cv_wdwT", [2, 128, KC * W])
        fw.op("sp", lambda e: [e.dma_start(out=cvv.rearrange("p a b -> p (a b)"), in_=cv_in[j]),
                               e.dma_start(out=wdw.rearrange("p a b -> p (a b)"), in_=wdw_in[j])], writes=[b_cv], dma=True, ndma=2)
        fw.op("pool", lambda e: e.memset(halo, 0.0), writes=b_halo)
        w1, w2 = WM[("pw1", l)], WM[("pw2", l)]
        mu = kb.alloc([128, TG], F32)
        rs = kb.alloc([128, TG], F32)
        b_st = Buf()
        for tg in range(S // TG):
            t0 = tg * TG
            norm_stage(l, 0, t0, TG, hT, b_hT, xc_rot, sq_rot)
            for c in range(KC):
                wa, bwa, *_ = load_block(w1, c)
                for k in range(KC):
                    fw.op("pe", lambda e, wa=wa, k=k: e.matmul(ps[0][:], lhsT=wa[:, k, :], rhs=hT[:, k, :], start=(k == 0), stop=(k == KC - 1)),
                          reads=[bwa, b_hT], writes=[b_ps[0]])
                wb_, bwb, *_ = load_block(w1, KC + c)
                for k in range(KC):
                    fw.op("pe", lambda e, wb_=wb_, k=k: e.matmul(ps[1][:], lhsT=wb_[:, k, :], rhs=hT[:, k, :], start=(k == 0), stop=(k == KC - 1)),
                          reads=[bwb, b_hT], writes=[b_ps[1]])
                u, bu = u_rot.next()
                a, ba = a_rot.next()
                fw.op("act", lambda e, a=a, c=c: e.activation(out=a, in_=ps[1][:], func=AF.Sigmoid, bias=cvv[:, 1, c:c + 1], scale=1.0),
                      reads=[b_ps[1], b_cv], writes=[ba])
                fw.op("pool", lambda e, u=u, c=c: e.tensor_copy(out=u[:, 0:32], in_=halo[:, c, :]), reads=[b_halo[c]], writes=[bu])
                fw.op("dve", lambda e, u=u, a=a, c=c: e.scalar_tensor_tensor(out=u[:, 32:32 + TG], in0=ps[0][:], scalar=cvv[:, 0, c:c + 1], in1=a,
                                                                          op0=ALU.add, op1=ALU.mult), reads=[b_ps[0], ba, b_cv], writes=[bu], join=True)
                fw.op("pool", lambda e, u=u, c=c: e.tensor_copy(out=halo[:, c, :], in_=u[:, TG:TG + 32]), reads=[bu], writes=[b_halo[c]])
                NDV = 19
                v2, bv2 = v2_rot.next()
                for wi in range(W):
                    src = u[:, 2 + wi:2 + wi + TG]
                    if wi == 0:
                        fw.op("dve", lambda e, src=src, c=c: e.tensor_scalar(out=vT[:, c, :], in0=src, scalar1=wdw[:, c, 0:1], scalar2=cvv[:, 2, c:c + 1],
                                                                             op0=ALU.mult, op1=ALU.add), reads=[bu, b_cv], writes=[b_v[c]])
                    elif wi < NDV:
                        fw.op("dve", lambda e, src=src, c=c, wi=wi: e.scalar_tensor_tensor(out=vT[:, c, :], in0=src, scalar=wdw[:, c, wi:wi + 1], in1=vT[:, c, :],
                                                                                           op0=ALU.mult, op1=ALU.add), reads=[bu, b_cv, b_v[c]], writes=[b_v[c]])
                    elif wi == NDV:
                        fw.op("act", lambda e, src=src, c=c, wi=wi, v2=v2: e.activation(out=v2, in_=src, func=AF.Copy, scale=wdw[:, c, wi:wi + 1]),
                              reads=[bu, b_cv], writes=[bv2])
                    else:
                        tm, btm = tm_rot.next()
                        fw.op("act", lambda e, src=src, c=c, wi=wi, tm=tm: e.activation(out=tm, in_=src, func=AF.Copy, scale=wdw[:, c, wi:wi + 1]),
                              reads=[bu, b_cv], writes=[btm])
                        fw.op("pool", lambda e, tm=tm, v2=v2: e.tensor_tensor(out=v2, in0=v2, in1=tm, op=ALU.add), reads=[btm, bv2], writes=[bv2])
                fw.op("dve", lambda e, c=c, v2=v2: e.tensor_tensor(out=vT[:, c, :], in0=vT[:, c, :], in1=v2, op=ALU.add), reads=[b_v[c], bv2], writes=[b_v[c]])
                vb, bvb = vb_rot.next()
                fw.op("act", lambda e, vb=vb, c=c: e.activation(out=vb[:, 0, :], in_=vT[:, c, :], func=AF.Copy), reads=[b_v[c]], writes=[bvb])
                fw.op("act", lambda e, vb=vb, c=c: e.activation(out=vb[:, 1, :], in_=vT[:, c, :], func=AF.Square), reads=[b_v[c]], writes=[bvb], join=True)
                fw.op("pe", lambda e, vb=vb, c=c: e.matmul(ps[2][:], lhsT=onesD, rhs=vb[:, 0, :], start=(c == 0), stop=(c == KC - 1)),
                      reads=[bvb, b_const], writes=[b_ps[2]])
                fw.op("pe", lambda e, vb=vb, c=c: e.matmul(ps[3][:], lhsT=onesD, rhs=vb[:, 1, :], start=(c == 0), stop=(c == KC - 1)),
                      reads=[bvb, b_const], writes=[b_ps[3]])
            fw.op("act", lambda e: e.activation(out=mu, in_=ps[2][:], func=AF.Copy), reads=[b_ps[2]], writes=[b_st])
            fw.op("dve", lambda e: e.tensor_tensor(out=rs, in0=mu, in1=mu, op=ALU.mult), reads=[b_st], writes=[b_st])
            fw.op("dve", lambda e: e.tensor_tensor(out=rs, in0=ps[3][:], in1=rs, op=ALU.subtract), reads=[b_st, b_ps[3]], writes=[b_st])
            fw.op("act", lambda e: e.activation(out=rs, in_=rs, func=AF.Sqrt, bias=eps6[:, 1:2], scale=1.0), reads=[b_st, b_const], writes=[b_st])
            fw.op("dve", lambda e: e.reciprocal(out=rs, in_=rs), reads=[b_st], writes=[b_st])
            for c in range(KC):
                fw.op("dve", lambda e, c=c: e.tensor_tensor(out=vT[:, c, :], in0=vT[:, c, :], in1=mu, op=ALU.subtract), reads=[b_v[c], b_st], writes=[b_v[c]])
                fw.op("pool", lambda e, c=c: e.tensor_tensor(out=vT[:, c, :], in0=vT[:, c, :], in1=rs, op=ALU.mult), reads=[b_v[c], b_st], writes=[b_v[c]])
                fw.op("act", lambda e, c=c: e.activation(out=hT[:, c, :], in_=vT[:, c, :], func=AF.Silu, bias=cvv[:, 4, c:c + 1], scale=cvv[:, 3, c:c + 1]),
                      reads=[b_v[c], b_cv], writes=[b_hT], join=(c > 0))
            for dp in range(D // 256):
                wt, bw, *_ = load_block(w2, dp)
                for jj in range(2):
                    c2 = dp * 2 + jj
                    for k in range(KC):
                        fw.op("pe", lambda e, wt=wt, jj=jj, k=k: e.matmul(ps[4 + jj][:], lhsT=wt[:, k, jj * 128:(jj + 1) * 128], rhs=hT[:, k, :],
                                                                         start=(k == 0), stop=(k == KC - 1)), reads=[bw, b_hT], writes=[b_ps[4 + jj]])
                    y, by = a_rot.next()
                    fw.op("act", lambda e, y=y, jj=jj, c2=c2: e.activation(out=y, in_=ps[4 + jj][:], func=AF.Identity, bias=cvv[:, 5, c2:c2 + 1], scale=1.0),
                          reads=[b_ps[4 + jj], b_cv], writes=[by])
                    resid_update(l, 0, c2, t0, TG, y, by, xc_rot)
        kb.pop()

    ev = {}
    kb.tapnames = []
    b_tap = Buf("tap")

    def sbtap(name, ap, buf):
        if name not in cfg.get("sbtaps", ()):
            return
        tout = nc.dram_tensor("tap_" + name, list(ap.shape), ap.dtype, kind="ExternalOutput").ap()
        kb.tapnames.append("tap_" + name)
        fw.op("sp", lambda e: e.dma_start(out=tout, in_=ap), reads=[buf], writes=[b_tap], dma=True, join=True)

    def even_scratch():
        if ev:
            return
        ev["qT"] = kb.scratch([NT, 128, 16 * 128], BF16)
        ev["onT"] = kb.scratch([NT, 128, 16 * 128], BF16)
        ev["gsig"] = kb.scratch([S, 48], F32)
        ev["gq"] = kb.scratch([S, 1024], F32)
        ev["gk"] = kb.scratch([S, 1024], F32)
        ev["gv"] = kb.scratch([S, 2048], BF16)
        ev["grs"] = kb.scratch([S, 2048], BF16)
        ev["alow"] = kb.scratch([16, S], F32)
        ev["b"] = {k: Buf("ev_" + k) for k in ("qT", "onT", "gsig", "gq", "gk", "gv", "grs", "alow")}
        kb.evnames = {k: ev[k].tensor.name for k in ("qT", "onT", "gsig", "gq", "gk", "gv", "grs", "alow")}

    def bfview(bank, a):
        return ps[bank][:].bitcast(BF16).rearrange("p (a b) -> p a b", a=a)

    def even_layer(l):
        j = l // 2
        even_scratch()
        eb = ev["b"]
        win, wout = WM[("win", l)], WM[("wout", l)]
        kb.push()
        cs = kb.alloc([128, NT, 32], F32)
        cscmp = kb.alloc([128, 32], F32)
        gvecs = kb.alloc([128, 4, 128], F32)
        b_ec = Buf("evconst")
        cs_in = kb.inp("c_cs", [128, NT * 32])
        cscmp_in = kb.inp("c_cscmp", [128, 32])
        gv_in = kb.inp("qk_gT", [2, 128, 4 * 128])
        fw.op("sp", lambda e: [e.dma_start(out=cs.rearrange("p a b -> p (a b)"), in_=cs_in),
                               e.dma_start(out=cscmp, in_=cscmp_in),
                               e.dma_start(out=gvecs.rearrange("p a b -> p (a b)"), in_=gv_in[j])], writes=[b_ec], dma=True, ndma=3)
        ksT = kb.alloc([128, 4, S], BF16)
        kwT = kb.alloc([128, 4, S], BF16)
        vsA = kb.alloc([128, NT, 4, 130], BF16)
        vwA = kb.alloc([128, NT, 4, 130], BF16)
        kcmpT = kb.alloc([128, 4, 128], BF16)
        VC = kb.alloc([128, 4, 162], BF16)
        hid_b = kb.alloc([128, 2, 4, 128], BF16)
        w2b = kb.alloc([128, 2, 128], BF16)
        b_ks, b_kw, b_vs, b_vw, b_kc, b_VC, b_hid, b_w2 = [Buf(n) for n in "ks kw vs vw kcmp VC hid w2".split()]
        fw.op("pool", lambda e: e.memset(vsA[:, :, :, 128:130], 1.0), writes=[b_vs])
        fw.op("pool", lambda e: e.memset(vwA[:, :, :, 128:130], 1.0), writes=[b_vw])
        fw.op("pool", lambda e: e.memset(hid_b, 0.0), writes=[b_hid])
        w2_in = kb.inp("cmp_w2", [2, 2, 128, 128])
        fw.op("pool", lambda e: [e.dma_start(out=w2b[:, kv, :], in_=w2_in[j, kv]) for kv in range(2)], writes=[b_w2], dma=True, ndma=2)

        hn_sq = Rot([kb.alloc([128, 2, 128], F32) for _ in range(2)])
        hn_xn = Rot([kb.alloc([128, 2, 128], F32) for _ in range(2)])
        hn_ss = Rot([kb.alloc([128, 2], F32) for _ in range(2)])
        hn_tt = Rot([kb.alloc([128, 2, 4, 16], F32) for _ in range(2)])
        hn_ob = Rot([kb.alloc([128, 2, 128], BF16) for _ in range(3)])

        def headnorm(src, b_src, H, gi, cos, sin, b_cs, extra=None):
            sq, bsq = hn_sq.next()
            xn, bxn = hn_xn.next()
            ss, bss = hn_ss.next()
            tt_, btt = hn_tt.next()
            ob, bob = hn_ob.next()
            fw.op("act", lambda e: e.activation(out=sq[:, 0:H, :], in_=src, func=AF.Square), reads=[b_src], writes=[bsq])
            fw.op("dve", lambda e: e.tensor_reduce(out=ss[:, 0:H], in_=sq[:, 0:H, :], axis=AX.X, op=ALU.add), reads=[bsq], writes=[bss])
            fw.op("act", lambda e: e.activation(out=ss[:, 0:H], in_=ss[:, 0:H], func=AF.Sqrt, bias=eps6[:, 0:1], scale=1.0 / 128), reads=[bss, b_const], writes=[bss])
            fw.op("dve", lambda e: e.reciprocal(out=ss[:, 0:H], in_=ss[:, 0:H]), reads=[bss], writes=[bss])
            if extra is not None:
                fw.op("dve", lambda e: e.tensor_scalar(out=ss[:, 0:H], in0=ss[:, 0:H], scalar1=float(extra), scalar2=None, op0=ALU.mult), reads=[bss], writes=[bss])
            for h in range(H):
                fw.op("dve", lambda e, h=h: e.scalar_tensor_tensor(out=xn[:, h, :], in0=src[:, h, :], scalar=ss[:, h:h + 1], in1=gvecs[:, gi, :],
                                                                  op0=ALU.mult, op1=ALU.mult), reads=[b_src, bss, b_ec], writes=[bxn], join=(h > 0))
            for h in range(H):
                x1, x2 = xn[:, h, 0:16], xn[:, h, 16:32]
                t = tt_[:, h]
                fw.op("pool", lambda e, x1=x1, t=t: e.tensor_tensor(out=t[:, 0, :], in0=x1, in1=cos, op=ALU.mult), reads=[bxn, b_cs], writes=[btt], join=True)
                fw.op("pool", lambda e, x2=x2, t=t: e.tensor_tensor(out=t[:, 1, :], in0=x2, in1=sin, op=ALU.mult), reads=[bxn, b_cs], writes=[btt], join=True)
                fw.op("pool", lambda e, x2=x2, t=t: e.tensor_tensor(out=t[:, 2, :], in0=x2, in1=cos, op=ALU.mult), reads=[bxn, b_cs], writes=[btt], join=True)
                fw.op("pool", lambda e, x1=x1, t=t: e.tensor_tensor(out=t[:, 3, :], in0=x1, in1=sin, op=ALU.mult), reads=[bxn, b_cs], writes=[btt], join=True)
            for h in range(H):
                t = tt_[:, h]
                fw.op("dve", lambda e, t=t, h=h: e.tensor_tensor(out=ob[:, h, 0:16], in0=t[:, 0, :], in1=t[:, 1, :], op=ALU.subtract), reads=[btt], writes=[bob], join=(h > 0))
                fw.op("dve", lambda e, t=t, h=h: e.tensor_tensor(out=ob[:, h, 16:32], in0=t[:, 2, :], in1=t[:, 3, :], op=ALU.add), reads=[btt], writes=[bob], join=True)
            fw.op("act", lambda e: e.activation(out=ob[:, 0:H, 32:128], in_=xn[:, 0:H, 32:128], func=AF.Copy), reads=[bxn], writes=[bob], join=True)
            return ob, bob

        def transpose_to(ob, bob, H, dsts, b_dst, bank=4):
            pv = bfview(bank, 8)
            for h in range(H):
                fw.op("pe", lambda e, h=h: e.transpose(out=pv[:, h, :], in_=ob[:, h, :], identity=identb), reads=[bob, b_const], writes=[b_ps[bank]])
            for h in range(H):
                fw.op("dve", lambda e, h=h: e.tensor_copy(out=dsts[h], in_=pv[:, h, :]), reads=[b_ps[bank]], writes=[b_dst], join=True)

        kb.push()
        TG = 256
        hT = kb.alloc([128, KC, TG], BF16)
        b_hT = Buf("hT")
        xc_rot = Rot([kb.alloc([128, TG], F32) for _ in range(4)])
        sq_rot = Rot([kb.alloc([128, TG], BF16) for _ in range(2)])
        W1b = kb.alloc([128, 2, 32, 128], BF16)
        posf = kb.alloc([128, 2, 32], F32)
        posb16 = kb.alloc([128, 2, 32], BF16)
        posbias = kb.alloc([128, 2], F32)
        kcbuf = kb.alloc([128, 2, 4, 272], BF16)
        b_W1, b_pos, b_pb, b_kcb = Buf(), Buf(), Buf(), Buf()
        w1_in = kb.inp("cmp_w1", [2, 2, D, 128])
        pos_in = kb.inp("cmp_posT", [2, 128, 64])
        fw.op("pool", lambda e: [e.dma_start(out=W1b[:, kv], in_=w1_in[j, kv].rearrange("(l d) c -> d l c", d=128)) for kv in range(2)],
              writes=[b_W1], dma=True, ndma=2)
        fw.op("sp", lambda e: e.dma_start(out=posf.rearrange("p a b -> p (a b)"), in_=pos_in[j]), writes=[b_pos], dma=True)
        fw.op("act", lambda e: e.activation(out=posb16, in_=posf, func=AF.Copy), reads=[b_pos], writes=[b_pos])
        for kv in range(2):
            for li in range(32):
                fw.op("pe", lambda e, kv=kv, li=li: e.matmul(ps[5][:, kv:kv + 1], lhsT=W1b[:, kv, li, :], rhs=posb16[:, kv, li:li + 1],
                                                            start=(li == 0), stop=(li == 31)), reads=[b_W1, b_pos], writes=[b_ps[5]])
        fw.op("dve", lambda e: e.tensor_copy(out=posbias, in_=ps[5][:, 0:2]), reads=[b_ps[5]], writes=[b_pb])
        fw.op("pool", lambda e: e.memset(kcbuf, 0.0), writes=[b_kcb])
        qst = [kb.alloc([128, 16, 128], BF16) for _ in range(2)]
        b_qst = [Buf(), Buf()]
        stf = Rot([kb.alloc([128, 256], F32) for _ in range(4)])
        stb = Rot([kb.alloc([128, 256], BF16) for _ in range(4)])
        gel = kb.alloc([128, 3, 4, 16], F32)
        b_gel = Buf()
        pbn = [0]

        def nextbank():
            pbn[0] = (pbn[0] + 1) % 4
            return pbn[0]

        for tg in range(S // TG):
            t0 = tg * TG
            norm_stage(l, 0, t0, TG, hT, b_hT, xc_rot, sq_rot)
            for bi in range(len(win.blocks)):
                wt, bw, _k0, _kk, n0, w = load_block(win, bi)
                if 8 <= bi <= 11 or bi == 37:
                    ng = 2 if bi != 37 else 1
                    for gi in range(ng):
                        pb = nextbank()
                        m = 128 if bi != 37 else 16
                        for c in range(KC):
                            fw.op("pe", lambda e, wt=wt, gi=gi, c=c, pb=pb, m=m: e.matmul(ps[pb][0:m, 0:TG], lhsT=wt[:, c, gi * 128:gi * 128 + m], rhs=hT[:, c, :],
                                                                                      start=(c == 0), stop=(c == KC - 1)), reads=[bw, b_hT], writes=[b_ps[pb]])
                        if bi != 37:
                            kv, g = (bi - 8) // 2, ((bi - 8) % 2) * 2 + gi
                            fw.op("act", lambda e, kv=kv, g=g, pb=pb: e.activation(out=kcbuf[:, kv, g, 16:272], in_=ps[pb][:, 0:TG], func=AF.Copy),
                                  reads=[b_ps[pb]], writes=[b_kcb], join=True)
                        else:
                            sf, bsf = stf.next()
                            fw.op("act", lambda e, sf=sf, pb=pb: e.activation(out=sf[0:16, 0:TG], in_=ps[pb][0:16, 0:TG], func=AF.Copy), reads=[b_ps[pb]], writes=[bsf])
                            fw.op("act", lambda e, sf=sf, t0=t0: e.dma_start(out=ev["alow"][:, t0:t0 + TG], in_=sf[0:16, 0:TG]), reads=[bsf], writes=[eb["alow"]],
                                  dma=True, semkey=bsf, join=True)
                    continue
                for tt in range(2):
                    T = tg * 2 + tt
                    pb = nextbank()
                    for c in range(KC):
                        fw.op("pe", lambda e, wt=wt, c=c, pb=pb, tt=tt, w=w: e.matmul(ps[pb][:, 0:w], lhsT=hT[:, c, tt * 128:(tt + 1) * 128], rhs=wt[:, c, 0:w],
                                                                                  start=(c == 0), stop=(c == KC - 1)), reads=[bw, b_hT], writes=[b_ps[pb]])
                    pt = ps[pb][:, 0:256].rearrange("p (a b) -> p a b", a=2)
                    rows = slice(T * 128, (T + 1) * 128)
                    if bi < 8:
                        ob, bob = headnorm(pt, b_ps[pb], 2, 0, cs[:, T, 0:16], cs[:, T, 16:32], b_ec, extra=128 ** -0.5)
                        transpose_to(ob, bob, 2, [qst[tt][:, bi * 2 + h, :] for h in range(2)], b_qst[tt])
                        if bi == 7:
                            fw.op("act", lambda e, tt=tt, T=T: e.dma_start(out=ev["qT"][T], in_=qst[tt].rearrange("p a b -> p (a b)")), reads=[b_qst[tt]],
                                  writes=[eb["qT"]], dma=True, semkey=b_qst[tt], join=True)
                    elif bi in (12, 13, 16, 17):
                        isw = bi >= 16
                        ob, bob = headnorm(pt, b_ps[pb], 2, 3 if isw else 2, cs[:, T, 0:16], cs[:, T, 16:32], b_ec)
                        dstT = kwT if isw else ksT
                        g0 = (bi % 2) * 2
                        transpose_to(ob, bob, 2, [dstT[:, g0 + h, T * 128:(T + 1) * 128] for h in range(2)], b_kw if isw else b_ks)
                    elif bi in (14, 15, 18, 19):
                        isw = bi >= 18
                        dv = vwA if isw else vsA
                        g0 = (bi % 2) * 2
                        fw.op("act", lambda e, dv=dv, T=T, g0=g0, pt=pt: e.activation(out=dv[:, T, g0:g0 + 2, 0:128], in_=pt, func=AF.Copy),
                              reads=[b_ps[pb]], writes=[b_vw if isw else b_vs], join=True)
                    elif bi == 20:
                        sf, bsf = stf.next()
                        fw.op("act", lambda e, sf=sf, pb=pb: e.activation(out=sf[:, 0:48], in_=ps[pb][:, 0:48], func=AF.Sigmoid), reads=[b_ps[pb]], writes=[bsf])
                        fw.op("act", lambda e, sf=sf, rows=rows: e.dma_start(out=ev["gsig"][rows, :], in_=sf[:, 0:48]), reads=[bsf], writes=[eb["gsig"]],
                              dma=True, semkey=bsf, join=True)
                    elif 21 <= bi <= 28:
                        nm = "gq" if bi <= 24 else "gk"
                        co = ((bi - 21) % 4) * 256
                        sf, bsf = stf.next()
                        fw.op("act", lambda e, sf=sf, pb=pb: e.activation(out=sf, in_=ps[pb][:, 0:256], func=AF.Copy), reads=[b_ps[pb]], writes=[bsf])
                        fw.op("act", lambda e, sf=sf, rows=rows, nm=nm, co=co: e.dma_start(out=ev[nm][rows, co:co + 256], in_=sf), reads=[bsf], writes=[eb[nm]],
                              dma=True, semkey=bsf, join=True)
                    else:
                        isr = bi >= 38
                        nm = "grs" if isr else "gv"
                        co = ((bi - 38) if isr else (bi - 29)) * 256
                        sb_, bsb = stb.next()
                        fw.op("act", lambda e, sb_=sb_, pb=pb, isr=isr: e.activation(out=sb_, in_=ps[pb][:, 0:256], func=AF.Silu if isr else AF.Copy),
                              reads=[b_ps[pb]], writes=[bsb])
                        fw.op("act", lambda e, sb_=sb_, rows=rows, nm=nm, co=co: e.dma_start(out=ev[nm][rows, co:co + 256], in_=sb_), reads=[bsb], writes=[eb[nm]],
                              dma=True, semkey=bsb, join=True)
                if bi == 12:
                    nl0 = 1 if tg == 0 else 0
                    pc = ps[5][:, 0:128].rearrange("p (a b) -> p a b", a=8)
                    for kv in range(2):
                        for g in range(4):
                            for li in range(32):
                                fw.op("pe", lambda e, kv=kv, g=g, li=li, nl0=nl0: e.matmul(pc[:, kv * 4 + g, nl0:16], lhsT=W1b[:, kv, li, :],
                                                                               rhs=kcbuf[:, kv, g, li + 16 * nl0:li + 241:16],
                                                                               start=(li == 0), stop=(li == 31)), reads=[b_W1, b_kcb], writes=[b_ps[5]])
                    pc4 = ps[5][:, 0:128].rearrange("p (k a b) -> p k a b", k=2, a=4)
                    nn0 = 16 * tg - 1 + nl0
                    for kv in range(2):
                        xx, x2, x3 = gel[:, 0, :, nl0:16], gel[:, 1, :, nl0:16], gel[:, 2, :, nl0:16]
                        fw.op("act", lambda e, kv=kv, xx=xx, nl0=nl0: e.activation(out=xx, in_=pc4[:, kv, :, nl0:16], func=AF.Identity, bias=posbias[:, kv:kv + 1], scale=1.0),
                              reads=[b_ps[5], b_pb], writes=[b_gel])
                        fw.op("dve", lambda e, xx=xx, x2=x2: e.tensor_tensor(out=x2, in0=xx, in1=xx, op=ALU.mult), reads=[b_gel], writes=[b_gel])
                        fw.op("dve", lambda e, x2=x2: e.tensor_scalar(out=x2, in0=x2, scalar1=0.044715, scalar2=1.0, op0=ALU.mult, op1=ALU.add), reads=[b_gel], writes=[b_gel])
                        fw.op("dve", lambda e, xx=xx, x2=x2, x3=x3: e.tensor_tensor(out=x3, in0=x2, in1=xx, op=ALU.mult), reads=[b_gel], writes=[b_gel])
                        fw.op("act", lambda e, x3=x3: e.activation(out=x3, in_=x3, func=AF.Sigmoid, scale=1.5957691216057308), reads=[b_gel], writes=[b_gel])
                        fw.op("dve", lambda e, kv=kv, xx=xx, x3=x3, nn0=nn0, nl0=nl0: e.tensor_tensor(out=hid_b[:, kv, :, nn0:nn0 + 16 - nl0], in0=xx, in1=x3, op=ALU.mult),
                              reads=[b_gel], writes=[b_hid], join=True)
                    fw.op("pool", lambda e: e.tensor_copy(out=kcbuf[:, :, :, 0:16], in_=kcbuf[:, :, :, 256:272]), reads=[b_kcb], writes=[b_kcb])
        kb.pop()

        if cfg.get("ev_stop") == 1:
            kb.pop()
            return
        ovl_in = kb.inp("c_ovl", [128, 33])
        ovf = kb.alloc([128, 33], F32)
        b_ov = Buf()
        fw.op("sp", lambda e: e.dma_start(out=ovf, in_=ovl_in), writes=[b_ov], dma=True)
        for g in range(4):
            fw.op("act", lambda e, g=g: e.activation(out=VC[:, g, 128:161], in_=ovf, func=AF.Copy), reads=[b_ov], writes=[b_VC], join=True)
            fw.op("pe", lambda e, g=g: e.matmul(ps[0][:, 0:128], lhsT=hid_b[:, 1, g, :], rhs=w2b[:, 1, :], start=True, stop=True), reads=[b_hid, b_w2], writes=[b_ps[0]])
            fw.op("act", lambda e, g=g: e.activation(out=VC[:, g, 0:128], in_=ps[0][:, 0:128], func=AF.Copy), reads=[b_ps[0]], writes=[b_VC], join=True)
            fw.op("pe", lambda e, g=g: e.matmul(ps[1][:, 0:128], lhsT=hid_b[:, 0, g, :], rhs=w2b[:, 0, :], start=True, stop=True), reads=[b_hid, b_w2], writes=[b_ps[1]])
            ob, bob = headnorm(ps[1][:, 0:128].rearrange("p (a b) -> p a b", a=1), b_ps[1], 1, 1, cscmp[:, 0:16], cscmp[:, 16:32], b_ec)
            transpose_to(ob, bob, 1, [kcmpT[:, g, :]], b_kc)

        sbtap("VC", VC, b_VC)
        sbtap("kcmpT", kcmpT, b_kc)
        sbtap("hid", hid_b, b_hid)
        if cfg.get("ev_stop") == 15:
            kb.pop()
            return
        kb.push()
        E_f = kb.alloc([32, NT * 128], F32)
        E_b = kb.alloc([32, NT, 128], BF16)
        E_in = kb.inp("c_E", [32, NT * 128])
        b_E = Buf()
        fw.op("sp", lambda e: e.dma_start(out=E_f, in_=E_in), writes=[b_E], dma=True)
        fw.op("act", lambda e: e.activation(out=E_b.rearrange("p a b -> p (a b)"), in_=E_f, func=AF.Copy), reads=[b_E], writes=[b_E])
        q_rot = Rot([kb.alloc([128, 16, 128], BF16) for _ in range(2)])
        gs_rot = Rot([kb.alloc([128, 16, 3], F32) for _ in range(2)])
        acc_rot = Rot([kb.alloc([128, 16, 128], F32) for _ in range(2)])
        accb_rot = Rot([kb.alloc([128, 16, 128], BF16) for _ in range(2)])
        on_rot = Rot([kb.alloc([128, 16, 128], BF16) for _ in range(2)])
        p_rot = Rot([kb.alloc([128, 4, 128], BF16) for _ in range(4)])
        sm_rot = Rot([kb.alloc([128, 160], F32) for _ in range(3)])
        ns_rot = Rot([kb.alloc([32, 4, 128], BF16) for _ in range(2)])
        BIG = 1.0e30
        def p2_tile(T):
            qt, bq = q_rot.next()
            gs, bgs = gs_rot.next()
            acc, bacc = acc_rot.next()
            fw.op("sp", lambda e, qt=qt, T=T: e.dma_start(out=qt.rearrange("p a b -> p (a b)"), in_=ev["qT"][T]), reads=[eb["qT"]], writes=[bq], dma=True)
            fw.op("sp", lambda e, gs=gs, T=T: e.dma_start(out=gs.rearrange("p a b -> p (a b)"), in_=ev["gsig"][T * 128:(T + 1) * 128, :]), reads=[eb["gsig"]], writes=[bgs], dma=True)
            nsT_static = [None]
            def p2_group(g):
                q4 = qt[:, 4 * g:4 * g + 4, :].rearrange("p a b -> p (a b)")
                sm, bsm = sm_rot.next()
                rc, wc, imp, impm, m8, m8b, tmp32 = sm[:, 0:4], sm[:, 4:8], sm[:, 8:40], sm[:, 40:72], sm[:, 72:80], sm[:, 80:88], sm[:, 88:120]
                r2, w2 = sm[:, 120:124], sm[:, 124:128]
                nsel = sm[:, 128:160]
                fw.op("pe", lambda e, g=g, q4=q4: e.matmul(ps[0][:], lhsT=kcmpT[:, g, :], rhs=q4, start=True, stop=True), reads=[b_kc, bq], writes=[b_ps[0]])
                if cfg.get("p2_cut", 99) <= 0:
                    return
                pt_, bp = p_rot.next()
                fw.op("act", lambda e, pt_=pt_: e.activation(out=pt_.rearrange("p a b -> p (a b)"), in_=ps[0][:], func=AF.Exp), reads=[b_ps[0]], writes=[bp])
                if cfg.get("p2_cut", 99) <= 1:
                    return
                fw.op("pool", lambda e, pt_=pt_, T=T: e.affine_select(out=pt_, in_=pt_, pattern=[[0, 4], [1, 128]], compare_op=ALU.is_ge, fill=0.0,
                                                                   base=128 * T - 31, channel_multiplier=-16), reads=[bp], writes=[bp])
                if cfg.get("p2_cut", 99) <= 2:
                    return
                U = [ps[2][:].rearrange("p (a b) -> p a b", a=2), ps[3][:].rearrange("p (a b) -> p a b", a=2)]
                for h in range(4):
                    fw.op("pe", lambda e, pt_=pt_, h=h, g=g: e.matmul(U[h // 2][:, h % 2, 0:161], lhsT=pt_[:, h, :], rhs=VC[:, g, 0:161], start=True, stop=True),
                          reads=[bp, b_VC], writes=[b_ps[2 + h // 2]])
                if cfg.get("p2_cut", 99) <= 3:
                    return
                for hh in range(2):
                    fw.op("dve", lambda e, hh=hh: e.tensor_scalar(out=rc[:, 2 * hh:2 * hh + 2], in0=U[hh][:, :, 160], scalar1=1e-30, scalar2=None, op0=ALU.max),
                          reads=[b_ps[2 + hh]], writes=[bsm], join=(hh > 0))
                fw.op("dve", lambda e: e.reciprocal(out=rc, in_=rc), reads=[bsm], writes=[bsm])
                for h in range(4):
                    if h == 0:
                        fw.op("dve", lambda e: e.tensor_scalar(out=imp, in0=U[0][:, 0, 128:160], scalar1=rc[:, 0:1], scalar2=None, op0=ALU.mult),
                              reads=[bsm, b_ps[2]], writes=[bsm])
                    else:
                        fw.op("dve", lambda e, h=h: e.scalar_tensor_tensor(out=imp, in0=U[h // 2][:, h % 2, 128:160], scalar=rc[:, h:h + 1], in1=imp, op0=ALU.mult, op1=ALU.add),
                              reads=[bsm, b_ps[2 + h // 2]], writes=[bsm])
                fw.op("dve", lambda e, g=g: e.tensor_tensor(out=wc, in0=rc, in1=gs[:, 4 * g:4 * g + 4, 0], op=ALU.mult), reads=[bsm, bgs], writes=[bsm])
                if cfg.get("p2_cut", 99) <= 4:
                    return
                for h in range(4):
                    fw.op("act", lambda e, h=h, g=g: e.activation(out=acc[:, 4 * g + h, :], in_=U[h // 2][:, h % 2, 0:128], func=AF.Copy, scale=wc[:, h:h + 1]),
                          reads=[bsm, b_ps[2 + h // 2]], writes=[bacc], join=True)
                if cfg.get("p2_cut", 99) <= 5:
                    return
                if T < 8 and nsT_static[0] is not None:
                    nsT, bns = nsT_static[0]
                else:
                    if T < 8:
                        fw.op("pool", lambda e: e.memset(nsel, -1.0), writes=[bsm])
                        for hf in range(2):
                            cur = 2 * T + hf
                            fw.op("pool", lambda e, hf=hf, cur=cur: e.memset(nsel[64 * hf:64 * hf + 64, 0:cur + 1], 0.0), writes=[bsm])
                    else:
                        fw.op("dve", lambda e: e.tensor_copy(out=impm, in_=imp), reads=[bsm], writes=[bsm])
                        for hf in range(2):
                            cur = 2 * T + hf
                            rs_ = slice(64 * hf, 64 * hf + 64)
                            if cur + 1 < 32:
                                fw.op("dve", lambda e, rs_=rs_, cur=cur: e.memset(impm[rs_, cur + 1:32], -BIG), reads=[bsm], writes=[bsm])
                            fw.op("dve", lambda e, rs_=rs_, cur=cur: e.memset(impm[rs_, cur - 1:cur + 1], BIG), reads=[bsm], writes=[bsm])
                        fw.op("dve", lambda e: e.memset(impm[:, 0:1], BIG), reads=[bsm], writes=[bsm])
                        fw.op("dve", lambda e: e.max(out=m8, in_=impm), reads=[bsm], writes=[bsm])
                        fw.op("dve", lambda e: e.match_replace(out=tmp32, in_to_replace=m8, in_values=impm, imm_value=-BIG), reads=[bsm], writes=[bsm])
                        fw.op("dve", lambda e: e.max(out=m8b, in_=tmp32), reads=[bsm], writes=[bsm])
                        fw.op("dve", lambda e: e.tensor_scalar(out=nsel, in0=impm, scalar1=m8b[:, 7:8], scalar2=1.0, op0=ALU.is_ge, op1=ALU.subtract), reads=[bsm], writes=[bsm])
                    if cfg.get("p2_cut", 99) <= 5.3:
                        return
                    fw.op("pe", lambda e: e.transpose(out=ps[1][0:32, 0:128], in_=nsel, identity=identf), reads=[bsm, b_const], writes=[b_ps[1]])
                    if cfg.get("p2_cut", 99) <= 5.6:
                        return
                    nsT, bns = ns_rot.next()
                    for h in range(4):
                        eng = "dve" if h % 2 == 0 else "act"
                        if eng == "dve":
                            fw.op("dve", lambda e, h=h, nsT=nsT: e.tensor_copy(out=nsT[:, h, :], in_=ps[1][0:32, 0:128]), reads=[b_ps[1]], writes=[bns], join=(h > 0))
                        else:
                            fw.op("act", lambda e, h=h, nsT=nsT: e.activation(out=nsT[:, h, :], in_=ps[1][0:32, 0:128], func=AF.Copy), reads=[b_ps[1]], writes=[bns], join=True)
                    if T < 8:
                        nsT_static[0] = (nsT, bns)
                if cfg.get("p2_cut", 99) <= 6:
                    return
                O = [ps[6][:].rearrange("p (a b) -> p a b", a=2), ps[7][:].rearrange("p (a b) -> p a b", a=2)]
                def p2_br(br):
                    kts = list(range(0, T + 1)) if br == 1 else list(range(max(0, T - 4), T + 1))
                    kT_, vA, bk_, bv_ = (ksT, vsA, b_ks, b_vs) if br == 1 else (kwT, vwA, b_kw, b_vw)
                    def p2_kt(ki, kt):
                        sb = 4 + (ki % 2)
                        fw.op("pe", lambda e, kT_=kT_, g=g, kt=kt, q4=q4, sb=sb, br=br: e.matmul(ps[sb][:], lhsT=kT_[:, g, kt * 128:(kt + 1) * 128], rhs=q4,
                                                                                             start=True, stop=(br == 2)), reads=[bk_, bq], writes=[b_ps[sb]])
                        if br == 1:
                            fw.op("pe", lambda e, kt=kt, nsT=nsT, sb=sb: e.matmul(ps[sb][:], lhsT=E_b[:, kt, :], rhs=nsT.rearrange("p a b -> p (a b)"),
                                                                                 start=False, stop=True), reads=[b_E, bns], writes=[b_ps[sb]])
                        pt_, bp = p_rot.next()
                        fw.op("act", lambda e, pt_=pt_, sb=sb: e.activation(out=pt_.rearrange("p a b -> p (a b)"), in_=ps[sb][:], func=AF.Exp), reads=[b_ps[sb]], writes=[bp])
                        if kt == T:
                            fw.op("pool", lambda e, pt_=pt_: e.affine_select(out=pt_, in_=pt_, pattern=[[0, 4], [1, 128]], compare_op=ALU.is_ge, fill=0.0,
                                                                            base=0, channel_multiplier=-1), reads=[bp], writes=[bp])
                        if br == 2 and kt == T - 4:
                            fw.op("pool", lambda e, pt_=pt_: e.affine_select(out=pt_, in_=pt_, pattern=[[0, 4], [-1, 128]], compare_op=ALU.is_gt, fill=0.0,
                                                                            base=0, channel_multiplier=1), reads=[bp], writes=[bp])
                        for h in range(4):
                            fw.op("pe", lambda e, pt_=pt_, h=h, vA=vA, kt=kt, g=g, ki=ki, nk=len(kts): e.matmul(
                                O[h // 2][:, h % 2, 0:129], lhsT=pt_[:, h, :], rhs=vA[:, kt, g, 0:129], start=(ki == 0 and h % 2 == 0), stop=(ki == nk - 1), skip_group_check=True),
                                reads=[bp, bv_], writes=[b_ps[6 + h // 2]])
                    for ki, kt in enumerate(kts):
                        p2_kt(ki, kt)
                    for hh in range(2):
                        fw.op("dve", lambda e, hh=hh: e.reciprocal(out=r2[:, 2 * hh:2 * hh + 2], in_=O[hh][:, :, 128]), reads=[b_ps[6 + hh]], writes=[bsm], join=True)
                    fw.op("dve", lambda e, g=g, br=br: e.tensor_tensor(out=w2, in0=r2, in1=gs[:, 4 * g:4 * g + 4, br], op=ALU.mult), reads=[bsm, bgs], writes=[bsm])
                    for h in range(4):
                        fw.op("dve", lambda e, h=h, g=g: e.scalar_tensor_tensor(out=acc[:, 4 * g + h, :], in0=O[h // 2][:, h % 2, 0:128], scalar=w2[:, h:h + 1],
                                                                             in1=acc[:, 4 * g + h, :], op0=ALU.mult, op1=ALU.add),
                              reads=[bsm, b_ps[6 + h // 2], bacc], writes=[bacc])
                for br in cfg.get("p2_brs", (1, 2)):
                    p2_br(br)
            for g in range(4):
                p2_group(g)
            if cfg.get("p2_cut", 99) <= 7:
                return
            accb, baccb = accb_rot.next()
            fw.op("act", lambda e, accb=accb, acc=acc: e.activation(out=accb.rearrange("p a b -> p (a b)"), in_=acc.rearrange("p a b -> p (a b)"), func=AF.Copy),
                  reads=[bacc], writes=[baccb])
            on, bon = on_rot.next()
            for hb in range(2):
                pv = bfview(1, 8)
                for h in range(8):
                    fw.op("pe", lambda e, hb=hb, h=h, accb=accb, pv=pv: e.transpose(out=pv[:, h, :], in_=accb[:, hb * 8 + h, :], identity=identb),
                          reads=[baccb, b_const], writes=[b_ps[1]])
                fw.op("dve", lambda e, hb=hb, on=on, pv=pv: e.tensor_copy(out=on[:, hb * 8:hb * 8 + 8, :], in_=pv), reads=[b_ps[1]], writes=[bon], join=(hb > 0))
            fw.op("act", lambda e, on=on, T=T: e.dma_start(out=ev["onT"][T], in_=on.rearrange("p a b -> p (a b)")), reads=[bon], writes=[eb["onT"]],
                  dma=True, semkey=bon, join=True)
        for T in range(cfg.get("p2_tiles", NT)):
            p2_tile(T)
        kb.pop()
        kb.pop()
        if cfg.get("ev_stop") == 2:
            return

        kb.push()
        tri_in = kb.inp("c_tri", [128, 3 * 128])
        cind_in = kb.inp("c_cind", [128, 2])
        gng_in = kb.inp("gla_ngT", [2, 128, 512])
        wa2_in = kb.inp("gla_w_a2", [2, 16, 1024])
        ba_in = kb.inp("gla_b_a", [2, 1024])
        tri = kb.alloc([128, 3, 128], F32)
        cind = kb.alloc([128, 2], F32)
        gng = kb.alloc([128, 512], F32)
        wa2 = kb.alloc([16, 1024], F32)
        ba = kb.alloc([1, 1024], F32)
        ones1 = kb.alloc([1, 128], F32)
        b_gc = Buf()
        fw.op("sp", lambda e: [e.dma_start(out=tri.rearrange("p a b -> p (a b)"), in_=tri_in), e.dma_start(out=cind, in_=cind_in),
                               e.dma_start(out=gng, in_=gng_in[j]), e.dma_start(out=wa2, in_=wa2_in[j]),
                               e.dma_start(out=ba, in_=ba_in[j:j + 1, :])], writes=[b_gc], dma=True, ndma=5)
        fw.op("pool", lambda e: e.memset(ones1, 1.0), writes=[b_gc], join=True)
        st_f = kb.alloc([128, 8, 512], F32)
        st_b = kb.alloc([128, 8, 512], BF16)
        b_stf = [Buf() for _ in range(8)]
        b_stb = [Buf() for _ in range(8)]
        fw.op("pool", lambda e: e.memset(st_f, 0.0), writes=b_stf)
        fw.op("pool", lambda e: e.memset(st_b, 0.0), writes=b_stb)
        gq_rot = Rot([kb.alloc([128, 1024], F32) for _ in range(2)])
        gk_rot = Rot([kb.alloc([128, 1024], F32) for _ in range(2)])
        gv_rot = Rot([kb.alloc([128, 2048], BF16) for _ in range(2)])
        gr_rot = Rot([kb.alloc([128, 2048], BF16) for _ in range(2)])
        al_rot = Rot([kb.alloc([16, 128], F32) for _ in range(2)])
        sp_t = kb.alloc([128, 1024], F32)
        ex_rot = Rot([kb.alloc([128, 1024], F32) for _ in range(2)])
        qin = kb.alloc([128, 1024], BF16)
        kin = kb.alloc([128, 1024], BF16)
        kout = kb.alloc([128, 1024], BF16)
        qT_ = kb.alloc([128, 8, 128], BF16)
        kT_g = kb.alloc([128, 8, 128], BF16)
        qA = kb.alloc([128, 8, 128], BF16)
        qB = kb.alloc([128, 8, 128], BF16)
        dec = kb.alloc([128, 8, 2], F32)
        aTb_rot = Rot([kb.alloc([128, 128], BF16) for _ in range(2)])
        on_f = Rot([kb.alloc([128, 512], F32) for _ in range(2)])
        ob_rot = Rot([kb.alloc([128, 4, 128], BF16) for _ in range(2)])
        ssg = Rot([kb.alloc([128, 2], F32) for _ in range(2)])
        omT = kb.alloc([128, KC, 256], BF16)
        b_om = Buf("omT")
        b_sp, b_qin, b_kin, b_kout, b_qT, b_kT, b_qA, b_qB, b_dec = [Buf() for _ in range(9)]
        fw.op("pool", lambda e: e.memset(qA, 0.0), writes=[b_qA])
        fw.op("pool", lambda e: e.memset(qB, 0.0), writes=[b_qB])
        xr_rot = Rot([kb.alloc([128, 256], F32) for _ in range(4)])
        def p3_tile(T):
            rows = slice(T * 128, (T + 1) * 128)
            tc0 = (T % 2) * 128
            gq, bgq = gq_rot.next()
            gk, bgk = gk_rot.next()
            gvt, bgv = gv_rot.next()
            grt, bgr = gr_rot.next()
            al, bal = al_rot.next()
            fw.op("sp", lambda e, gq=gq, rows=rows: e.dma_start(out=gq, in_=ev["gq"][rows, :]), reads=[eb["gq"]], writes=[bgq], dma=True)
            fw.op("sp", lambda e, gk=gk, rows=rows: e.dma_start(out=gk, in_=ev["gk"][rows, :]), reads=[eb["gk"]], writes=[bgk], dma=True)
            fw.op("sp", lambda e, gvt=gvt, rows=rows: e.dma_start(out=gvt, in_=ev["gv"][rows, :]), reads=[eb["gv"]], writes=[bgv], dma=True)
            fw.op("sp", lambda e, grt=grt, rows=rows: e.dma_start(out=grt, in_=ev["grs"][rows, :]), reads=[eb["grs"]], writes=[bgr], dma=True)
            fw.op("sp", lambda e, al=al, rows=rows: e.dma_start(out=al, in_=ev["alow"][:, rows]), reads=[eb["alow"]], writes=[bal], dma=True)
            fw.op("sp", lambda e, T=T, tc0=tc0: e.dma_start(out=omT[:, 0:16, tc0:tc0 + 128], in_=ev["onT"][T].rearrange("p (a b) -> p a b", a=16)),
                  reads=[eb["onT"]], writes=[b_om], dma=True, join=True)
            for hf in range(2):
                fw.op("pe", lambda e, hf=hf, al=al: e.matmul(ps[hf][:], lhsT=al, rhs=wa2[:, hf * 512:(hf + 1) * 512], start=True, stop=False),
                      reads=[bal, b_gc], writes=[b_ps[hf]])
                fw.op("pe", lambda e, hf=hf: e.matmul(ps[hf][:], lhsT=ones1, rhs=ba[:, hf * 512:(hf + 1) * 512], start=False, stop=True),
                      reads=[b_gc], writes=[b_ps[hf]])
                fw.op("act", lambda e, hf=hf: e.activation(out=sp_t[:, hf * 512:(hf + 1) * 512], in_=ps[hf][:], func=AF.Exp, scale=-1.0), reads=[b_ps[hf]], writes=[b_sp], join=(hf > 0))
            fw.op("act", lambda e: e.activation(out=sp_t, in_=sp_t, func=AF.Ln, bias=1.0, scale=1.0), reads=[b_sp], writes=[b_sp])
            for hf in range(2):
                fw.op("pe", lambda e, hf=hf: e.matmul(ps[2 + hf][:], lhsT=tri[:, 0, :], rhs=sp_t[:, hf * 512:(hf + 1) * 512], start=True, stop=True),
                      reads=[b_sp, b_gc], writes=[b_ps[2 + hf]])
                fw.op("pe", lambda e, hf=hf: e.matmul(ps[4 + hf][:], lhsT=tri[:, 1, :], rhs=sp_t[:, hf * 512:(hf + 1) * 512], start=True, stop=True),
                      reads=[b_sp, b_gc], writes=[b_ps[4 + hf]])
            for ds in range(8):
                fw.op("pe", lambda e, ds=ds: e.matmul(ps[6][:, 2 * ds:2 * ds + 2], lhsT=sp_t[:, ds * 128:(ds + 1) * 128], rhs=cind, start=True, stop=True),
                      reads=[b_sp, b_gc], writes=[b_ps[6]])
            fw.op("act", lambda e: e.activation(out=dec.rearrange("p a b -> p (a b)"), in_=ps[6][:, 0:16], func=AF.Exp, scale=-1.0 / 16), reads=[b_ps[6]], writes=[b_dec])
            ex, bex = ex_rot.next()
            for hf in range(2):
                fw.op("act", lambda e, hf=hf, ex=ex: e.activation(out=ex[:, hf * 512:(hf + 1) * 512], in_=ps[2 + hf][:], func=AF.Exp), reads=[b_ps[2 + hf]], writes=[bex], join=(hf > 0))
            fw.op("dve", lambda e, ex=ex, gq=gq: e.scalar_tensor_tensor(out=qin, in0=gq, scalar=0.0625, in1=ex, op0=ALU.mult, op1=ALU.mult), reads=[bex, bgq], writes=[b_qin])
            ex, bex = ex_rot.next()
            for hf in range(2):
                fw.op("act", lambda e, hf=hf, ex=ex: e.activation(out=ex[:, hf * 512:(hf + 1) * 512], in_=ps[2 + hf][:], func=AF.Exp, scale=-1.0), reads=[b_ps[2 + hf]], writes=[bex], join=(hf > 0))
            fw.op("dve", lambda e, ex=ex, gk=gk: e.tensor_tensor(out=kin, in0=gk, in1=ex, op=ALU.mult), reads=[bex, bgk], writes=[b_kin])
            ex, bex = ex_rot.next()
            for hf in range(2):
                fw.op("act", lambda e, hf=hf, ex=ex: e.activation(out=ex[:, hf * 512:(hf + 1) * 512], in_=ps[4 + hf][:], func=AF.Exp), reads=[b_ps[4 + hf]], writes=[bex], join=(hf > 0))
            fw.op("dve", lambda e, ex=ex, gk=gk: e.tensor_tensor(out=kout, in0=gk, in1=ex, op=ALU.mult), reads=[bex, bgk], writes=[b_kout])
            pv = bfview(7, 8)
            for (src, bsrc, dst, bdst) in ((qin, b_qin, qT_, b_qT), (kin, b_kin, kT_g, b_kT)):
                for s8 in range(8):
                    fw.op("pe", lambda e, src=src, s8=s8: e.transpose(out=pv[:, s8, :], in_=src[:, s8 * 128:(s8 + 1) * 128], identity=identb),
                          reads=[bsrc, b_const], writes=[b_ps[7]])
                fw.op("dve", lambda e, dst=dst: e.tensor_copy(out=dst, in_=pv), reads=[b_ps[7]], writes=[bdst])
            fw.op("act", lambda e: e.activation(out=qA[:, :, 0:64], in_=qT_[:, :, 0:64], func=AF.Copy), reads=[b_qT], writes=[b_qA])
            fw.op("act", lambda e: e.activation(out=qB[:, :, 64:128], in_=qT_[:, :, 64:128], func=AF.Copy), reads=[b_qT], writes=[b_qB])
            def p3_head(hd):
                for ch in range(2):
                    s8 = hd * 2 + ch
                    fw.op("pe", lambda e, s8=s8, ch=ch: e.matmul(ps[0][:, 0:128], lhsT=kT_g[:, s8, :], rhs=qT_[:, s8, :], start=(ch == 0), stop=(ch == 1)),
                          reads=[b_kT, b_qT], writes=[b_ps[0]])
                aTb, baT = aTb_rot.next()
                fw.op("dve", lambda e, aTb=aTb: e.tensor_tensor(out=aTb, in0=ps[0][:, 0:128], in1=tri[:, 2, :], op=ALU.mult), reads=[b_ps[0], b_gc], writes=[baT])
                vh = gvt[:, hd * 512:(hd + 1) * 512]
                fw.op("pe", lambda e, aTb=aTb, vh=vh: e.matmul(ps[1][:], lhsT=aTb, rhs=vh, start=True, stop=False), reads=[baT, bgv], writes=[b_ps[1]])
                for half, qX, bqX in ((0, qA, b_qA), (1, qB, b_qB)):
                    for ch in range(2):
                        s8 = hd * 2 + ch
                        fw.op("pe", lambda e, qX=qX, s8=s8, half=half, ch=ch: e.matmul(ps[1][:], lhsT=qX[:, s8, :], rhs=st_b[:, s8, :], start=False,
                                                                                    stop=(half == 1 and ch == 1)), reads=[bqX, b_stb[s8]], writes=[b_ps[1]])
                    r0 = 64 * half
                    for ch in range(2):
                        s8 = hd * 2 + ch
                        pb = 2 + ch + 2 * half
                        fw.op("pe", lambda e, s8=s8, r0=r0, pb=pb, vh=vh: e.matmul(ps[pb][:], lhsT=kout[r0:r0 + 64, s8 * 128:(s8 + 1) * 128], rhs=vh[r0:r0 + 64, :],
                                                                                start=True, stop=True), reads=[b_kout, bgv], writes=[b_ps[pb]])
                        fw.op("dve", lambda e, s8=s8, pb=pb, half=half: e.scalar_tensor_tensor(out=st_f[:, s8, :], in0=st_f[:, s8, :], scalar=dec[:, s8, half:half + 1],
                                                                                             in1=ps[pb][:], op0=ALU.mult, op1=ALU.add),
                              reads=[b_stf[s8], b_dec, b_ps[pb]], writes=[b_stf[s8]])
                        fw.op("act", lambda e, s8=s8: e.activation(out=st_b[:, s8, :], in_=st_f[:, s8, :], func=AF.Copy), reads=[b_stf[s8]], writes=[b_stb[s8]])
                onf, bonf = on_f.next()
                sg_, bsg_ = ssg.next()
                fw.op("act", lambda e, onf=onf, sg_=sg_: e.activation(out=onf, in_=ps[1][:], func=AF.Square, accum_out=sg_[:, 0:1]), reads=[b_ps[1]], writes=[bonf, bsg_])
                fw.op("act", lambda e, sg_=sg_: e.activation(out=sg_[:, 0:1], in_=sg_[:, 0:1], func=AF.Sqrt, bias=eps6[:, 0:1], scale=1.0 / 512), reads=[bsg_, b_const], writes=[bsg_])
                fw.op("dve", lambda e, sg_=sg_: e.reciprocal(out=sg_[:, 0:1], in_=sg_[:, 0:1]), reads=[bsg_], writes=[bsg_])
                fw.op("dve", lambda e, onf=onf, sg_=sg_: e.scalar_tensor_tensor(out=onf, in0=ps[1][:], scalar=sg_[:, 0:1], in1=gng, op0=ALU.mult, op1=ALU.mult),
                      reads=[b_ps[1], bsg_, b_gc, bonf], writes=[bonf])
                ob, bob = ob_rot.next()
                fw.op("dve", lambda e, onf=onf, ob=ob, hd=hd, grt=grt: e.tensor_tensor(out=ob.rearrange("p a b -> p (a b)"), in0=onf, in1=grt[:, hd * 512:(hd + 1) * 512], op=ALU.mult),
                      reads=[bonf, bgr], writes=[bob])
                pv2 = bfview(6, 8)
                for k4 in range(4):
                    fw.op("pe", lambda e, ob=ob, k4=k4: e.transpose(out=pv2[:, k4, :], in_=ob[:, k4, :], identity=identb), reads=[bob, b_const], writes=[b_ps[6]])
                fw.op("act", lambda e, hd=hd, tc0=tc0: e.activation(out=omT[:, 16 + hd * 4:16 + hd * 4 + 4, tc0:tc0 + 128], in_=pv2[:, 0:4, :], func=AF.Copy),
                      reads=[b_ps[6]], writes=[b_om], join=True)
            for hd in range(4):
                p3_head(hd)
            if T % 2 == 1:
                t0 = (T - 1) * 128
                for dp in range(D // 256):
                    wt, bw, *_ = load_block(wout, dp)
                    for jj in range(2):
                        c2 = dp * 2 + jj
                        pb = 4 + jj
                        for k in range(KC):
                            fw.op("pe", lambda e, wt=wt, jj=jj, k=k, pb=pb: e.matmul(ps[pb][:, 0:256], lhsT=wt[:, k, jj * 128:(jj + 1) * 128], rhs=omT[:, k, :],
                                                                                 start=(k == 0), stop=(k == KC - 1)), reads=[bw, b_om], writes=[b_ps[pb]])
                        resid_update(l, 0, c2, t0, 256, ps[pb][:, 0:256], b_ps[pb], xr_rot)
        for T in range(NT):
            p3_tile(T)
        kb.pop()

    for (kind, l) in layers:
        if kind == "ffn":
            ffn_layer(l)
        elif kind == "odd":
            odd_layer(l)
        elif kind == "even":
            even_layer(l)

    for k in cfg.get("taps", ()):
        src = ev[k]
        tout = nc.dram_tensor("tap_" + k, list(src.shape), src.dtype, kind="ExternalOutput").ap()
        kb.tapnames.append("tap_" + k)
        fw.op("sp", lambda e, tout=tout, src=src: e.dma_start(out=tout, in_=src), reads=[ev["b"][k]], writes=[b_tap], dma=True, join=True)
    if kb.tapnames:
        fw.op("sp", None, reads=[b_tap])
    kb.push()
    orow = Rot([kb.alloc([128, D], F32) for _ in range(2)])
    xld = Rot([kb.alloc([128, 4, 128], F32) for _ in range(4)])
    b_out = Buf("out")
    for t in range(NT):
        orw, bo = orow.next()
        for cg in range(8):
            xl, bl = xld.next()
            fw.op("sp", lambda e, xl=xl, cg=cg, t=t: e.dma_start(out=xl, in_=xT[cg * 4:(cg + 1) * 4, :, t * 128:(t + 1) * 128].rearrange("c p n -> p c n")),
                  reads=[b_xT[t]], writes=[bl], dma=True)
            pb = 4 + (cg % 2)
            pst = ps[pb][:].rearrange("p (a b) -> p a b", a=4)
            for k in range(4):
                fw.op("pe", lambda e, xl=xl, k=k, pst=pst: e.transpose(out=pst[:, k, :], in_=xl[:, k, :], identity=identf),
                      reads=[bl, b_const], writes=[b_ps[pb]])
            if cg % 2 == 0:
                fw.op("dve", lambda e, orw=orw, cg=cg, pb=pb: e.tensor_copy(out=orw[:, cg * 512:(cg + 1) * 512], in_=ps[pb][:]), reads=[b_ps[pb]], writes=[bo], join=(cg > 0))
            else:
                fw.op("act", lambda e, orw=orw, cg=cg, pb=pb: e.activation(out=orw[:, cg * 512:(cg + 1) * 512], in_=ps[pb][:], func=AF.Copy), reads=[b_ps[pb]], writes=[bo], join=True)
        fw.op("act", lambda e, orw=orw, t=t: e.dma_start(out=out_ap[t * 128:(t + 1) * 128, :], in_=orw), reads=[bo], writes=[b_out], dma=True, semkey=bo, join=True)
    fw.op("sp", None, reads=[b_out])
    kb.pop()
    cnt = fw.emit()
    kb.st.close()
    return nc, kb, cnt


EV_SEGS = ([(i * 256, 256) for i in range(8)] + [(2048 + i * 256, 256) for i in range(12)] + [(5120, 48)]
           + [(5168 + i * 256, 256) for i in range(4)] + [(6192 + i * 256, 256) for i in range(4)]
           + [(7216 + i * 256, 256) for i in range(8)] + [(9264, 16)] + [(9280 + i * 256, 256) for i in range(8)])


def vecT(v):
    v = np.asarray(v, np.float32)
    lead = v.shape[:-1]
    n = v.shape[-1] // 128
    a = v.reshape(lead + (n, 128))
    a = np.moveaxis(a, -1, 0)
    return np.ascontiguousarray(a.reshape(128, -1))


def host_consts(inputs, b):
    m = {}
    m["c_ident"] = np.eye(128, dtype=np.float32)
    m["cT"] = vecT(inputs["c"][b])
    m["b_modT"] = vecT(inputs["b_mod"].reshape(6, D))
    m["adaT"] = vecT(inputs["ada_table"])
    m["gmixT"] = vecT(inputs["norm_mix_g"])
    m["gffnT"] = vecT(inputs["norm_ffn_g"])
    if "cv_b_pw1" in inputs:
        cv = np.stack([np.stack([inputs["cv_b_pw1"][j][:D], inputs["cv_b_pw1"][j][D:], inputs["cv_b_dw"][j], inputs["cv_ln_g"][j],
                                 inputs["cv_ln_b"][j], inputs["cv_b_pw2"][j]]) for j in range(2)])
        m["cv_vecT"] = np.stack([vecT(cv[j]) for j in range(2)])
        wd = np.asarray(inputs["cv_w_dw"], np.float32).reshape(2, 31, KC, 128)
        m["cv_wdwT"] = np.ascontiguousarray(wd.transpose(0, 3, 2, 1).reshape(2, 128, KC * 31))
    if "w_in" in inputs:
        half = 16
        inv = (500000.0 ** (-np.arange(half, dtype=np.float32) / half)).astype(np.float32)
        pos = np.arange(S, dtype=np.float32)
        ang = pos[:, None] * inv[None, :]
        cs = np.concatenate([np.cos(ang), np.sin(ang)], -1).astype(np.float32)
        m["c_cs"] = np.ascontiguousarray(cs.reshape(NT, 128, 32).transpose(1, 0, 2).reshape(128, NT * 32))
        pc = (np.arange(128, dtype=np.float32) * 16 + 31)
        angc = pc[:, None] * inv[None, :]
        m["c_cscmp"] = np.concatenate([np.cos(angc), np.sin(angc)], -1).astype(np.float32)
        g4 = np.stack([np.stack([inputs["q_norm_g"][j], inputs["k_norm_g"][j][0], inputs["k_norm_g"][j][1], inputs["k_norm_g"][j][2]]) for j in range(2)])
        m["qk_gT"] = np.ascontiguousarray(np.broadcast_to(g4.reshape(2, 1, 4 * 128), (2, 128, 4 * 128))).astype(np.float32)
        m["cmp_posT"] = np.ascontiguousarray(np.asarray(inputs["cmp_pos"], np.float32).transpose(0, 3, 1, 2).reshape(2, 128, 64))
        n = np.arange(128)
        jb = np.arange(32)
        ov = np.clip(np.minimum(16 * n[:, None] + 32, 64 * jb[None, :] + 64) - np.maximum(16 * n[:, None], 64 * jb[None, :]), 0, None) / 16.0
        ov[127] = 0
        m["c_ovl"] = np.concatenate([ov, np.ones((128, 1))], -1).astype(np.float32)
        E = np.zeros((32, NT, 128), np.float32)
        for kt in range(NT):
            for k in range(128):
                E[2 * kt + k // 64, kt, k] = 30000.0
        m["c_E"] = E.reshape(32, NT * 128)
        jj, ii = np.meshgrid(np.arange(128), np.arange(128), indexing="ij")
        same = (jj // 64) == (ii // 64)
        tri = np.stack([np.where(same & (jj <= ii), -1.0 / 16, 0.0), np.where(same & (jj > ii), -1.0 / 16, 0.0), np.where(same & (jj <= ii), 1.0, 0.0)], 1)
        m["c_tri"] = np.ascontiguousarray(tri.reshape(128, 3 * 128)).astype(np.float32)
        ci = np.zeros((128, 2), np.float32)
        ci[:64, 0] = 1
        ci[64:, 1] = 1
        m["c_cind"] = ci
        m["gla_ngT"] = np.ascontiguousarray(np.broadcast_to(np.asarray(inputs["gla_norm_g"], np.float32)[:, None, :], (2, 128, 512)))
    return m


_CACHE = {}


def run(inputs, cfg, cores):
    key = repr(cfg)
    if key not in _CACHE:
        _CACHE[key] = build(cfg)
    nc, kb, cnt = _CACHE[key]
    in_maps = []
    for b in cores:
        hc = host_consts(inputs, b)
        m = {}
        for name in kb.din:
            if name == "x":
                m[name] = np.ascontiguousarray(inputs["x"][b])
            elif name in hc:
                m[name] = hc[name]
            else:
                m[name] = np.ascontiguousarray(inputs[name])
        in_maps.append(m)
    res = run_bass_kernel_spmd(nc, in_maps, core_ids=list(range(len(cores))))
    if getattr(kb, "tapnames", None):
        run.taps = {k: np.asarray(res.results[0][k]) for k in kb.tapnames}
    return np.stack([r["out"] for r in res.results], axis=0)


FULL_CFG = {"layers": [("even", 0), ("ffn", 0), ("odd", 1), ("ffn", 1), ("even", 2), ("ffn", 2), ("odd", 3), ("ffn", 3)]}


def kernel(**inputs):
    inputs = {k: np.asarray(v) for k, v in inputs.items()}
    return run(inputs, FULL_CFG, list(range(8))).astype(np.float32)
```

```python
import contextlib
import numpy as np
import concourse.bass as bass
import concourse.mybir as mybir
from concourse.bass_utils import run_bass_kernel_spmd

F32 = mybir.dt.float32
BF16 = mybir.dt.bfloat16
ALU = mybir.AluOpType
AF = mybir.ActivationFunctionType
AX = mybir.AxisListType

S = 2048
D = 4096
KC = 32
FH = 11008
FC = 86
DEPTH = 4
EVEN_IN = 11328
NT = S // 128


class Buf:
    __slots__ = ("name", "writers", "readers", "pw", "pr", "excl", "opener")

    def __init__(self, name="", excl=False):
        self.name = name
        self.excl = excl
        self.opener = None
        self.writers = []
        self.readers = []
        self.pw = []
        self.pr = []


class Ins:
    __slots__ = ("eng", "fn", "deps", "signal", "sigidx", "dma", "sem", "semval", "idx", "ndma")


class FW:
    ENGS = ("pe", "act", "dve", "pool", "sp")

    def __init__(self, nc):
        self.nc = nc
        self.ins = []
        self.last = {e: None for e in self.ENGS}
        self.last_dma = {}

    def op(self, eng, fn, reads=(), writes=(), dma=False, ndma=1, semkey=None, join=False, extra=None):
        i = Ins()
        i.eng, i.fn, i.dma, i.ndma = eng, fn, dma, ndma
        i.signal, i.sigidx, i.sem, i.semval = False, None, None, None
        i.idx = len(self.ins)
        deps = {}
        xr = [b for b in reads if b.excl]
        reads = [b for b in reads if not b.excl]
        for b in reads:
            for w in b.writers:
                deps[w] = "raw"
        for b in xr:
            for w in b.writers:
                deps[w] = "raw"
            for r in b.readers:
                deps.setdefault(r, "war")
        for b in writes:
            if join and not b.excl:
                for w in b.pw:
                    deps.setdefault(w, "waw")
                for r in b.pr:
                    deps.setdefault(r, "war")
                if b.opener is not None:
                    deps.setdefault(b.opener, "raw")
            else:
                for w in b.writers:
                    deps.setdefault(w, "waw")
            for r in b.readers:
                deps.setdefault(r, "war")
        if extra:
            for j in extra:
                deps[j] = "raw"
        for b in reads:
            b.readers.append(i.idx)
        for b in xr:
            b.readers.append(i.idx)
        for b in writes:
            if join and not b.excl:
                b.writers.append(i.idx)
            else:
                b.pw, b.pr = b.writers, b.readers
                b.writers = [i.idx]
                b.readers = []
                b.opener = i.idx
        deps.pop(i.idx, None)
        i.deps = deps
        if dma:
            i.sem = semkey if semkey is not None else writes[0]
            self.last_dma[id(i.sem)] = i.idx
        else:
            if fn is not None:
                self.last[eng] = i.idx
        self.ins.append(i)
        return i

    def barrier(self):
        ex = [v for v in self.last.values() if v is not None] + list(self.last_dma.values())
        for e in self.ENGS:
            self.op(e, None, extra=ex)
        self.last_dma = {}

    def emit(self):
        nc = self.nc
        ins = self.ins
        for i in ins:
            real = {}
            best = {}
            for j, kind in i.deps.items():
                pj = ins[j]
                if not pj.dma and not i.dma and pj.eng == i.eng and i.fn is not None:
                    if i.eng == "pe" or (kind != "raw" and i.eng != "pool"):
                        continue
                if pj.dma:
                    real[j] = kind
                else:
                    if pj.eng not in best or j > best[pj.eng]:
                        best[pj.eng] = j
            for e_, j in best.items():
                real[j] = "dep"
                ins[j].signal = True
            i.deps = real
        cnt = {e: 0 for e in self.ENGS}
        for i in ins:
            if not i.dma and i.signal:
                cnt[i.eng] += 1
                i.sigidx = cnt[i.eng]
        klast = {}
        for i in ins:
            if i.dma:
                klast[id(i.sem)] = max(klast.get(id(i.sem), 0), i.idx)
        for i in ins:
            for j in i.deps:
                if ins[j].dma:
                    k = id(ins[j].sem)
                    klast[k] = max(klast[k], i.idx)
        bar_ends = [i.idx for i in ins if i.fn is None and i.eng == "sp" and len(i.deps) > 1]
        import bisect
        phys = []
        keys = {}
        for i in ins:
            if not i.dma:
                continue
            k = id(i.sem)
            if k not in keys:
                pos = bisect.bisect_left(bar_ends, i.idx)
                lastbar = bar_ends[pos - 1] if pos > 0 else -1
                found = None
                sw = (i.eng == "pool")
                if not sw:
                    for n, p in enumerate(phys):
                        if p[0] <= lastbar and not p[2]:
                            found = n
                            break
                if found is None:
                    phys.append([klast[k], 0, sw])
                    found = len(phys) - 1
                else:
                    phys[found][0] = klast[k]
                keys[k] = found
            n = keys[k]
            phys[n][1] += 16 * i.ndma
            i.semval = phys[n][1]
            i.sem = n
        self.n_dma_sems = len(phys)
        with contextlib.ExitStack() as st:
            esem = {e: st.enter_context(nc.semaphore("s_" + e)) for e in self.ENGS}
            dsem = [st.enter_context(nc.semaphore("d_%d" % n)) for n in range(len(phys))]
            for i in ins:
                if i.dma:
                    i.sem = dsem[i.sem]
            block = st.enter_context(nc.Block())
            per = {e: [i for i in ins if i.eng == e] for e in self.ENGS}

            def run(e, engobj):
                waited = {}
                for i in per[e]:
                    for j in sorted(i.deps):
                        pj = ins[j]
                        s, v = (pj.sem, pj.semval) if pj.dma else (esem[pj.eng], pj.sigidx)
                        if waited.get(id(s), 0) >= v:
                            continue
                        waited[id(s)] = v
                        engobj.wait_ge(s, v)
                    if i.fn is None:
                        continue
                    r = i.fn(engobj)
                    if i.dma:
                        rs = r if isinstance(r, (list, tuple)) else [r]
                        assert len(rs) == i.ndma, (len(rs), i.ndma)
                        for x in rs:
                            x.then_inc(i.sem, 16)
                    elif i.signal:
                        last = r[-1] if isinstance(r, (list, tuple)) else r
                        last.then_inc(esem[e], 1)

            @block.tensor
            def _(eng):
                run("pe", eng)

            @block.scalar
            def _(eng):
                run("act", eng)

            @block.vector
            def _(eng):
                run("dve", eng)

            @block.gpsimd
            def _(eng):
                run("pool", eng)

            @block.sync
            def _(eng):
                run("sp", eng)
        return cnt


class Rot:
    def __init__(self, tiles):
        self.tiles = tiles
        self.bufs = [Buf() for _ in tiles]
        self.n = 0

    def next(self):
        k = self.n % len(self.tiles)
        self.n += 1
        return self.tiles[k], self.bufs[k]


class KB:
    def __init__(self, cfg, taps=()):
        self.cfg = cfg
        self.nc = nc = bass.Bass("TRN2", target_bir_lowering=False)
        self.fw = FW(nc)
        self.st = contextlib.ExitStack()
        self.din = {}
        self.nscr = 0

    def inp(self, name, shape, dt=F32):
        if name not in self.din:
            self.din[name] = self.nc.dram_tensor(name, list(shape), dt, kind="ExternalInput").ap()
        return self.din[name]

    def scratch(self, shape, dt):
        self.nscr += 1
        return self.nc.dram_tensor("scr%d" % self.nscr, list(shape), dt, kind="Internal").ap()

    def arena_init(self, words):
        self.arena = self.st.enter_context(self.nc.sbuf_tensor("arena", [128, words], F32))
        self.awords = words
        self.aoff = 0
        self.amark = []

    def alloc(self, shape, dt=F32):
        n = int(np.prod(shape[1:]))
        w = n if dt == F32 else (n + 1) // 2
        assert self.aoff + w <= self.awords, ("arena overflow", self.aoff, w, self.awords)
        ap = self.arena[:, self.aoff:self.aoff + w]
        self.aoff += w
        if dt != F32:
            ap = ap.bitcast(dt)
            if n % 2:
                ap = ap[:, 0:n]
        if len(shape) == 3:
            ap = ap.rearrange("p (a b) -> p a b", a=shape[1])
        elif len(shape) == 4:
            ap = ap.rearrange("p (a b c) -> p a b c", a=shape[1], b=shape[2])
        if shape[0] < 128:
            ap = ap[0:shape[0]]
        return ap

    def push(self):
        self.amark.append(self.aoff)

    def pop(self):
        self.fw.barrier()
        self.aoff = self.amark.pop()


def wblocks_plain(K, N, kmax=32, nw=256):
    out = []
    kc = K // 128
    nks = (kc + kmax - 1) // kmax
    ksz = [(kc + nks - 1 - i) // nks for i in range(nks)]
    for n0 in range(0, N, nw):
        w = min(nw, N - n0)
        k0 = 0
        for kk in ksz:
            out.append((k0, kk, n0, w))
            k0 += kk
    return out


class WMat:
    def __init__(self, kb, name, w_ap, K, N, kmax=32, nw=256, segs=None):
        self.blocks = []
        self.buf = Buf(name)
        fw = kb.fw
        wv = w_ap.rearrange("(c p) n -> p c n", p=128)
        bl = wblocks_plain(K, N, kmax, nw) if segs is None else [(0, K // 128, n0, w) for (n0, w) in segs]
        km = max(b[1] for b in bl)
        scr_all = kb.scratch([len(bl), 128, km * nw], BF16)
        for bi, (k0, kk, n0, w) in enumerate(bl):
            scr = scr_all[bi, :, 0:kk * w]
            self.blocks.append((scr, k0, kk, n0, w))
            fw.op("pool", lambda e, scr=scr, k0=k0, kk=kk, n0=n0, w=w: e.dma_start(
                out=scr.rearrange("p (c n) -> p c n", c=kk), in_=wv[:, k0:k0 + kk, n0:n0 + w]),
                writes=[self.buf], dma=True, semkey=self.buf, join=True)


def build(cfg):
    kb = KB(cfg)
    nc, fw = kb.nc, kb.fw
    st = kb.st
    x_in = kb.inp("x", [S, D])
    out_ap = nc.dram_tensor("out", [S, D], F32, kind="ExternalOutput").ap()
    xT = kb.scratch([KC, 128, S], F32)
    b_xT = [Buf("xT%d" % t) for t in range(NT)]

    kb.arena_init(51000)
    ps = [st.enter_context(nc.psum_tensor("ps%d" % i, [128, 512], F32)) for i in range(8)]
    b_ps = [Buf("ps%d" % i, excl=True) for i in range(8)]

    identf = kb.alloc([128, 128], F32)
    identb = kb.alloc([128, 128], BF16)
    onesD = kb.alloc([128, 128], BF16)
    b_const = Buf("const")
    c_ident = kb.inp("c_ident", [128, 128])
    fw.op("sp", lambda e: e.dma_start(out=identf, in_=c_ident), writes=[b_const], dma=True)
    fw.op("act", lambda e: e.activation(out=identb, in_=identf, func=AF.Copy), reads=[b_const], writes=[b_const])
    fw.op("pool", lambda e: e.memset(onesD, 1.0 / D), writes=[b_const], join=True)

    nslot = 3
    wslots = Rot([kb.alloc([128, 32 * 256], BF16) for _ in range(nslot)])

    def load_block(wm, bi):
        scr, k0, kk, n0, w = wm.blocks[bi]
        t, b = wslots.next()
        fw.op("sp", lambda e: e.dma_start(out=t[:, 0:kk * w], in_=scr), reads=[wm.buf], writes=[b], dma=True)
        return t[:, 0:kk * w].rearrange("p (c n) -> p c n", c=kk), b, k0, kk, n0, w

    modv = kb.alloc([128, DEPTH, 6, KC], F32)
    b_mod = Buf("mod")
    if cfg.get("mod", True):
        kb.push()
        cT_in = kb.inp("cT", [128, KC])
        bmod_in = kb.inp("b_modT", [128, 6 * KC])
        ada_in = kb.inp("adaT", [128, DEPTH * 6 * KC])
        gmix_in = kb.inp("gmixT", [128, DEPTH * KC])
        gffn_in = kb.inp("gffnT", [128, DEPTH * KC])
        wmod_in = kb.inp("w_mod", [D, 6 * D])
        cT = kb.alloc([128, KC], F32)
        sc = kb.alloc([128, KC], F32)
        bm = kb.alloc([128, 6, KC], F32)
        ada = kb.alloc([128, DEPTH, 6, KC], F32)
        gmx = kb.alloc([128, DEPTH, KC], F32)
        gff = kb.alloc([128, DEPTH, KC], F32)
        b_v = Buf()
        fw.op("sp", lambda e: [e.dma_start(out=cT, in_=cT_in),
                               e.dma_start(out=bm.rearrange("p a b -> p (a b)"), in_=bmod_in),
                               e.dma_start(out=ada.rearrange("p a b c -> p (a b c)"), in_=ada_in),
                               e.dma_start(out=gmx.rearrange("p a b -> p (a b)"), in_=gmix_in),
                               e.dma_start(out=gff.rearrange("p a b -> p (a b)"), in_=gffn_in)],
              writes=[b_v], dma=True, ndma=5)
        b_sc = Buf()
        fw.op("act", lambda e: e.activation(out=sc, in_=cT, func=AF.Silu), reads=[b_v], writes=[b_sc])
        wpan = Rot([kb.alloc([128, KC, 256], F32) for _ in range(2)])
        wmv = wmod_in.rearrange("(c p) n -> p c n", p=128)
        psm = ps[0][:, 0:192]
        for pn in range(96):
            t, b = wpan.next()
            fw.op("sp", lambda e, t=t, pn=pn: e.dma_start(out=t, in_=wmv[:, :, pn * 256:(pn + 1) * 256]), writes=[b], dma=True)
            for j in range(2):
                ntile = pn * 2 + j
                for c in range(KC):
                    fw.op("pe", lambda e, t=t, j=j, c=c, ntile=ntile: e.matmul(
                        psm[:, ntile:ntile + 1], lhsT=t[:, c, j * 128:(j + 1) * 128], rhs=sc[:, c:c + 1],
                        start=(c == 0), stop=(c == KC - 1)), reads=[b, b_sc], writes=[b_ps[0]])
        mm = kb.alloc([128, 6, KC], F32)
        b_mm = Buf()
        fw.op("dve", lambda e: e.tensor_tensor(out=mm.rearrange("p a b -> p (a b)"), in0=psm, in1=bm.rearrange("p a b -> p (a b)"), op=ALU.add),
              reads=[b_ps[0], b_v], writes=[b_mm])
        for l in range(DEPTH):
            fw.op("dve", lambda e, l=l: e.tensor_tensor(out=modv[:, l], in0=mm, in1=ada[:, l], op=ALU.add),
                  reads=[b_mm, b_v], writes=[b_mod], join=(l > 0))
        for l in range(DEPTH):
            for (si, gsrc) in ((1, gmx), (4, gff)):
                fw.op("dve", lambda e, l=l, si=si, gsrc=gsrc: e.scalar_tensor_tensor(
                    out=modv[:, l, si], in0=modv[:, l, si], scalar=1.0, in1=gsrc[:, l], op0=ALU.add, op1=ALU.mult),
                    reads=[b_mod, b_v], writes=[b_mod])
        kb.pop()

    layers = cfg["layers"]
    WM = {}
    precast_done = set()

    def precast(idx):
        if idx >= len(layers) or idx in precast_done:
            return
        precast_done.add(idx)
        kind, l = layers[idx]
        j = l // 2
        if kind == "ffn":
            wg = kb.inp("ffn_w_gate", [DEPTH, D, FH])
            wu = kb.inp("ffn_w_up", [DEPTH, D, FH])
            wd = kb.inp("ffn_w_down", [DEPTH, FH, D])
            WM[("g", l)] = WMat(kb, "wg%d" % l, wg[l], D, FH)
            WM[("u", l)] = WMat(kb, "wu%d" % l, wu[l], D, FH)
            WM[("d", l)] = WMat(kb, "wd%d" % l, wd[l], FH, D)
        elif kind == "even":
            wi = kb.inp("w_in", [2, D, EVEN_IN])
            wo = kb.inp("w_out", [2, D, D])
            WM[("win", l)] = WMat(kb, "win%d" % l, wi[j], D, EVEN_IN, segs=EV_SEGS)
            WM[("wout", l)] = WMat(kb, "wout%d" % l, wo[j], D, D)
        elif kind == "odd":
            w1 = kb.inp("cv_w_pw1", [2, D, 2 * D])
            w2 = kb.inp("cv_w_pw2", [2, D, D])
            WM[("pw1", l)] = WMat(kb, "pw1_%d" % l, w1[j], D, 2 * D, nw=128)
            WM[("pw2", l)] = WMat(kb, "pw2_%d" % l, w2[j], D, D)

    precast(0)
    precast(1)

    if cfg.get("xin", True):
        kb.push()
        xrow = Rot([kb.alloc([128, D], F32) for _ in range(2)])
        xst = Rot([kb.alloc([128, 4, 128], F32) for _ in range(4)])
        for t in range(NT):
            xr, bx = xrow.next()
            fw.op("sp", lambda e, xr=xr, t=t: e.dma_start(out=xr, in_=x_in[t * 128:(t + 1) * 128, :]), writes=[bx], dma=True)
            for cg in range(8):
                pb = 4 + (cg % 2)
                pst = ps[pb][:].rearrange("p (a b) -> p a b", a=4)
                for k in range(4):
                    c = cg * 4 + k
                    fw.op("pe", lambda e, xr=xr, c=c, k=k, pst=pst: e.transpose(out=pst[:, k, :], in_=xr[:, c * 128:(c + 1) * 128], identity=identf),
                          reads=[bx, b_const], writes=[b_ps[pb]])
                xs, bs = xst.next()
                eng = "dve" if cg % 2 == 0 else "act"
                if eng == "dve":
                    fw.op("dve", lambda e, xs=xs, pst=pst: e.tensor_copy(out=xs, in_=pst), reads=[b_ps[pb]], writes=[bs])
                else:
                    fw.op("act", lambda e, xs=xs, pst=pst: e.activation(out=xs, in_=pst, func=AF.Copy), reads=[b_ps[pb]], writes=[bs])
                fw.op("act", lambda e, xs=xs, cg=cg, t=t: e.dma_start(
                    out=xT[cg * 4:(cg + 1) * 4, :, t * 128:(t + 1) * 128].rearrange("c p n -> p c n"), in_=xs),
                    reads=[bs], writes=[b_xT[t]], dma=True, semkey=bs, join=True)
        kb.pop()

    def norm_stage(l, which, t0, n, hT, b_hT, xc_rot, sq_rot):
        sh = modv[:, l, 3 * which + 0]
        gsc = modv[:, l, 3 * which + 1]
        tl = list(range(t0 // 128, (t0 + n) // 128))
        rb = [b_xT[t] for t in tl]
        pss = ps[6][:, 0:n]
        for c in range(KC):
            xc, bxc = xc_rot.next()
            fw.op("sp", lambda e, xc=xc, c=c: e.dma_start(out=xc[:, 0:n], in_=xT[c, :, t0:t0 + n]), reads=rb, writes=[bxc], dma=True)
            sq, bsq = sq_rot.next()
            fw.op("act", lambda e, xc=xc, sq=sq: e.activation(out=sq[:, 0:n], in_=xc[:, 0:n], func=AF.Square), reads=[bxc], writes=[bsq])
            fw.op("pe", lambda e, sq=sq, c=c: e.matmul(pss, lhsT=onesD, rhs=sq[:, 0:n], start=(c == 0), stop=(c == KC - 1)),
                  reads=[bsq, b_const], writes=[b_ps[6]])
        rstd = kb_rstd[:, 0:n]
        fw.op("act", lambda e: e.activation(out=rstd, in_=pss, func=AF.Sqrt, bias=eps6[:, 0:1], scale=1.0), reads=[b_ps[6], b_const], writes=[b_rstd])
        fw.op("dve", lambda e: e.reciprocal(out=rstd, in_=rstd), reads=[b_rstd], writes=[b_rstd])
        for c in range(KC):
            xc, bxc = xc_rot.next()
            fw.op("sp", lambda e, xc=xc, c=c: e.dma_start(out=xc[:, 0:n], in_=xT[c, :, t0:t0 + n]), reads=rb, writes=[bxc], dma=True)
            fw.op("dve", lambda e, xc=xc: e.tensor_tensor(out=xc[:, 0:n], in0=xc[:, 0:n], in1=rstd, op=ALU.mult), reads=[bxc, b_rstd], writes=[bxc])
            fw.op("act", lambda e, xc=xc, c=c: e.activation(out=hT[:, c, 0:n], in_=xc[:, 0:n], func=AF.Identity,
                                                            bias=sh[:, c:c + 1], scale=gsc[:, c:c + 1]),
                  reads=[bxc, b_mod], writes=[b_hT], join=(c > 0))

    kb_rstd = kb.alloc([128, 512], F32)
    b_rstd = Buf("rstd")
    eps6 = kb.alloc([128, 2], F32)
    fw.op("pool", lambda e: e.memset(eps6[:, 0:1], 1e-6), writes=[b_const], join=True)
    fw.op("pool", lambda e: e.memset(eps6[:, 1:2], 1e-5), writes=[b_const], join=True)

    def resid_update(l, which, c2, t0, n, psum_ap, b_psum, xr_rot):
        gate = modv[:, l, 3 * which + 2]
        tl = list(range(t0 // 128, (t0 + n) // 128))
        rb = [b_xT[t] for t in tl]
        xr, bxr = xr_rot.next()
        fw.op("sp", lambda e: e.dma_start(out=xr[:, 0:n], in_=xT[c2, :, t0:t0 + n]), reads=rb, writes=[bxr], dma=True)
        fw.op("dve", lambda e: e.scalar_tensor_tensor(out=xr[:, 0:n], in0=psum_ap, scalar=gate[:, c2:c2 + 1], in1=xr[:, 0:n],
                                                      op0=ALU.mult, op1=ALU.add), reads=[bxr, b_psum, b_mod], writes=[bxr])
        fw.op("act", lambda e: e.dma_start(out=xT[c2, :, t0:t0 + n], in_=xr[:, 0:n]), reads=[bxr], writes=rb, dma=True, semkey=bxr, join=True)

    def ffn_layer(l):
        kb.push()
        TG = 512
        hT = kb.alloc([128, KC, TG], BF16)
        b_hT = Buf("hT")
        actT = kb.alloc([128, FC, TG], BF16)
        b_act = [Buf() for _ in range(FC)]
        xc_rot = Rot([kb.alloc([128, TG], F32) for _ in range(4)])
        sq_rot = Rot([kb.alloc([128, TG], BF16) for _ in range(2)])
        sg_rot = Rot([kb.alloc([128, TG], F32) for _ in range(4)])
        wg, wu, wd = WM[("g", l)], WM[("u", l)], WM[("d", l)]
        for tg in range(S // TG):
            t0 = tg * TG
            norm_stage(l, 1, t0, TG, hT, b_hT, xc_rot, sq_rot)
            for fp in range(FC // 2):
                wt, bw, k0, kk, n0, w = load_block(wg, fp)
                sgs = []
                for j in range(2):
                    for c in range(KC):
                        fw.op("pe", lambda e, wt=wt, j=j, c=c: e.matmul(ps[j][:], lhsT=wt[:, c, j * 128:(j + 1) * 128], rhs=hT[:, c, :],
                                                                       start=(c == 0), stop=(c == KC - 1)), reads=[bw, b_hT], writes=[b_ps[j]])
                    sg, bsg = sg_rot.next()
                    fw.op("act", lambda e, sg=sg, j=j: e.activation(out=sg, in_=ps[j][:], func=AF.Silu), reads=[b_ps[j]], writes=[bsg])
                    sgs.append((sg, bsg))
                wt, bw, k0, kk, n0, w = load_block(wu, fp)
                for j in range(2):
                    for c in range(KC):
                        fw.op("pe", lambda e, wt=wt, j=j, c=c: e.matmul(ps[2 + j][:], lhsT=wt[:, c, j * 128:(j + 1) * 128], rhs=hT[:, c, :],
                                                                       start=(c == 0), stop=(c == KC - 1)), reads=[bw, b_hT], writes=[b_ps[2 + j]])
                    sg, bsg = sgs[j]
                    f = fp * 2 + j
                    fw.op("dve", lambda e, sg=sg, j=j, f=f: e.tensor_tensor(out=actT[:, f, :], in0=sg, in1=ps[2 + j][:], op=ALU.mult),
                          reads=[bsg, b_ps[2 + j]], writes=[b_act[f]])
            nks = len(wd.blocks) // (D // 256)
            for dp in range(D // 256):
                for ks in range(nks):
                    wt, bw, k0, kk, n0, w = load_block(wd, dp * nks + ks)
                    for j in range(2):
                        pbd = 4 + 2 * (dp % 2) + j
                        for c in range(kk):
                            f = k0 + c
                            fw.op("pe", lambda e, wt=wt, j=j, c=c, f=f, pbd=pbd: e.matmul(ps[pbd][:], lhsT=wt[:, c, j * 128:(j + 1) * 128], rhs=actT[:, f, :],
                                                                                         start=(f == 0), stop=(f == FC - 1)),
                                  reads=[bw, b_act[f]], writes=[b_ps[pbd]])
                for j in range(2):
                    pbd = 4 + 2 * (dp % 2) + j
                    resid_update(l, 1, dp * 2 + j, t0, TG, ps[pbd][:], b_ps[pbd], xc_rot)
        kb.pop()

    def odd_layer(l):
        j = l // 2
        kb.push()
        TG = 512
        W = 31
        hT = kb.alloc([128, KC, TG], BF16)
        b_hT = Buf("hT")
        vT = kb.alloc([128, KC, TG], F32)
        b_v = [Buf() for _ in range(KC)]
        halo = kb.alloc([128, KC, 32], F32)
        b_halo = [Buf() for _ in range(KC)]
        xc_rot = Rot([kb.alloc([128, TG], F32) for _ in range(4)])
        sq_rot = Rot([kb.alloc([128, TG], BF16) for _ in range(2)])
        u_rot = Rot([kb.alloc([128, 32 + TG], F32) for _ in range(2)])
        a_rot = Rot([kb.alloc([128, TG], F32) for _ in range(2)])
        vb_rot = Rot([kb.alloc([128, 2, TG], BF16) for _ in range(2)])
        v2_rot = Rot([kb.alloc([128, TG], F32) for _ in range(2)])
        tm_rot = Rot([kb.alloc([128, TG], F32) for _ in range(4)])
        cvv = kb.alloc([128, 6, KC], F32)
        wdw = kb.alloc([128, KC, W], F32)
        b_cv = Buf()
        cv_in = kb.inp("cv_vecT", [2, 128, 6 * KC])
        wdw_in = kb.inp("cv_wdwT", [2, 128, KC * W])
        fw.op("sp", lambda e: [e.dma_start(out=cvv.rearrange("p a b -> p (a b)"), in_=cv_in[j]),
                               e.dma_start(out=wdw.rearrange("p a b -> p (a b)"), in_=wdw_in[j])], writes=[b_cv], dma=True, ndma=2)
        fw.op("pool", lambda e: e.memset(halo, 0.0), writes=b_halo)
        w1, w2 = WM[("pw1", l)], WM[("pw2", l)]
        mu = kb.alloc([128, TG], F32)
        rs = kb.alloc([128, TG], F32)
        b_st = Buf()
        for tg in range(S // TG):
            t0 = tg * TG
            norm_stage(l, 0, t0, TG, hT, b_hT, xc_rot, sq_rot)
            for c in range(KC):
                wa, bwa, *_ = load_block(w1, c)
                for k in range(KC):
                    fw.op("pe", lambda e, wa=wa, k=k: e.matmul(ps[0][:], lhsT=wa[:, k, :], rhs=hT[:, k, :], start=(k == 0), stop=(k == KC - 1)),
                          reads=[bwa, b_hT], writes=[b_ps[0]])
                wb_, bwb, *_ = load_block(w1, KC + c)
                for k in range(KC):
                    fw.op("pe", lambda e, wb_=wb_, k=k: e.matmul(ps[1][:], lhsT=wb_[:, k, :], rhs=hT[:, k, :], start=(k == 0), stop=(k == KC - 1)),
                          reads=[bwb, b_hT], writes=[b_ps[1]])
                u, bu = u_rot.next()
                a, ba = a_rot.next()
                fw.op("act", lambda e, a=a, c=c: e.activation(out=a, in_=ps[1][:], func=AF.Sigmoid, bias=cvv[:, 1, c:c + 1], scale=1.0),
                      reads=[b_ps[1], b_cv], writes=[ba])
                fw.op("pool", lambda e, u=u, c=c: e.tensor_copy(out=u[:, 0:32], in_=halo[:, c, :]), reads=[b_halo[c]], writes=[bu])
                fw.op("dve", lambda e, u=u, a=a, c=c: e.scalar_tensor_tensor(out=u[:, 32:32 + TG], in0=ps[0][:], scalar=cvv[:, 0, c:c + 1], in1=a,
                                                                          op0=ALU.add, op1=ALU.mult), reads=[b_ps[0], ba, b_cv], writes=[bu], join=True)
                fw.op("pool", lambda e, u=u, c=c: e.tensor_copy(out=halo[:, c, :], in_=u[:, TG:TG + 32]), reads=[bu], writes=[b_halo[c]])
                NDV = 19
                v2, bv2 = v2_rot.next()
                for wi in range(W):
                    src = u[:, 2 + wi:2 + wi + TG]
                    if wi == 0:
                        fw.op("dve", lambda e, src=src, c=c: e.tensor_scalar(out=vT[:, c, :], in0=src, scalar1=wdw[:, c, 0:1], scalar2=cvv[:, 2, c:c + 1],
                                                                             op0=ALU.mult, op1=ALU.add), reads=[bu, b_cv], writes=[b_v[c]])
                    elif wi < NDV:
                        fw.op("dve", lambda e, src=src, c=c, wi=wi: e.scalar_tensor_tensor(out=vT[:, c, :], in0=src, scalar=wdw[:, c, wi:wi + 1], in1=vT[:, c, :],
                                                                                           op0=ALU.mult, op1=ALU.add), reads=[bu, b_cv, b_v[c]], writes=[b_v[c]])
                    elif wi == NDV:
                        fw.op("act", lambda e, src=src, c=c, wi=wi, v2=v2: e.activation(out=v2, in_=src, func=AF.Copy, scale=wdw[:, c, wi:wi + 1]),
                              reads=[bu, b_cv], writes=[bv2])
                    else:
                        tm, btm = tm_rot.next()
                        fw.op("act", lambda e, src=src, c=c, wi=wi, tm=tm: e.activation(out=tm, in_=src, func=AF.Copy, scale=wdw[:, c, wi:wi + 1]),
                              reads=[bu, b_cv], writes=[btm])
                        fw.op("pool", lambda e, tm=tm, v2=v2: e.tensor_tensor(out=v2, in0=v2, in1=tm, op=ALU.add), reads=[btm, bv2], writes=[bv2])
                fw.op("dve", lambda e, c=c, v2=v2: e.tensor_tensor(out=vT[:, c, :], in0=vT[:, c, :], in1=v2, op=ALU.add), reads=[b_v[c], bv2], writes=[b_v[c]])
                vb, bvb = vb_rot.next()
                fw.op("act", lambda e, vb=vb, c=c: e.activation(out=vb[:, 0, :], in_=vT[:, c, :], func=AF.Copy), reads=[b_v[c]], writes=[bvb])
                fw.op("act", lambda e, vb=vb, c=c: e.activation(out=vb[:, 1, :], in_=vT[:, c, :], func=AF.Square), reads=[b_v[c]], writes=[bvb], join=True)
                fw.op("pe", lambda e, vb=vb, c=c: e.matmul(ps[2][:], lhsT=onesD, rhs=vb[:, 0, :], start=(c == 0), stop=(c == KC - 1)),
                      reads=[bvb, b_const], writes=[b_ps[2]])
                fw.op("pe", lambda e, vb=vb, c=c: e.matmul(ps[3][:], lhsT=onesD, rhs=vb[:, 1, :], start=(c == 0), stop=(c == KC - 1)),
                      reads=[bvb, b_const], writes=[b_ps[3]])
            fw.op("act", lambda e: e.activation(out=mu, in_=ps[2][:], func=AF.Copy), reads=[b_ps[2]], writes=[b_st])
            fw.op("dve", lambda e: e.tensor_tensor(out=rs, in0=mu, in1=mu, op=ALU.mult), reads=[b_st], writes=[b_st])
            fw.op("dve", lambda e: e.tensor_tensor(out=rs, in0=ps[3][:], in1=rs, op=ALU.subtract), reads=[b_st, b_ps[3]], writes=[b_st])
            fw.op("act", lambda e: e.activation(out=rs, in_=rs, func=AF.Sqrt, bias=eps6[:, 1:2], scale=1.0), reads=[b_st, b_const], writes=[b_st])
            fw.op("dve", lambda e: e.reciprocal(out=rs, in_=rs), reads=[b_st], writes=[b_st])
            for c in range(KC):
                fw.op("dve", lambda e, c=c: e.tensor_tensor(out=vT[:, c, :], in0=vT[:, c, :], in1=mu, op=ALU.subtract), reads=[b_v[c], b_st], writes=[b_v[c]])
                fw.op("pool", lambda e, c=c: e.tensor_tensor(out=vT[:, c, :], in0=vT[:, c, :], in1=rs, op=ALU.mult), reads=[b_v[c], b_st], writes=[b_v[c]])
                fw.op("act", lambda e, c=c: e.activation(out=hT[:, c, :], in_=vT[:, c, :], func=AF.Silu, bias=cvv[:, 4, c:c + 1], scale=cvv[:, 3, c:c + 1]),
                      reads=[b_v[c], b_cv], writes=[b_hT], join=(c > 0))
            for dp in range(D // 256):
                wt, bw, *_ = load_block(w2, dp)
                for jj in range(2):
                    c2 = dp * 2 + jj
                    for k in range(KC):
                        fw.op("pe", lambda e, wt=wt, jj=jj, k=k: e.matmul(ps[4 + jj][:], lhsT=wt[:, k, jj * 128:(jj + 1) * 128], rhs=hT[:, k, :],
                                                                         start=(k == 0), stop=(k == KC - 1)), reads=[bw, b_hT], writes=[b_ps[4 + jj]])
                    y, by = a_rot.next()
                    fw.op("act", lambda e, y=y, jj=jj, c2=c2: e.activation(out=y, in_=ps[4 + jj][:], func=AF.Identity, bias=cvv[:, 5, c2:c2 + 1], scale=1.0),
                          reads=[b_ps[4 + jj], b_cv], writes=[by])
                    resid_update(l, 0, c2, t0, TG, y, by, xc_rot)
        kb.pop()

    ev = {}
    kb.tapnames = []
    b_tap = Buf("tap")

    def sbtap(name, ap, buf):
        if name not in cfg.get("sbtaps", ()):
            return
        tout = nc.dram_tensor("tap_" + name, list(ap.shape), ap.dtype, kind="ExternalOutput").ap()
        kb.tapnames.append("tap_" + name)
        fw.op("sp", lambda e: e.dma_start(out=tout, in_=ap), reads=[buf], writes=[b_tap], dma=True, join=True)

    def even_scratch():
        if ev:
            return
        ev["qT"] = kb.scratch([NT, 128, 16 * 128], BF16)
        ev["onT"] = kb.scratch([NT, 128, 16 * 128], BF16)
        ev["gsig"] = kb.scratch([S, 48], F32)
        ev["gq"] = kb.scratch([S, 1024], F32)
        ev["gk"] = kb.scratch([S, 1024], F32)
        ev["gv"] = kb.scratch([S, 2048], BF16)
        ev["grs"] = kb.scratch([S, 2048], BF16)
        ev["alow"] = kb.scratch([16, S], F32)
        ev["b"] = {k: Buf("ev_" + k) for k in ("qT", "onT", "gsig", "gq", "gk", "gv", "grs", "alow")}
        kb.evnames = {k: ev[k].tensor.name for k in ("qT", "onT", "gsig", "gq", "gk", "gv", "grs", "alow")}

    def bfview(bank, a):
        return ps[bank][:].bitcast(BF16).rearrange("p (a b) -> p a b", a=a)

    def even_layer(l):
        j = l // 2
        even_scratch()
        eb = ev["b"]
        win, wout = WM[("win", l)], WM[("wout", l)]
        kb.push()
        cs = kb.alloc([128, NT, 32], F32)
        cscmp = kb.alloc([128, 32], F32)
        gvecs = kb.alloc([128, 4, 128], F32)
        b_ec = Buf("evconst")
        cs_in = kb.inp("c_cs", [128, NT * 32])
        cscmp_in = kb.inp("c_cscmp", [128, 32])
        gv_in = kb.inp("qk_gT", [2, 128, 4 * 128])
        fw.op("sp", lambda e: [e.dma_start(out=cs.rearrange("p a b -> p (a b)"), in_=cs_in),
                               e.dma_start(out=cscmp, in_=cscmp_in),
                               e.dma_start(out=gvecs.rearrange("p a b -> p (a b)"), in_=gv_in[j])], writes=[b_ec], dma=True, ndma=3)
        ksT = kb.alloc([128, 4, S], BF16)
        kwT = kb.alloc([128, 4, S], BF16)
        vsA = kb.alloc([128, NT, 4, 130], BF16)
        vwA = kb.alloc([128, NT, 4, 130], BF16)
        kcmpT = kb.alloc([128, 4, 128], BF16)
        VC = kb.alloc([128, 4, 162], BF16)
        hid_b = kb.alloc([128, 2, 4, 128], BF16)
        w2b = kb.alloc([128, 2, 128], BF16)
        b_ks, b_kw, b_vs, b_vw, b_kc, b_VC, b_hid, b_w2 = [Buf(n) for n in "ks kw vs vw kcmp VC hid w2".split()]
        fw.op("pool", lambda e: e.memset(vsA[:, :, :, 128:130], 1.0), writes=[b_vs])
        fw.op("pool", lambda e: e.memset(vwA[:, :, :, 128:130], 1.0), writes=[b_vw])
        fw.op("pool", lambda e: e.memset(hid_b, 0.0), writes=[b_hid])
        w2_in = kb.inp("cmp_w2", [2, 2, 128, 128])
        fw.op("pool", lambda e: [e.dma_start(out=w2b[:, kv, :], in_=w2_in[j, kv]) for kv in range(2)], writes=[b_w2], dma=True, ndma=2)

        hn_sq = Rot([kb.alloc([128, 2, 128], F32) for _ in range(2)])
        hn_xn = Rot([kb.alloc([128, 2, 128], F32) for _ in range(2)])
        hn_ss = Rot([kb.alloc([128, 2], F32) for _ in range(2)])
        hn_tt = Rot([kb.alloc([128, 2, 4, 16], F32) for _ in range(2)])
        hn_ob = Rot([kb.alloc([128, 2, 128], BF16) for _ in range(3)])

        def headnorm(src, b_src, H, gi, cos, sin, b_cs, extra=None):
            sq, bsq = hn_sq.next()
            xn, bxn = hn_xn.next()
            ss, bss = hn_ss.next()
            tt_, btt = hn_tt.next()
            ob, bob = hn_ob.next()
            fw.op("act", lambda e: e.activation(out=sq[:, 0:H, :], in_=src, func=AF.Square), reads=[b_src], writes=[bsq])
            fw.op("dve", lambda e: e.tensor_reduce(out=ss[:, 0:H], in_=sq[:, 0:H, :], axis=AX.X, op=ALU.add), reads=[bsq], writes=[bss])
            fw.op("act", lambda e: e.activation(out=ss[:, 0:H], in_=ss[:, 0:H], func=AF.Sqrt, bias=eps6[:, 0:1], scale=1.0 / 128), reads=[bss, b_const], writes=[bss])
            fw.op("dve", lambda e: e.reciprocal(out=ss[:, 0:H], in_=ss[:, 0:H]), reads=[bss], writes=[bss])
            if extra is not None:
                fw.op("dve", lambda e: e.tensor_scalar(out=ss[:, 0:H], in0=ss[:, 0:H], scalar1=float(extra), scalar2=None, op0=ALU.mult), reads=[bss], writes=[bss])
            for h in range(H):
                fw.op("dve", lambda e, h=h: e.scalar_tensor_tensor(out=xn[:, h, :], in0=src[:, h, :], scalar=ss[:, h:h + 1], in1=gvecs[:, gi, :],
                                                                  op0=ALU.mult, op1=ALU.mult), reads=[b_src, bss, b_ec], writes=[bxn], join=(h > 0))
            for h in range(H):
                x1, x2 = xn[:, h, 0:16], xn[:, h, 16:32]
                t = tt_[:, h]
                fw.op("pool", lambda e, x1=x1, t=t: e.tensor_tensor(out=t[:, 0, :], in0=x1, in1=cos, op=ALU.mult), reads=[bxn, b_cs], writes=[btt], join=True)
                fw.op("pool", lambda e, x2=x2, t=t: e.tensor_tensor(out=t[:, 1, :], in0=x2, in1=sin, op=ALU.mult), reads=[bxn, b_cs], writes=[btt], join=True)
                fw.op("pool", lambda e, x2=x2, t=t: e.tensor_tensor(out=t[:, 2, :], in0=x2, in1=cos, op=ALU.mult), reads=[bxn, b_cs], writes=[btt], join=True)
                fw.op("pool", lambda e, x1=x1, t=t: e.tensor_tensor(out=t[:, 3, :], in0=x1, in1=sin, op=ALU.mult), reads=[bxn, b_cs], writes=[btt], join=True)
            for h in range(H):
                t = tt_[:, h]
                fw.op("dve", lambda e, t=t, h=h: e.tensor_tensor(out=ob[:, h, 0:16], in0=t[:, 0, :], in1=t[:, 1, :], op=ALU.subtract), reads=[btt], writes=[bob], join=(h > 0))
                fw.op("dve", lambda e, t=t, h=h: e.tensor_tensor(out=ob[:, h, 16:32], in0=t[:, 2, :], in1=t[:, 3, :], op=ALU.add), reads=[btt], writes=[bob], join=True)
            fw.op("act", lambda e: e.activation(out=ob[:, 0:H, 32:128], in_=xn[:, 0:H, 32:128], func=AF.Copy), reads=[bxn], writes=[bob], join=True)
            return ob, bob

        def transpose_to(ob, bob, H, dsts, b_dst, bank=4):
            pv = bfview(bank, 8)
            for h in range(H):
                fw.op("pe", lambda e, h=h: e.transpose(out=pv[:, h, :], in_=ob[:, h, :], identity=identb), reads=[bob, b_const], writes=[b_ps[bank]])
            for h in range(H):
                fw.op("dve", lambda e, h=h: e.tensor_copy(out=dsts[h], in_=pv[:, h, :]), reads=[b_ps[bank]], writes=[b_dst], join=True)

        kb.push()
        TG = 256
        hT = kb.alloc([128, KC, TG], BF16)
        b_hT = Buf("hT")
        xc_rot = Rot([kb.alloc([128, TG], F32) for _ in range(4)])
        sq_rot = Rot([kb.alloc([128, TG], BF16) for _ in range(2)])
        W1b = kb.alloc([128, 2, 32, 128], BF16)
        posf = kb.alloc([128, 2, 32], F32)
        posb16 = kb.alloc([128, 2, 32], BF16)
        posbias = kb.alloc([128, 2], F32)
        kcbuf = kb.alloc([128, 2, 4, 272], BF16)
        b_W1, b_pos, b_pb, b_kcb = Buf(), Buf(), Buf(), Buf()
        w1_in = kb.inp("cmp_w1", [2, 2, D, 128])
        pos_in = kb.inp("cmp_posT", [2, 128, 64])
        fw.op("pool", lambda e: [e.dma_start(out=W1b[:, kv], in_=w1_in[j, kv].rearrange("(l d) c -> d l c", d=128)) for kv in range(2)],
              writes=[b_W1], dma=True, ndma=2)
        fw.op("sp", lambda e: e.dma_start(out=posf.rearrange("p a b -> p (a b)"), in_=pos_in[j]), writes=[b_pos], dma=True)
        fw.op("act", lambda e: e.activation(out=posb16, in_=posf, func=AF.Copy), reads=[b_pos], writes=[b_pos])
        for kv in range(2):
            for li in range(32):
                fw.op("pe", lambda e, kv=kv, li=li: e.matmul(ps[5][:, kv:kv + 1], lhsT=W1b[:, kv, li, :], rhs=posb16[:, kv, li:li + 1],
                                                            start=(li == 0), stop=(li == 31)), reads=[b_W1, b_pos], writes=[b_ps[5]])
        fw.op("dve", lambda e: e.tensor_copy(out=posbias, in_=ps[5][:, 0:2]), reads=[b_ps[5]], writes=[b_pb])
        fw.op("pool", lambda e: e.memset(kcbuf, 0.0), writes=[b_kcb])
        qst = [kb.alloc([128, 16, 128], BF16) for _ in range(2)]
        b_qst = [Buf(), Buf()]
        stf = Rot([kb.alloc([128, 256], F32) for _ in range(4)])
        stb = Rot([kb.alloc([128, 256], BF16) for _ in range(4)])
        gel = kb.alloc([128, 3, 4, 16], F32)
        b_gel = Buf()
        pbn = [0]

        def nextbank():
            pbn[0] = (pbn[0] + 1) % 4
            return pbn[0]

        for tg in range(S // TG):
            t0 = tg * TG
            norm_stage(l, 0, t0, TG, hT, b_hT, xc_rot, sq_rot)
            for bi in range(len(win.blocks)):
                wt, bw, _k0, _kk, n0, w = load_block(win, bi)
                if 8 <= bi <= 11 or bi == 37:
                    ng = 2 if bi != 37 else 1
                    for gi in range(ng):
                        pb = nextbank()
                        m = 128 if bi != 37 else 16
                        for c in range(KC):
                            fw.op("pe", lambda e, wt=wt, gi=gi, c=c, pb=pb, m=m: e.matmul(ps[pb][0:m, 0:TG], lhsT=wt[:, c, gi * 128:gi * 128 + m], rhs=hT[:, c, :],
                                                                                      start=(c == 0), stop=(c == KC - 1)), reads=[bw, b_hT], writes=[b_ps[pb]])
                        if bi != 37:
                            kv, g = (bi - 8) // 2, ((bi - 8) % 2) * 2 + gi
                            fw.op("act", lambda e, kv=kv, g=g, pb=pb: e.activation(out=kcbuf[:, kv, g, 16:272], in_=ps[pb][:, 0:TG], func=AF.Copy),
                                  reads=[b_ps[pb]], writes=[b_kcb], join=True)
                        else:
                            sf, bsf = stf.next()
                            fw.op("act", lambda e, sf=sf, pb=pb: e.activation(out=sf[0:16, 0:TG], in_=ps[pb][0:16, 0:TG], func=AF.Copy), reads=[b_ps[pb]], writes=[bsf])
                            fw.op("act", lambda e, sf=sf, t0=t0: e.dma_start(out=ev["alow"][:, t0:t0 + TG], in_=sf[0:16, 0:TG]), reads=[bsf], writes=[eb["alow"]],
                                  dma=True, semkey=bsf, join=True)
                    continue
                for tt in range(2):
                    T = tg * 2 + tt
                    pb = nextbank()
                    for c in range(KC):
                        fw.op("pe", lambda e, wt=wt, c=c, pb=pb, tt=tt, w=w: e.matmul(ps[pb][:, 0:w], lhsT=hT[:, c, tt * 128:(tt + 1) * 128], rhs=wt[:, c, 0:w],
                                                                                  start=(c == 0), stop=(c == KC - 1)), reads=[bw, b_hT], writes=[b_ps[pb]])
                    pt = ps[pb][:, 0:256].rearrange("p (a b) -> p a b", a=2)
                    rows = slice(T * 128, (T + 1) * 128)
                    if bi < 8:
                        ob, bob = headnorm(pt, b_ps[pb], 2, 0, cs[:, T, 0:16], cs[:, T, 16:32], b_ec, extra=128 ** -0.5)
                        transpose_to(ob, bob, 2, [qst[tt][:, bi * 2 + h, :] for h in range(2)], b_qst[tt])
                        if bi == 7:
                            fw.op("act", lambda e, tt=tt, T=T: e.dma_start(out=ev["qT"][T], in_=qst[tt].rearrange("p a b -> p (a b)")), reads=[b_qst[tt]],
                                  writes=[eb["qT"]], dma=True, semkey=b_qst[tt], join=True)
                    elif bi in (12, 13, 16, 17):
                        isw = bi >= 16
                        ob, bob = headnorm(pt, b_ps[pb], 2, 3 if isw else 2, cs[:, T, 0:16], cs[:, T, 16:32], b_ec)
                        dstT = kwT if isw else ksT
                        g0 = (bi % 2) * 2
                        transpose_to(ob, bob, 2, [dstT[:, g0 + h, T * 128:(T + 1) * 128] for h in range(2)], b_kw if isw else b_ks)
                    elif bi in (14, 15, 18, 19):
                        isw = bi >= 18
                        dv = vwA if isw else vsA
                        g0 = (bi % 2) * 2
                        fw.op("act", lambda e, dv=dv, T=T, g0=g0, pt=pt: e.activation(out=dv[:, T, g0:g0 + 2, 0:128], in_=pt, func=AF.Copy),
                              reads=[b_ps[pb]], writes=[b_vw if isw else b_vs], join=True)
                    elif bi == 20:
                        sf, bsf = stf.next()
                        fw.op("act", lambda e, sf=sf, pb=pb: e.activation(out=sf[:, 0:48], in_=ps[pb][:, 0:48], func=AF.Sigmoid), reads=[b_ps[pb]], writes=[bsf])
                        fw.op("act", lambda e, sf=sf, rows=rows: e.dma_start(out=ev["gsig"][rows, :], in_=sf[:, 0:48]), reads=[bsf], writes=[eb["gsig"]],
                              dma=True, semkey=bsf, join=True)
                    elif 21 <= bi <= 28:
                        nm = "gq" if bi <= 24 else "gk"
                        co = ((bi - 21) % 4) * 256
                        sf, bsf = stf.next()
                        fw.op("act", lambda e, sf=sf, pb=pb: e.activation(out=sf, in_=ps[pb][:, 0:256], func=AF.Copy), reads=[b_ps[pb]], writes=[bsf])
                        fw.op("act", lambda e, sf=sf, rows=rows, nm=nm, co=co: e.dma_start(out=ev[nm][rows, co:co + 256], in_=sf), reads=[bsf], writes=[eb[nm]],
                              dma=True, semkey=bsf, join=True)
                    else:
                        isr = bi >= 38
                        nm = "grs" if isr else "gv"
                        co = ((bi - 38) if isr else (bi - 29)) * 256
                        sb_, bsb = stb.next()
                        fw.op("act", lambda e, sb_=sb_, pb=pb, isr=isr: e.activation(out=sb_, in_=ps[pb][:, 0:256], func=AF.Silu if isr else AF.Copy),
                              reads=[b_ps[pb]], writes=[bsb])
                        fw.op("act", lambda e, sb_=sb_, rows=rows, nm=nm, co=co: e.dma_start(out=ev[nm][rows, co:co + 256], in_=sb_), reads=[bsb], writes=[eb[nm]],
                              dma=True, semkey=bsb, join=True)
                if bi == 12:
                    nl0 = 1 if tg == 0 else 0
                    pc = ps[5][:, 0:128].rearrange("p (a b) -> p a b", a=8)
                    for kv in range(2):
                        for g in range(4):
                            for li in range(32):
                                fw.op("pe", lambda e, kv=kv, g=g, li=li, nl0=nl0: e.matmul(pc[:, kv * 4 + g, nl0:16], lhsT=W1b[:, kv, li, :],
                                                                               rhs=kcbuf[:, kv, g, li + 16 * nl0:li + 241:16],
                                                                               start=(li == 0), stop=(li == 31)), reads=[b_W1, b_kcb], writes=[b_ps[5]])
                    pc4 = ps[5][:, 0:128].rearrange("p (k a b) -> p k a b", k=2, a=4)
                    nn0 = 16 * tg - 1 + nl0
                    for kv in range(2):
                        xx, x2, x3 = gel[:, 0, :, nl0:16], gel[:, 1, :, nl0:16], gel[:, 2, :, nl0:16]
                        fw.op("act", lambda e, kv=kv, xx=xx, nl0=nl0: e.activation(out=xx, in_=pc4[:, kv, :, nl0:16], func=AF.Identity, bias=posbias[:, kv:kv + 1], scale=1.0),
                              reads=[b_ps[5], b_pb], writes=[b_gel])
                        fw.op("dve", lambda e, xx=xx, x2=x2: e.tensor_tensor(out=x2, in0=xx, in1=xx, op=ALU.mult), reads=[b_gel], writes=[b_gel])
                        fw.op("dve", lambda e, x2=x2: e.tensor_scalar(out=x2, in0=x2, scalar1=0.044715, scalar2=1.0, op0=ALU.mult, op1=ALU.add), reads=[b_gel], writes=[b_gel])
                        fw.op("dve", lambda e, xx=xx, x2=x2, x3=x3: e.tensor_tensor(out=x3, in0=x2, in1=xx, op=ALU.mult), reads=[b_gel], writes=[b_gel])
                        fw.op("act", lambda e, x3=x3: e.activation(out=x3, in_=x3, func=AF.Sigmoid, scale=1.5957691216057308), reads=[b_gel], writes=[b_gel])
                        fw.op("dve", lambda e, kv=kv, xx=xx, x3=x3, nn0=nn0, nl0=nl0: e.tensor_tensor(out=hid_b[:, kv, :, nn0:nn0 + 16 - nl0], in0=xx, in1=x3, op=ALU.mult),
                              reads=[b_gel], writes=[b_hid], join=True)
                    fw.op("pool", lambda e: e.tensor_copy(out=kcbuf[:, :, :, 0:16], in_=kcbuf[:, :, :, 256:272]), reads=[b_kcb], writes=[b_kcb])
        kb.pop()

        if cfg.get("ev_stop") == 1:
            kb.pop()
            return
        ovl_in = kb.inp("c_ovl", [128, 33])
        ovf = kb.alloc([128, 33], F32)
        b_ov = Buf()
        fw.op("sp", lambda e: e.dma_start(out=ovf, in_=ovl_in), writes=[b_ov], dma=True)
        for g in range(4):
            fw.op("act", lambda e, g=g: e.activation(out=VC[:, g, 128:161], in_=ovf, func=AF.Copy), reads=[b_ov], writes=[b_VC], join=True)
            fw.op("pe", lambda e, g=g: e.matmul(ps[0][:, 0:128], lhsT=hid_b[:, 1, g, :], rhs=w2b[:, 1, :], start=True, stop=True), reads=[b_hid, b_w2], writes=[b_ps[0]])
            fw.op("act", lambda e, g=g: e.activation(out=VC[:, g, 0:128], in_=ps[0][:, 0:128], func=AF.Copy), reads=[b_ps[0]], writes=[b_VC], join=True)
            fw.op("pe", lambda e, g=g: e.matmul(ps[1][:, 0:128], lhsT=hid_b[:, 0, g, :], rhs=w2b[:, 0, :], start=True, stop=True), reads=[b_hid, b_w2], writes=[b_ps[1]])
            ob, bob = headnorm(ps[1][:, 0:128].rearrange("p (a b) -> p a b", a=1), b_ps[1], 1, 1, cscmp[:, 0:16], cscmp[:, 16:32], b_ec)
            transpose_to(ob, bob, 1, [kcmpT[:, g, :]], b_kc)

        sbtap("VC", VC, b_VC)
        sbtap("kcmpT", kcmpT, b_kc)
        sbtap("hid", hid_b, b_hid)
        if cfg.get("ev_stop") == 15:
            kb.pop()
            return
        kb.push()
        E_f = kb.alloc([32, NT * 128], F32)
        E_b = kb.alloc([32, NT, 128], BF16)
        E_in = kb.inp("c_E", [32, NT * 128])
        b_E = Buf()
        fw.op("sp", lambda e: e.dma_start(out=E_f, in_=E_in), writes=[b_E], dma=True)
        fw.op("act", lambda e: e.activation(out=E_b.rearrange("p a b -> p (a b)"), in_=E_f, func=AF.Copy), reads=[b_E], writes=[b_E])
        q_rot = Rot([kb.alloc([128, 16, 128], BF16) for _ in range(2)])
        gs_rot = Rot([kb.alloc([128, 16, 3], F32) for _ in range(2)])
        acc_rot = Rot([kb.alloc([128, 16, 128], F32) for _ in range(2)])
        accb_rot = Rot([kb.alloc([128, 16, 128], BF16) for _ in range(2)])
        on_rot = Rot([kb.alloc([128, 16, 128], BF16) for _ in range(2)])
        p_rot = Rot([kb.alloc([128, 4, 128], BF16) for _ in range(4)])
        sm_rot = Rot([kb.alloc([128, 160], F32) for _ in range(3)])
        ns_rot = Rot([kb.alloc([32, 4, 128], BF16) for _ in range(2)])
        BIG = 1.0e30
        def p2_tile(T):
            qt, bq = q_rot.next()
            gs, bgs = gs_rot.next()
            acc, bacc = acc_rot.next()
            fw.op("sp", lambda e, qt=qt, T=T: e.dma_start(out=qt.rearrange("p a b -> p (a b)"), in_=ev["qT"][T]), reads=[eb["qT"]], writes=[bq], dma=True)
            fw.op("sp", lambda e, gs=gs, T=T: e.dma_start(out=gs.rearrange("p a b -> p (a b)"), in_=ev["gsig"][T * 128:(T + 1) * 128, :]), reads=[eb["gsig"]], writes=[bgs], dma=True)
            nsT_static = [None]
            def p2_group(g):
                q4 = qt[:, 4 * g:4 * g + 4, :].rearrange("p a b -> p (a b)")
                sm, bsm = sm_rot.next()
                rc, wc, imp, impm, m8, m8b, tmp32 = sm[:, 0:4], sm[:, 4:8], sm[:, 8:40], sm[:, 40:72], sm[:, 72:80], sm[:, 80:88], sm[:, 88:120]
                r2, w2 = sm[:, 120:124], sm[:, 124:128]
                nsel = sm[:, 128:160]
                fw.op("pe", lambda e, g=g, q4=q4: e.matmul(ps[0][:], lhsT=kcmpT[:, g, :], rhs=q4, start=True, stop=True), reads=[b_kc, bq], writes=[b_ps[0]])
                if cfg.get("p2_cut", 99) <= 0:
                    return
                pt_, bp = p_rot.next()
                fw.op("act", lambda e, pt_=pt_: e.activation(out=pt_.rearrange("p a b -> p (a b)"), in_=ps[0][:], func=AF.Exp), reads=[b_ps[0]], writes=[bp])
                if cfg.get("p2_cut", 99) <= 1:
                    return
                fw.op("pool", lambda e, pt_=pt_, T=T: e.affine_select(out=pt_, in_=pt_, pattern=[[0, 4], [1, 128]], compare_op=ALU.is_ge, fill=0.0,
                                                                   base=128 * T - 31, channel_multiplier=-16), reads=[bp], writes=[bp])
                if cfg.get("p2_cut", 99) <= 2:
                    return
                U = [ps[2][:].rearrange("p (a b) -> p a b", a=2), ps[3][:].rearrange("p (a b) -> p a b", a=2)]
                for h in range(4):
                    fw.op("pe", lambda e, pt_=pt_, h=h, g=g: e.matmul(U[h // 2][:, h % 2, 0:161], lhsT=pt_[:, h, :], rhs=VC[:, g, 0:161], start=True, stop=True),
                          reads=[bp, b_VC], writes=[b_ps[2 + h // 2]])
                if cfg.get("p2_cut", 99) <= 3:
                    return
                for hh in range(2):
                    fw.op("dve", lambda e, hh=hh: e.tensor_scalar(out=rc[:, 2 * hh:2 * hh + 2], in0=U[hh][:, :, 160], scalar1=1e-30, scalar2=None, op0=ALU.max),
                          reads=[b_ps[2 + hh]], writes=[bsm], join=(hh > 0))
                fw.op("dve", lambda e: e.reciprocal(out=rc, in_=rc), reads=[bsm], writes=[bsm])
                for h in range(4):
                    if h == 0:
                        fw.op("dve", lambda e: e.tensor_scalar(out=imp, in0=U[0][:, 0, 128:160], scalar1=rc[:, 0:1], scalar2=None, op0=ALU.mult),
                              reads=[bsm, b_ps[2]], writes=[bsm])
                    else:
                        fw.op("dve", lambda e, h=h: e.scalar_tensor_tensor(out=imp, in0=U[h // 2][:, h % 2, 128:160], scalar=rc[:, h:h + 1], in1=imp, op0=ALU.mult, op1=ALU.add),
                              reads=[bsm, b_ps[2 + h // 2]], writes=[bsm])
                fw.op("dve", lambda e, g=g: e.tensor_tensor(out=wc, in0=rc, in1=gs[:, 4 * g:4 * g + 4, 0], op=ALU.mult), reads=[bsm, bgs], writes=[bsm])
                if cfg.get("p2_cut", 99) <= 4:
                    return
                for h in range(4):
                    fw.op("act", lambda e, h=h, g=g: e.activation(out=acc[:, 4 * g + h, :], in_=U[h // 2][:, h % 2, 0:128], func=AF.Copy, scale=wc[:, h:h + 1]),
                          reads=[bsm, b_ps[2 + h // 2]], writes=[bacc], join=True)
                if cfg.get("p2_cut", 99) <= 5:
                    return
                if T < 8 and nsT_static[0] is not None:
                    nsT, bns = nsT_static[0]
                else:
                    if T < 8:
                        fw.op("pool", lambda e: e.memset(nsel, -1.0), writes=[bsm])
                        for hf in range(2):
                            cur = 2 * T + hf
                            fw.op("pool", lambda e, hf=hf, cur=cur: e.memset(nsel[64 * hf:64 * hf + 64, 0:cur + 1], 0.0), writes=[bsm])
                    else:
                        fw.op("dve", lambda e: e.tensor_copy(out=impm, in_=imp), reads=[bsm], writes=[bsm])
                        for hf in range(2):
                            cur = 2 * T + hf
                            rs_ = slice(64 * hf, 64 * hf + 64)
                            if cur + 1 < 32:
                                fw.op("dve", lambda e, rs_=rs_, cur=cur: e.memset(impm[rs_, cur + 1:32], -BIG), reads=[bsm], writes=[bsm])
                            fw.op("dve", lambda e, rs_=rs_, cur=cur: e.memset(impm[rs_, cur - 1:cur + 1], BIG), reads=[bsm], writes=[bsm])
                        fw.op("dve", lambda e: e.memset(impm[:, 0:1], BIG), reads=[bsm], writes=[bsm])
                        fw.op("dve", lambda e: e.max(out=m8, in_=impm), reads=[bsm], writes=[bsm])
                        fw.op("dve", lambda e: e.match_replace(out=tmp32, in_to_replace=m8, in_values=impm, imm_value=-BIG), reads=[bsm], writes=[bsm])
                        fw.op("dve", lambda e: e.max(out=m8b, in_=tmp32), reads=[bsm], writes=[bsm])
                        fw.op("dve", lambda e: e.tensor_scalar(out=nsel, in0=impm, scalar1=m8b[:, 7:8], scalar2=1.0, op0=ALU.is_ge, op1=ALU.subtract), reads=[bsm], writes=[bsm])
                    if cfg.get("p2_cut", 99) <= 5.3:
                        return
                    fw.op("pe", lambda e: e.transpose(out=ps[1][0:32, 0:128], in_=nsel, identity=identf), reads=[bsm, b_const], writes=[b_ps[1]])
                    if cfg.get("p2_cut", 99) <= 5.6:
                        return
                    nsT, bns = ns_rot.next()
                    for h in range(4):
                        eng = "dve" if h % 2 == 0 else "act"
                        if eng == "dve":
                            fw.op("dve", lambda e, h=h, nsT=nsT: e.tensor_copy(out=nsT[:, h, :], in_=ps[1][0:32, 0:128]), reads=[b_ps[1]], writes=[bns], join=(h > 0))
                        else:
                            fw.op("act", lambda e, h=h, nsT=nsT: e.activation(out=nsT[:, h, :], in_=ps[1][0:32, 0:128], func=AF.Copy), reads=[b_ps[1]], writes=[bns], join=True)
                    if T < 8:
                        nsT_static[0] = (nsT, bns)
                if cfg.get("p2_cut", 99) <= 6:
                    return
                O = [ps[6][:].rearrange("p (a b) -> p a b", a=2), ps[7][:].rearrange("p (a b) -> p a b", a=2)]
                def p2_br(br):
                    kts = list(range(0, T + 1)) if br == 1 else list(range(max(0, T - 4), T + 1))
                    kT_, vA, bk_, bv_ = (ksT, vsA, b_ks, b_vs) if br == 1 else (kwT, vwA, b_kw, b_vw)
                    def p2_kt(ki, kt):
                        sb = 4 + (ki % 2)
                        fw.op("pe", lambda e, kT_=kT_, g=g, kt=kt, q4=q4, sb=sb, br=br: e.matmul(ps[sb][:], lhsT=kT_[:, g, kt * 128:(kt + 1) * 128], rhs=q4,
                                                                                             start=True, stop=(br == 2)), reads=[bk_, bq], writes=[b_ps[sb]])
                        if br == 1:
                            fw.op("pe", lambda e, kt=kt, nsT=nsT, sb=sb: e.matmul(ps[sb][:], lhsT=E_b[:, kt, :], rhs=nsT.rearrange("p a b -> p (a b)"),
                                                                                 start=False, stop=True), reads=[b_E, bns], writes=[b_ps[sb]])
                        pt_, bp = p_rot.next()
                        fw.op("act", lambda e, pt_=pt_, sb=sb: e.activation(out=pt_.rearrange("p a b -> p (a b)"), in_=ps[sb][:], func=AF.Exp), reads=[b_ps[sb]], writes=[bp])
                        if kt == T:
                            fw.op("pool", lambda e, pt_=pt_: e.affine_select(out=pt_, in_=pt_, pattern=[[0, 4], [1, 128]], compare_op=ALU.is_ge, fill=0.0,
                                                                            base=0, channel_multiplier=-1), reads=[bp], writes=[bp])
                        if br == 2 and kt == T - 4:
                            fw.op("pool", lambda e, pt_=pt_: e.affine_select(out=pt_, in_=pt_, pattern=[[0, 4], [-1, 128]], compare_op=ALU.is_gt, fill=0.0,
                                                                            base=0, channel_multiplier=1), reads=[bp], writes=[bp])
                        for h in range(4):
                            fw.op("pe", lambda e, pt_=pt_, h=h, vA=vA, kt=kt, g=g, ki=ki, nk=len(kts): e.matmul(
                                O[h // 2][:, h % 2, 0:129], lhsT=pt_[:, h, :], rhs=vA[:, kt, g, 0:129], start=(ki == 0 and h % 2 == 0), stop=(ki == nk - 1), skip_group_check=True),
                                reads=[bp, bv_], writes=[b_ps[6 + h // 2]])
                    for ki, kt in enumerate(kts):
                        p2_kt(ki, kt)
                    for hh in range(2):
                        fw.op("dve", lambda e, hh=hh: e.reciprocal(out=r2[:, 2 * hh:2 * hh + 2], in_=O[hh][:, :, 128]), reads=[b_ps[6 + hh]], writes=[bsm], join=True)
                    fw.op("dve", lambda e, g=g, br=br: e.tensor_tensor(out=w2, in0=r2, in1=gs[:, 4 * g:4 * g + 4, br], op=ALU.mult), reads=[bsm, bgs], writes=[bsm])
                    for h in range(4):
                        fw.op("dve", lambda e, h=h, g=g: e.scalar_tensor_tensor(out=acc[:, 4 * g + h, :], in0=O[h // 2][:, h % 2, 0:128], scalar=w2[:, h:h + 1],
                                                                             in1=acc[:, 4 * g + h, :], op0=ALU.mult, op1=ALU.add),
                              reads=[bsm, b_ps[6 + h // 2], bacc], writes=[bacc])
                for br in cfg.get("p2_brs", (1, 2)):
                    p2_br(br)
            for g in range(4):
                p2_group(g)
            if cfg.get("p2_cut", 99) <= 7:
                return
            accb, baccb = accb_rot.next()
            fw.op("act", lambda e, accb=accb, acc=acc: e.activation(out=accb.rearrange("p a b -> p (a b)"), in_=acc.rearrange("p a b -> p (a b)"), func=AF.Copy),
                  reads=[bacc], writes=[baccb])
            on, bon = on_rot.next()
            for hb in range(2):
                pv = bfview(1, 8)
                for h in range(8):
                    fw.op("pe", lambda e, hb=hb, h=h, accb=accb, pv=pv: e.transpose(out=pv[:, h, :], in_=accb[:, hb * 8 + h, :], identity=identb),
                          reads=[baccb, b_const], writes=[b_ps[1]])
                fw.op("dve", lambda e, hb=hb, on=on, pv=pv: e.tensor_copy(out=on[:, hb * 8:hb * 8 + 8, :], in_=pv), reads=[b_ps[1]], writes=[bon], join=(hb > 0))
            fw.op("act", lambda e, on=on, T=T: e.dma_start(out=ev["onT"][T], in_=on.rearrange("p a b -> p (a b)")), reads=[bon], writes=[eb["onT"]],
                  dma=True, semkey=bon, join=True)
        for T in range(cfg.get("p2_tiles", NT)):
            p2_tile(T)
        kb.pop()
        kb.pop()
        if cfg.get("ev_stop") == 2:
            return

        kb.push()
        tri_in = kb.inp("c_tri", [128, 3 * 128])
        cind_in = kb.inp("c_cind", [128, 2])
        gng_in = kb.inp("gla_ngT", [2, 128, 512])
        wa2_in = kb.inp("gla_w_a2", [2, 16, 1024])
        ba_in = kb.inp("gla_b_a", [2, 1024])
        tri = kb.alloc([128, 3, 128], F32)
        cind = kb.alloc([128, 2], F32)
        gng = kb.alloc([128, 512], F32)
        wa2 = kb.alloc([16, 1024], F32)
        ba = kb.alloc([1, 1024], F32)
        ones1 = kb.alloc([1, 128], F32)
        b_gc = Buf()
        fw.op("sp", lambda e: [e.dma_start(out=tri.rearrange("p a b -> p (a b)"), in_=tri_in), e.dma_start(out=cind, in_=cind_in),
                               e.dma_start(out=gng, in_=gng_in[j]), e.dma_start(out=wa2, in_=wa2_in[j]),
                               e.dma_start(out=ba, in_=ba_in[j:j + 1, :])], writes=[b_gc], dma=True, ndma=5)
        fw.op("pool", lambda e: e.memset(ones1, 1.0), writes=[b_gc], join=True)
        st_f = kb.alloc([128, 8, 512], F32)
        st_b = kb.alloc([128, 8, 512], BF16)
        b_stf = [Buf() for _ in range(8)]
        b_stb = [Buf() for _ in range(8)]
        fw.op("pool", lambda e: e.memset(st_f, 0.0), writes=b_stf)
        fw.op("pool", lambda e: e.memset(st_b, 0.0), writes=b_stb)
        gq_rot = Rot([kb.alloc([128, 1024], F32) for _ in range(2)])
        gk_rot = Rot([kb.alloc([128, 1024], F32) for _ in range(2)])
        gv_rot = Rot([kb.alloc([128, 2048], BF16) for _ in range(2)])
        gr_rot = Rot([kb.alloc([128, 2048], BF16) for _ in range(2)])
        al_rot = Rot([kb.alloc([16, 128], F32) for _ in range(2)])
        sp_t = kb.alloc([128, 1024], F32)
        ex_rot = Rot([kb.alloc([128, 1024], F32) for _ in range(2)])
        qin = kb.alloc([128, 1024], BF16)
        kin = kb.alloc([128, 1024], BF16)
        kout = kb.alloc([128, 1024], BF16)
        qT_ = kb.alloc([128, 8, 128], BF16)
        kT_g = kb.alloc([128, 8, 128], BF16)
        qA = kb.alloc([128, 8, 128], BF16)
        qB = kb.alloc([128, 8, 128], BF16)
        dec = kb.alloc([128, 8, 2], F32)
        aTb_rot = Rot([kb.alloc([128, 128], BF16) for _ in range(2)])
        on_f = Rot([kb.alloc([128, 512], F32) for _ in range(2)])
        ob_rot = Rot([kb.alloc([128, 4, 128], BF16) for _ in range(2)])
        ssg = Rot([kb.alloc([128, 2], F32) for _ in range(2)])
        omT = kb.alloc([128, KC, 256], BF16)
        b_om = Buf("omT")
        b_sp, b_qin, b_kin, b_kout, b_qT, b_kT, b_qA, b_qB, b_dec = [Buf() for _ in range(9)]
        fw.op("pool", lambda e: e.memset(qA, 0.0), writes=[b_qA])
        fw.op("pool", lambda e: e.memset(qB, 0.0), writes=[b_qB])
        xr_rot = Rot([kb.alloc([128, 256], F32) for _ in range(4)])
        def p3_tile(T):
            rows = slice(T * 128, (T + 1) * 128)
            tc0 = (T % 2) * 128
            gq, bgq = gq_rot.next()
            gk, bgk = gk_rot.next()
            gvt, bgv = gv_rot.next()
            grt, bgr = gr_rot.next()
            al, bal = al_rot.next()
            fw.op("sp", lambda e, gq=gq, rows=rows: e.dma_start(out=gq, in_=ev["gq"][rows, :]), reads=[eb["gq"]], writes=[bgq], dma=True)
            fw.op("sp", lambda e, gk=gk, rows=rows: e.dma_start(out=gk, in_=ev["gk"][rows, :]), reads=[eb["gk"]], writes=[bgk], dma=True)
            fw.op("sp", lambda e, gvt=gvt, rows=rows: e.dma_start(out=gvt, in_=ev["gv"][rows, :]), reads=[eb["gv"]], writes=[bgv], dma=True)
            fw.op("sp", lambda e, grt=grt, rows=rows: e.dma_start(out=grt, in_=ev["grs"][rows, :]), reads=[eb["grs"]], writes=[bgr], dma=True)
            fw.op("sp", lambda e, al=al, rows=rows: e.dma_start(out=al, in_=ev["alow"][:, rows]), reads=[eb["alow"]], writes=[bal], dma=True)
            fw.op("sp", lambda e, T=T, tc0=tc0: e.dma_start(out=omT[:, 0:16, tc0:tc0 + 128], in_=ev["onT"][T].rearrange("p (a b) -> p a b", a=16)),
                  reads=[eb["onT"]], writes=[b_om], dma=True, join=True)
            for hf in range(2):
                fw.op("pe", lambda e, hf=hf, al=al: e.matmul(ps[hf][:], lhsT=al, rhs=wa2[:, hf * 512:(hf + 1) * 512], start=True, stop=False),
                      reads=[bal, b_gc], writes=[b_ps[hf]])
                fw.op("pe", lambda e, hf=hf: e.matmul(ps[hf][:], lhsT=ones1, rhs=ba[:, hf * 512:(hf + 1) * 512], start=False, stop=True),
                      reads=[b_gc], writes=[b_ps[hf]])
                fw.op("act", lambda e, hf=hf: e.activation(out=sp_t[:, hf * 512:(hf + 1) * 512], in_=ps[hf][:], func=AF.Exp, scale=-1.0), reads=[b_ps[hf]], writes=[b_sp], join=(hf > 0))
            fw.op("act", lambda e: e.activation(out=sp_t, in_=sp_t, func=AF.Ln, bias=1.0, scale=1.0), reads=[b_sp], writes=[b_sp])
            for hf in range(2):
                fw.op("pe", lambda e, hf=hf: e.matmul(ps[2 + hf][:], lhsT=tri[:, 0, :], rhs=sp_t[:, hf * 512:(hf + 1) * 512], start=True, stop=True),
                      reads=[b_sp, b_gc], writes=[b_ps[2 + hf]])
                fw.op("pe", lambda e, hf=hf: e.matmul(ps[4 + hf][:], lhsT=tri[:, 1, :], rhs=sp_t[:, hf * 512:(hf + 1) * 512], start=True, stop=True),
                      reads=[b_sp, b_gc], writes=[b_ps[4 + hf]])
            for ds in range(8):
                fw.op("pe", lambda e, ds=ds: e.matmul(ps[6][:, 2 * ds:2 * ds + 2], lhsT=sp_t[:, ds * 128:(ds + 1) * 128], rhs=cind, start=True, stop=True),
                      reads=[b_sp, b_gc], writes=[b_ps[6]])
            fw.op("act", lambda e: e.activation(out=dec.rearrange("p a b -> p (a b)"), in_=ps[6][:, 0:16], func=AF.Exp, scale=-1.0 / 16), reads=[b_ps[6]], writes=[b_dec])
            ex, bex = ex_rot.next()
            for hf in range(2):
                fw.op("act", lambda e, hf=hf, ex=ex: e.activation(out=ex[:, hf * 512:(hf + 1) * 512], in_=ps[2 + hf][:], func=AF.Exp), reads=[b_ps[2 + hf]], writes=[bex], join=(hf > 0))
            fw.op("dve", lambda e, ex=ex, gq=gq: e.scalar_tensor_tensor(out=qin, in0=gq, scalar=0.0625, in1=ex, op0=ALU.mult, op1=ALU.mult), reads=[bex, bgq], writes=[b_qin])
            ex, bex = ex_rot.next()
            for hf in range(2):
                fw.op("act", lambda e, hf=hf, ex=ex: e.activation(out=ex[:, hf * 512:(hf + 1) * 512], in_=ps[2 + hf][:], func=AF.Exp, scale=-1.0), reads=[b_ps[2 + hf]], writes=[bex], join=(hf > 0))
            fw.op("dve", lambda e, ex=ex, gk=gk: e.tensor_tensor(out=kin, in0=gk, in1=ex, op=ALU.mult), reads=[bex, bgk], writes=[b_kin])
            ex, bex = ex_rot.next()
            for hf in range(2):
                fw.op("act", lambda e, hf=hf, ex=ex: e.activation(out=ex[:, hf * 512:(hf + 1) * 512], in_=ps[4 + hf][:], func=AF.Exp), reads=[b_ps[4 + hf]], writes=[bex], join=(hf > 0))
            fw.op("dve", lambda e, ex=ex, gk=gk: e.tensor_tensor(out=kout, in0=gk, in1=ex, op=ALU.mult), reads=[bex, bgk], writes=[b_kout])
            pv = bfview(7, 8)
            for (src, bsrc, dst, bdst) in ((qin, b_qin, qT_, b_qT), (kin, b_kin, kT_g, b_kT)):
                for s8 in range(8):
                    fw.op("pe", lambda e, src=src, s8=s8: e.transpose(out=pv[:, s8, :], in_=src[:, s8 * 128:(s8 + 1) * 128], identity=identb),
                          reads=[bsrc, b_const], writes=[b_ps[7]])
                fw.op("dve", lambda e, dst=dst: e.tensor_copy(out=dst, in_=pv), reads=[b_ps[7]], writes=[bdst])
            fw.op("act", lambda e: e.activation(out=qA[:, :, 0:64], in_=qT_[:, :, 0:64], func=AF.Copy), reads=[b_qT], writes=[b_qA])
            fw.op("act", lambda e: e.activation(out=qB[:, :, 64:128], in_=qT_[:, :, 64:128], func=AF.Copy), reads=[b_qT], writes=[b_qB])
            def p3_head(hd):
                for ch in range(2):
                    s8 = hd * 2 + ch
                    fw.op("pe", lambda e, s8=s8, ch=ch: e.matmul(ps[0][:, 0:128], lhsT=kT_g[:, s8, :], rhs=qT_[:, s8, :], start=(ch == 0), stop=(ch == 1)),
                          reads=[b_kT, b_qT], writes=[b_ps[0]])
                aTb, baT = aTb_rot.next()
                fw.op("dve", lambda e, aTb=aTb: e.tensor_tensor(out=aTb, in0=ps[0][:, 0:128], in1=tri[:, 2, :], op=ALU.mult), reads=[b_ps[0], b_gc], writes=[baT])
                vh = gvt[:, hd * 512:(hd + 1) * 512]
                fw.op("pe", lambda e, aTb=aTb, vh=vh: e.matmul(ps[1][:], lhsT=aTb, rhs=vh, start=True, stop=False), reads=[baT, bgv], writes=[b_ps[1]])
                for half, qX, bqX in ((0, qA, b_qA), (1, qB, b_qB)):
                    for ch in range(2):
                        s8 = hd * 2 + ch
                        fw.op("pe", lambda e, qX=qX, s8=s8, half=half, ch=ch: e.matmul(ps[1][:], lhsT=qX[:, s8, :], rhs=st_b[:, s8, :], start=False,
                                                                                    stop=(half == 1 and ch == 1)), reads=[bqX, b_stb[s8]], writes=[b_ps[1]])
                    r0 = 64 * half
                    for ch in range(2):
                        s8 = hd * 2 + ch
                        pb = 2 + ch + 2 * half
                        fw.op("pe", lambda e, s8=s8, r0=r0, pb=pb, vh=vh: e.matmul(ps[pb][:], lhsT=kout[r0:r0 + 64, s8 * 128:(s8 + 1) * 128], rhs=vh[r0:r0 + 64, :],
                                                                                start=True, stop=True), reads=[b_kout, bgv], writes=[b_ps[pb]])
                        fw.op("dve", lambda e, s8=s8, pb=pb, half=half: e.scalar_tensor_tensor(out=st_f[:, s8, :], in0=st_f[:, s8, :], scalar=dec[:, s8, half:half + 1],
                                                                                             in1=ps[pb][:], op0=ALU.mult, op1=ALU.add),
                              reads=[b_stf[s8], b_dec, b_ps[pb]], writes=[b_stf[s8]])
                        fw.op("act", lambda e, s8=s8: e.activation(out=st_b[:, s8, :], in_=st_f[:, s8, :], func=AF.Copy), reads=[b_stf[s8]], writes=[b_stb[s8]])
                onf, bonf = on_f.next()
                sg_, bsg_ = ssg.next()
                fw.op("act", lambda e, onf=onf, sg_=sg_: e.activation(out=onf, in_=ps[1][:], func=AF.Square, accum_out=sg_[:, 0:1]), reads=[b_ps[1]], writes=[bonf, bsg_])
                fw.op("act", lambda e, sg_=sg_: e.activation(out=sg_[:, 0:1], in_=sg_[:, 0:1], func=AF.Sqrt, bias=eps6[:, 0:1], scale=1.0 / 512), reads=[bsg_, b_const], writes=[bsg_])
                fw.op("dve", lambda e, sg_=sg_: e.reciprocal(out=sg_[:, 0:1], in_=sg_[:, 0:1]), reads=[bsg_], writes=[bsg_])
                fw.op("dve", lambda e, onf=onf, sg_=sg_: e.scalar_tensor_tensor(out=onf, in0=ps[1][:], scalar=sg_[:, 0:1], in1=gng, op0=ALU.mult, op1=ALU.mult),
                      reads=[b_ps[1], bsg_, b_gc, bonf], writes=[bonf])
                ob, bob = ob_rot.next()
                fw.op("dve", lambda e, onf=onf, ob=ob, hd=hd, grt=grt: e.tensor_tensor(out=ob.rearrange("p a b -> p (a b)"), in0=onf, in1=grt[:, hd * 512:(hd + 1) * 512], op=ALU.mult),
                      reads=[bonf, bgr], writes=[bob])
                pv2 = bfview(6, 8)
                for k4 in range(4):
                    fw.op("pe", lambda e, ob=ob, k4=k4: e.transpose(out=pv2[:, k4, :], in_=ob[:, k4, :], identity=identb), reads=[bob, b_const], writes=[b_ps[6]])
                fw.op("act", lambda e, hd=hd, tc0=tc0: e.activation(out=omT[:, 16 + hd * 4:16 + hd * 4 + 4, tc0:tc0 + 128], in_=pv2[:, 0:4, :], func=AF.Copy),
                      reads=[b_ps[6]], writes=[b_om], join=True)
            for hd in range(4):
                p3_head(hd)
            if T % 2 == 1:
                t0 = (T - 1) * 128
                for dp in range(D // 256):
                    wt, bw, *_ = load_block(wout, dp)
                    for jj in range(2):
                        c2 = dp * 2 + jj
                        pb = 4 + jj
                        for k in range(KC):
                            fw.op("pe", lambda e, wt=wt, jj=jj, k=k, pb=pb: e.matmul(ps[pb][:, 0:256], lhsT=wt[:, k, jj * 128:(jj + 1) * 128], rhs=omT[:, k, :],
                                                                                 start=(k == 0), stop=(k == KC - 1)), reads=[bw, b_om], writes=[b_ps[pb]])
                        resid_update(l, 0, c2, t0, 256, ps[pb][:, 0:256], b_ps[pb], xr_rot)
        for T in range(NT):
            p3_tile(T)
        kb.pop()

    for li_, (kind, l) in enumerate(layers):
        precast(li_)
        if kind == "ffn":
            precast(li_ + 1)
            precast(li_ + 2)
            ffn_layer(l)
        elif kind == "odd":
            odd_layer(l)
        elif kind == "even":
            even_layer(l)

    for k in cfg.get("taps", ()):
        src = ev[k]
        tout = nc.dram_tensor("tap_" + k, list(src.shape), src.dtype, kind="ExternalOutput").ap()
        kb.tapnames.append("tap_" + k)
        fw.op("sp", lambda e, tout=tout, src=src: e.dma_start(out=tout, in_=src), reads=[ev["b"][k]], writes=[b_tap], dma=True, join=True)
    if kb.tapnames:
        fw.op("sp", None, reads=[b_tap])
    kb.push()
    orow = Rot([kb.alloc([128, D], F32) for _ in range(2)])
    xld = Rot([kb.alloc([128, 4, 128], F32) for _ in range(4)])
    b_out = Buf("out")
    for t in range(NT):
        orw, bo = orow.next()
        for cg in range(8):
            xl, bl = xld.next()
            fw.op("sp", lambda e, xl=xl, cg=cg, t=t: e.dma_start(out=xl, in_=xT[cg * 4:(cg + 1) * 4, :, t * 128:(t + 1) * 128].rearrange("c p n -> p c n")),
                  reads=[b_xT[t]], writes=[bl], dma=True)
            pb = 4 + (cg % 2)
            pst = ps[pb][:].rearrange("p (a b) -> p a b", a=4)
            for k in range(4):
                fw.op("pe", lambda e, xl=xl, k=k, pst=pst: e.transpose(out=pst[:, k, :], in_=xl[:, k, :], identity=identf),
                      reads=[bl, b_const], writes=[b_ps[pb]])
            if cg % 2 == 0:
                fw.op("dve", lambda e, orw=orw, cg=cg, pb=pb: e.tensor_copy(out=orw[:, cg * 512:(cg + 1) * 512], in_=ps[pb][:]), reads=[b_ps[pb]], writes=[bo], join=(cg > 0))
            else:
                fw.op("act", lambda e, orw=orw, cg=cg, pb=pb: e.activation(out=orw[:, cg * 512:(cg + 1) * 512], in_=ps[pb][:], func=AF.Copy), reads=[b_ps[pb]], writes=[bo], join=True)
        fw.op("act", lambda e, orw=orw, t=t: e.dma_start(out=out_ap[t * 128:(t + 1) * 128, :], in_=orw), reads=[bo], writes=[b_out], dma=True, semkey=bo, join=True)
    fw.op("sp", None, reads=[b_out])
    kb.pop()
    cnt = fw.emit()
    kb.st.close()
    return nc, kb, cnt


EV_SEGS = ([(i * 256, 256) for i in range(8)] + [(2048 + i * 256, 256) for i in range(12)] + [(5120, 48)]
           + [(5168 + i * 256, 256) for i in range(4)] + [(6192 + i * 256, 256) for i in range(4)]
           + [(7216 + i * 256, 256) for i in range(8)] + [(9264, 16)] + [(9280 + i * 256, 256) for i in range(8)])


def vecT(v):
    v = np.asarray(v, np.float32)
    lead = v.shape[:-1]
    n = v.shape[-1] // 128
    a = v.reshape(lead + (n, 128))
    a = np.moveaxis(a, -1, 0)
    return np.ascontiguousarray(a.reshape(128, -1))


def host_consts(inputs, b):
    m = {}
    m["c_ident"] = np.eye(128, dtype=np.float32)
    m["cT"] = vecT(inputs["c"][b])
    m["b_modT"] = vecT(inputs["b_mod"].reshape(6, D))
    m["adaT"] = vecT(inputs["ada_table"])
    m["gmixT"] = vecT(inputs["norm_mix_g"])
    m["gffnT"] = vecT(inputs["norm_ffn_g"])
    if "cv_b_pw1" in inputs:
        cv = np.stack([np.stack([inputs["cv_b_pw1"][j][:D], inputs["cv_b_pw1"][j][D:], inputs["cv_b_dw"][j], inputs["cv_ln_g"][j],
                                 inputs["cv_ln_b"][j], inputs["cv_b_pw2"][j]]) for j in range(2)])
        m["cv_vecT"] = np.stack([vecT(cv[j]) for j in range(2)])
        wd = np.asarray(inputs["cv_w_dw"], np.float32).reshape(2, 31, KC, 128)
        m["cv_wdwT"] = np.ascontiguousarray(wd.transpose(0, 3, 2, 1).reshape(2, 128, KC * 31))
    if "w_in" in inputs:
        half = 16
        inv = (500000.0 ** (-np.arange(half, dtype=np.float32) / half)).astype(np.float32)
        pos = np.arange(S, dtype=np.float32)
        ang = pos[:, None] * inv[None, :]
        cs = np.concatenate([np.cos(ang), np.sin(ang)], -1).astype(np.float32)
        m["c_cs"] = np.ascontiguousarray(cs.reshape(NT, 128, 32).transpose(1, 0, 2).reshape(128, NT * 32))
        pc = (np.arange(128, dtype=np.float32) * 16 + 31)
        angc = pc[:, None] * inv[None, :]
        m["c_cscmp"] = np.concatenate([np.cos(angc), np.sin(angc)], -1).astype(np.float32)
        g4 = np.stack([np.stack([inputs["q_norm_g"][j], inputs["k_norm_g"][j][0], inputs["k_norm_g"][j][1], inputs["k_norm_g"][j][2]]) for j in range(2)])
        m["qk_gT"] = np.ascontiguousarray(np.broadcast_to(g4.reshape(2, 1, 4 * 128), (2, 128, 4 * 128))).astype(np.float32)
        m["cmp_posT"] = np.ascontiguousarray(np.asarray(inputs["cmp_pos"], np.float32).transpose(0, 3, 1, 2).reshape(2, 128, 64))
        n = np.arange(128)
        jb = np.arange(32)
        ov = np.clip(np.minimum(16 * n[:, None] + 32, 64 * jb[None, :] + 64) - np.maximum(16 * n[:, None], 64 * jb[None, :]), 0, None) / 16.0
        ov[127] = 0
        m["c_ovl"] = np.concatenate([ov, np.ones((128, 1))], -1).astype(np.float32)
        E = np.zeros((32, NT, 128), np.float32)
        for kt in range(NT):
            for k in range(128):
                E[2 * kt + k // 64, kt, k] = 30000.0
        m["c_E"] = E.reshape(32, NT * 128)
        jj, ii = np.meshgrid(np.arange(128), np.arange(128), indexing="ij")
        same = (jj // 64) == (ii // 64)
        tri = np.stack([np.where(same & (jj <= ii), -1.0 / 16, 0.0), np.where(same & (jj > ii), -1.0 / 16, 0.0), np.where(same & (jj <= ii), 1.0, 0.0)], 1)
        m["c_tri"] = np.ascontiguousarray(tri.reshape(128, 3 * 128)).astype(np.float32)
        ci = np.zeros((128, 2), np.float32)
        ci[:64, 0] = 1
        ci[64:, 1] = 1
        m["c_cind"] = ci
        m["gla_ngT"] = np.ascontiguousarray(np.broadcast_to(np.asarray(inputs["gla_norm_g"], np.float32)[:, None, :], (2, 128, 512)))
    return m


_CACHE = {}


def run(inputs, cfg, cores):
    key = repr(cfg)
    if key not in _CACHE:
        _CACHE[key] = build(cfg)
    nc, kb, cnt = _CACHE[key]
    in_maps = []
    for b in cores:
        hc = host_consts(inputs, b)
        m = {}
        for name in kb.din:
            if name == "x":
                m[name] = np.ascontiguousarray(inputs["x"][b])
            elif name in hc:
                m[name] = hc[name]
            else:
                m[name] = np.ascontiguousarray(inputs[name])
        in_maps.append(m)
    res = run_bass_kernel_spmd(nc, in_maps, core_ids=list(range(len(cores))))
    if getattr(kb, "tapnames", None):
        run.taps = {k: np.asarray(res.results[0][k]) for k in kb.tapnames}
    return np.stack([r["out"] for r in res.results], axis=0)


FULL_CFG = {"layers": [("even", 0), ("ffn", 0), ("odd", 1), ("ffn", 1), ("even", 2), ("ffn", 2), ("odd", 3), ("ffn", 3)]}


def kernel(**inputs):
    inputs = {k: np.asarray(v) for k, v in inputs.items()}
    return run(inputs, FULL_CFG, list(range(8))).astype(np.float32)
```
